# Optimizing a Trainium2 kernel written in Bass

```python
import jax, jax.numpy as jnp
from jax import lax
import numpy as np

D_MODEL = 4096
BATCH = 4
SEQ = 2048
DEPTH = 2
DEC_BATCH = 128
DEC_SEQ = 8
PAST_LEN = 16384
PAGE_SIZE = 128

N_EVEN = (DEPTH + 1) // 2
N_ODD = DEPTH // 2
A_W = D_MODEL
A_HEADS = 8
A_HD = A_W // A_HEADS
CHUNK = 128
B_W = D_MODEL
B_HEADS = 8
B_HD = B_W // B_HEADS
CONV_W = 31
C_W = 2 * D_MODEL
POOL_WINDOWS = (2, 4, 8, 16)
POOL_GROUPS = len(POOL_WINDOWS)
C_G = C_W // POOL_GROUPS
POOL_MAX = max(POOL_WINDOWS)

EVEN_IN = 3 * A_W + 3 * B_W
ODD_IN = 2 * C_W
RMS_EPS = 1e-6
LN_EPS = 1e-5

kernel_name = "hybrid_sgu_conv_pool_decoder_step"


def _rmsnorm(x, g):
    xf = x.astype(jnp.float32)
    y = xf * lax.rsqrt(jnp.mean(xf * xf, axis=-1, keepdims=True) + RMS_EPS)
    return (y * g.astype(jnp.float32)).astype(x.dtype)


def _layernorm(x, g, b):
    xf = x.astype(jnp.float32)
    mu = jnp.mean(xf, axis=-1, keepdims=True)
    var = jnp.mean(jnp.square(xf - mu), axis=-1, keepdims=True)
    y = (xf - mu) * lax.rsqrt(var + LN_EPS)
    return (y * g.astype(jnp.float32) + b.astype(jnp.float32)).astype(x.dtype)


def _spatial_gate(u, v, w_s, b_s):
    n, L, _ = v.shape
    cl = min(L, CHUNK)
    nc = L // cl
    mask = jnp.tril(jnp.ones((cl, cl), dtype=bool))
    w = jnp.where(mask[None], w_s[:, :cl, :cl], jnp.zeros((), w_s.dtype)).astype(v.dtype)
    vc = v.reshape(n, nc, cl, A_HEADS, A_HD)
    mixed = jnp.einsum('hts,ncshd->ncthd', w, vc)
    mixed = mixed + b_s[:, :cl].T.astype(v.dtype)[None, None, :, :, None]
    return u * mixed.reshape(n, L, A_W)


def _causal_dwconv(ext, w, b):
    y = lax.conv_general_dilated(
        ext, w[:, None, :].astype(ext.dtype), window_strides=(1,), padding='VALID',
        dimension_numbers=('NWC', 'WIO', 'NWC'), feature_group_count=ext.shape[-1])
    return y + b.astype(ext.dtype)


def _even_layer(x, conv_prefix, g_in, w_in, sgu_w, sgu_b, v_g, v_b, conv_w, conv_b, cn_g, cn_b, w_out):
    n, L, _ = x.shape
    h = _rmsnorm(x, g_in)
    z = h @ w_in.astype(x.dtype)
    u, v, gate_a, glu_a, glu_b, gate_b = jnp.split(
        z, [A_W, 2 * A_W, 3 * A_W, 3 * A_W + B_W, 3 * A_W + 2 * B_W], axis=-1)
    u = jax.nn.gelu(u, approximate=False)
    v = _layernorm(jax.nn.gelu(v, approximate=False), v_g, v_b)
    y_a = _spatial_gate(u, v, sgu_w, sgu_b) * jax.nn.silu(gate_a)
    glu = glu_a * jax.nn.sigmoid(glu_b)
    ext = jnp.concatenate([conv_prefix.astype(glu.dtype), glu], axis=1)
    c = _causal_dwconv(ext, conv_w, conv_b)
    c = _layernorm(c.reshape(n, L, B_HEADS, B_HD), cn_g.reshape(B_HEADS, B_HD),
                   cn_b.reshape(B_HEADS, B_HD)).reshape(n, L, B_W)
    y_b = jax.nn.silu(c) * jax.nn.silu(gate_b)
    y = jnp.concatenate([y_a, y_b], axis=-1) @ w_out.astype(x.dtype)
    return x + y, v, ext[:, -(CONV_W - 1):]


def _odd_layer(x, pool_prefix, pos0, g_in, w_in, pool_w, pool_scale, w_out):
    n, L, _ = x.shape
    h = _rmsnorm(x, g_in)
    z = h @ w_in.astype(x.dtype)
    xc, gate_c = jnp.split(z, [C_W], axis=-1)
    ext = jnp.concatenate([pool_prefix.astype(xc.dtype), xc], axis=1)
    cs = jnp.cumsum(ext.astype(jnp.float32), axis=1)
    cs = jnp.concatenate([jnp.zeros((n, 1, C_W), jnp.float32), cs], axis=1)
    pos = pos0 + jnp.arange(L)
    diffs = []
    for gi, w in enumerate(POOL_WINDOWS):
        sl = slice(gi * C_G, (gi + 1) * C_G)
        s = cs[:, POOL_MAX:POOL_MAX + L, sl] - cs[:, POOL_MAX - w:POOL_MAX - w + L, sl]
        cnt = jnp.minimum(w, pos + 1).astype(jnp.float32)[None, :, None]
        diffs.append((s / cnt).astype(xc.dtype) - xc[..., sl])
    d = jnp.stack(diffs, axis=2)
    y_c = jnp.einsum('nlgc,gcd->nlgd', d, pool_w.astype(xc.dtype)).reshape(n, L, C_W)
    y_c = y_c * pool_scale.astype(xc.dtype) * jax.nn.silu(gate_c)
    y = y_c @ w_out.astype(x.dtype)
    return x + y, ext[:, -(POOL_MAX - 1):]


def setup_inputs(seed: int = 0) -> dict:
    key = jax.random.key(seed)
    ks = jax.random.split(key, 24)
    f32 = jnp.float32
    nrm = lambda k, shape, s: jax.random.normal(k, shape, f32) * s
    return {
        "x_prompt": nrm(ks[0], (BATCH, SEQ, D_MODEL), 1.0),
        "x_sample": nrm(ks[1], (DEC_BATCH, DEC_SEQ, D_MODEL), 1.0),
        "state_conv": nrm(ks[2], (N_EVEN, DEC_BATCH, CONV_W - 1, B_W), 0.5),
        "state_pool": nrm(ks[3], (N_ODD, DEC_BATCH, POOL_MAX - 1, C_W), 1.0),
        "norm_in": 1.0 + nrm(ks[4], (DEPTH, D_MODEL), 0.02),
        "w_in_even": nrm(ks[5], (N_EVEN, D_MODEL, EVEN_IN), D_MODEL ** -0.5),
        "sgu_w": nrm(ks[6], (N_EVEN, A_HEADS, CHUNK, CHUNK), 0.5 * CHUNK ** -0.5),
        "sgu_b": 1.0 + nrm(ks[7], (N_EVEN, A_HEADS, CHUNK), 0.01),
        "v_norm_g": 1.0 + nrm(ks[8], (N_EVEN, A_W), 0.02),
        "v_norm_b": nrm(ks[9], (N_EVEN, A_W), 0.02),
        "conv_w": nrm(ks[10], (N_EVEN, CONV_W, B_W), CONV_W ** -0.5),
        "conv_b": nrm(ks[11], (N_EVEN, B_W), 0.02),
        "conv_norm_g": 1.0 + nrm(ks[12], (N_EVEN, B_W), 0.02),
        "conv_norm_b": nrm(ks[13], (N_EVEN, B_W), 0.02),
        "w_out_even": nrm(ks[14], (N_EVEN, A_W + B_W, D_MODEL), (A_W + B_W) ** -0.5),
        "w_in_odd": nrm(ks[15], (N_ODD, D_MODEL, ODD_IN), D_MODEL ** -0.5),
        "pool_w": nrm(ks[16], (N_ODD, POOL_GROUPS, C_G, C_G), C_G ** -0.5),
        "pool_scale": 1.0 + nrm(ks[17], (N_ODD, C_W), 0.02),
        "w_out_odd": nrm(ks[18], (N_ODD, C_W, D_MODEL), C_W ** -0.5),
        "norm_f": 1.0 + nrm(ks[19], (D_MODEL,), 0.02),
    }


def reference(x_prompt, x_sample, state_conv, state_pool, norm_in, w_in_even, sgu_w, sgu_b,
              v_norm_g, v_norm_b, conv_w, conv_b, conv_norm_g, conv_norm_b, w_out_even,
              w_in_odd, pool_w, pool_scale, w_out_odd, norm_f):
    xp, xs = x_prompt, x_sample
    n_p = x_prompt.shape[0]
    chunk_v_s, conv_p, conv_s, pool_p, pool_s = [], [], [], [], []
    for layer in range(DEPTH):
        if layer % 2 == 0:
            i = layer // 2
            prm = (norm_in[layer], w_in_even[i], sgu_w[i], sgu_b[i], v_norm_g[i], v_norm_b[i],
                   conv_w[i], conv_b[i], conv_norm_g[i], conv_norm_b[i], w_out_even[i])
            zero_conv = jnp.zeros((n_p, CONV_W - 1, B_W), xp.dtype)
            xp, _, cp = _even_layer(xp, zero_conv, *prm)
            xs, vs, cs = _even_layer(xs, state_conv[i], *prm)
            chunk_v_s.append(vs)
            conv_p.append(cp)
            conv_s.append(cs)
        else:
            i = layer // 2
            prm = (norm_in[layer], w_in_odd[i], pool_w[i], pool_scale[i], w_out_odd[i])
            zero_pool = jnp.zeros((n_p, POOL_MAX - 1, C_W), xp.dtype)
            xp, pp = _odd_layer(xp, zero_pool, 0, *prm)
            xs, ps = _odd_layer(xs, state_pool[i], PAST_LEN, *prm)
            pool_p.append(pp)
            pool_s.append(ps)
    y_prompt = _rmsnorm(xp, norm_f)
    y_sample = _rmsnorm(xs, norm_f)
    return (y_prompt, y_sample, jnp.stack(chunk_v_s), jnp.stack(conv_p), jnp.stack(conv_s),
            jnp.stack(pool_p), jnp.stack(pool_s))
```

```python
import numpy as np
from contextlib import ExitStack
import concourse.bass as bass
import concourse.mybir as mybir
from concourse.bass_utils import run_bass_kernel_spmd

F32 = mybir.dt.float32
BF16 = mybir.dt.bfloat16
AF = mybir.ActivationFunctionType
ALU = mybir.AluOpType
AX = mybir.AxisListType

CONV_W = 31
NPRE = CONV_W - 1
POOLW = (2, 4, 8, 16)
PPRE = 15
RMS_EPS = 1e-6
LN_EPS = 1e-5
NHEAD = 8
BLK = 3
NCORES = 8
ENGS = ("pe", "act", "dve", "pool", "sp")


class Tr:
    def __init__(self, nc, es):
        self.nc = nc
        self.es = es
        self.E = {"pe": nc.tensor, "act": nc.scalar, "dve": nc.vector, "pool": nc.gpsimd, "sp": nc.sync}
        self.nops = 0
        self.cnt = {}
        self.sem = {}
        self.seen = {e: {} for e in ENGS}
        self.W = {}
        self.R = {}
        for e in ("pe", "act", "dve", "pool"):
            self._mksem(e)

    def _mksem(self, name):
        if name not in self.sem:
            self.sem[name] = self.es.enter_context(self.nc.semaphore("s_" + name))
            self.cnt[name] = 0

    def _deps(self, eng, reads, writes):
        need = {}
        for k in reads:
            for s, t in self.W.get(k, {}).items():
                need[s] = max(need.get(s, 0), t)
        for k in writes:
            for s, t in self.W.get(k, {}).items():
                need[s] = max(need.get(s, 0), t)
            for s, t in self.R.get(k, {}).items():
                need[s] = max(need.get(s, 0), t)
        for s, t in need.items():
            if s == "pe" and eng == "pe":
                continue
            if self.seen[eng].get(s, 0) >= t:
                continue
            self.seen[eng][s] = t
            self.E[eng].wait_ge(self.sem[s], t)

    def _mark(self, s, tick, reads, writes):
        for k in reads:
            d = self.R.setdefault(k, {})
            d[s] = max(d.get(s, 0), tick)
        for k in writes:
            self.W[k] = {s: tick}
            self.R[k] = {}

    def op(self, eng, fn, reads=(), writes=(), sig=True):
        self._deps(eng, reads, writes)
        if sig:
            self.cnt[eng] += 1
            tick = self.cnt[eng]
            fn(self.E[eng]).then_inc(self.sem[eng], 1)
        else:
            tick = self.cnt[eng] + 1
            fn(self.E[eng])
        self.nops += 1
        self._mark(eng, tick, reads, writes)

    def dma(self, eng, fn, slot, reads=(), writes=()):
        self._mksem(slot)
        self._deps(eng, reads, writes)
        self.cnt[slot] += 16
        fn(self.E[eng]).then_inc(self.sem[slot], 16)
        self.nops += 1
        self._mark(slot, self.cnt[slot], reads, writes)

    def alias(self, new_keys, old_keys):
        w, r = {}, {}
        for k in old_keys:
            for s, t in self.W.get(k, {}).items():
                w[s] = max(w.get(s, 0), t)
            for s, t in self.R.get(k, {}).items():
                r[s] = max(r.get(s, 0), t)
        for k in new_keys:
            self.W[k] = dict(w)
            self.R[k] = dict(r)

    def final_wait(self, eng):
        for s, c in self.cnt.items():
            if c > 0 and self.seen[eng].get(s, 0) < c:
                self.E[eng].wait_ge(self.sem[s], c)
                self.seen[eng][s] = c

    def emit(self, block):
        nc = self.nc

        def run(e, items):
            for it in items:
                if it[0] == "w":
                    e.wait_ge(self.sem[it[1]], it[2])
                else:
                    ins = it[1](e)
                    if it[2] is not None:
                        ins.then_inc(self.sem[it[2]], it[3])

        q = self.q

        @block.tensor
        def _(e):
            run(e, q["pe"])

        @block.scalar
        def _(e):
            run(e, q["act"])

        @block.vector
        def _(e):
            run(e, q["dve"])

        @block.gpsimd
        def _(e):
            run(e, q["pool"])

        @block.sync
        def _(e):
            run(e, q["sp"])


def build_program(D, NPT):
    KD = D // 128
    HT = KD // NHEAD
    KC = 2 * KD
    CGT = KC // 4
    NCB = D // 512
    assert HT >= 2 and HT % 2 == 0 and KD % 16 == 0
    NTIN = NPT + 2
    NTOUT = NPT + 1
    TMAX = BLK * 128
    KH = 16
    NQ = 16
    DS = 8

    nc = bass.Bass("TRN2", target_bir_lowering=False)

    def din(name, shape):
        return nc.dram_tensor(name, list(shape), F32, kind="ExternalInput").ap()

    def dout(name, shape):
        return nc.dram_tensor(name, list(shape), F32, kind="ExternalOutput").ap()

    xin = din("xin", [NTIN * 128, D])
    w_in_e = din("w_in_e", [D, 6 * D])
    w_out_e = din("w_out_e", [2 * D, D])
    w_in_o = din("w_in_o", [D, 4 * D])
    pool_w = din("pool_w", [4 * 2 * KD // 4 * 128 * 1, CGT * 128])
    w_out_o = din("w_out_o", [2 * D, D])
    NCOL = 8 * KD + KC + KD * CONV_W
    cols_d = din("cols", [128, NCOL])
    nf_d = din("nf", [1, D])
    vg_d = din("vg", [1, D])
    vb_d = din("vb", [1, D])
    sgub_d = din("sgub", [2, NHEAD * 128])
    sgu_d = din("sgu", [2 * 128, NHEAD * 128])
    mask_d = din("mask", [2 * 128, 128])
    ident_d = din("ident", [128, 128])
    posv_d = din("posv", [128, 16])
    stcT_d = din("stcT", [D, NQ * NPRE])
    stpT_d = din("stpT", [2 * D, NQ * PPRE])
    stc_tm = din("stc_tm", [NQ, NPRE * D])
    stp_tm = din("stp_tm", [NQ, PPRE * 2 * D])

    y_d = dout("y", [NTOUT * 128, D])
    vout_d = dout("vout", [128, D])
    gluTs_d = dout("gluTs", [D, 128])
    xcTs_d = dout("xcTs", [2 * D, 128])
    gluTp_d = dout("gluTp", [D, NPRE])
    xcTp_d = dout("xcTp", [2 * D, PPRE])
    convs_old = dout("convs_old", [NQ, (NPRE - DS) * D])
    pools_old = dout("pools_old", [NQ, (PPRE - DS) * 2 * D])

    NW_ALLOC = 15 * D * D // (128 * 4096) + 64
    WSC = 160
    wscr_l = [nc.dram_tensor(f"wscr{i}", [WSC * 128, 4096], BF16).ap() for i in range((NW_ALLOC + WSC - 1) // WSC)]

    def wscr_rows(n):
        return wscr_l[n // WSC][(n % WSC) * 128:(n % WSC + 1) * 128, :]

    C_G0, C_G1, C_VG, C_VB, C_CB, C_CNG, C_CNB = [i * KD for i in range(7)]
    C_PS = 7 * KD
    C_INVW = C_PS + KC
    C_CW = 8 * KD + KC

    with ExitStack() as es:
        def sb(name, shape, dt=F32):
            return es.enter_context(nc.sbuf_tensor("sb_" + name, list(shape), dt))

        tr = Tr(nc, es)
        ps = [es.enter_context(nc.psum_tensor(f"ps{b}", [128, 512], F32)) for b in range(8)]
        ps_rr = [0]

        def psalloc(n):
            r = [(ps_rr[0] + i) % 8 for i in range(n)]
            ps_rr[0] = (ps_rr[0] + n) % 8
            return r

        RGN = 7232
        xres = sb("xres", [128, BLK, D])
        hT = sb("hT", [128, KD, TMAX], BF16)
        RG = sb("RG", [128, RGN])
        vstage = sb("vstage", [128, D])
        yT = [sb(f"yT{i}", [128, 8, TMAX], BF16) for i in range(2)]
        NWB = 4
        wblk = [sb(f"wb{i}", [128, 4096], BF16) for i in range(NWB)]
        cols = sb("cols", [128, NCOL])
        rs_bc = sb("rs_bc", [128, NHEAD * 128])
        sb_bc = sb("sb_bc", [128, NHEAD * 128])
        wTm = sb("wTm", [128, NHEAD * 128], BF16)
        wTs = sb("wTs", [128, BLK, NHEAD * 128], BF16)
        nm = sb("nm", [128, BLK, 128], BF16)
        ident = sb("ident", [128, 128])
        ones_f = sb("ones_f", [128, 128])
        ones_b = sb("ones_b", [128, 128], BF16)
        maskt = sb("maskt", [128, 128])
        carry_c = sb("carry_c", [128, KD, NPRE])
        carry_x = sb("carry_x", [128, KC, PPRE])
        invc = sb("invc", [128, 4, 16])
        posv = sb("posv", [128, 16])
        ug = [sb(f"ug{i}", [128, TMAX]) for i in range(2)]
        sg = [sb(f"sg{i}", [128, TMAX]) for i in range(2)]
        Et = sb("Et", [128, TMAX])
        At = sb("At", [128, 128])
        small = sb("small", [128, 64])
        bnst = sb("bnst", [128, BLK, 8, 6])
        junk = sb("junk", [128, 512])

        vbf = RG[:, 0:BLK * D // 2].bitcast(BF16)
        off = [0]

        def rg(n):
            a = off[0]
            off[0] += n
            assert off[0] <= RGN, off[0]
            return a

        off[0] = 0
        LC = NPRE + TMAX
        cbuf = [rg(TMAX) for _ in range(HT)]
        cvb = [rg(LC) for _ in range(2)]
        sqb = [rg(TMAX) for _ in range(2)]
        meanb, rstdb, msqb = rg(TMAX), rg(TMAX), rg(TMAX)
        sgbb = [rg(TMAX) for _ in range(2)]
        sigb = [rg(TMAX) for _ in range(2)]
        cnb = [rg(TMAX) for _ in range(2)]
        cvs = rg(NQ * (NPRE + DS))
        p3_end = off[0]
        off[0] = 0
        dT_off = rg(CGT * TMAX // 2)
        LX = PPRE + TMAX
        xcb = [rg(LX) for _ in range(2)]
        sA = [rg(LX) for _ in range(2)]
        sB = [rg(LX) for _ in range(2)]
        sgcb = [rg(TMAX) for _ in range(2)]
        t16 = rg(16)
        p5_end = off[0]
        dT = RG[:, dT_off:dT_off + CGT * TMAX // 2].bitcast(BF16)

        RGK = ["RG"]

        wb_i = [0]
        wn = [0]
        cur_blk = [0]
        NPB = (NPT + 1 + BLK - 1) // BLK

        def wload(src_ap, nk, width):
            s = wb_i[0] % NWB
            wb_i[0] += 1
            n = wn[0]
            wn[0] += 1
            b = cur_blk[0]
            cb_ = n % NPB
            nel = nk * width
            if cb_ < b:
                srcb = wscr_rows(n)[:, 0:nel]
                dstb = wblk[s][:, 0:nel]
                tr.dma("pool", lambda e, dstb=dstb, srcb=srcb: e.dma_start(out=dstb, in_=srcb),
                       f"wb{s}", reads=[("ws", n)], writes=[("wb", s)])
            else:
                dst = wblk[s][:, 0:nel].rearrange("p (k c) -> p k c", k=nk)
                src = src_ap.rearrange("(k p) c -> p k c", p=128)
                tr.dma("pool", lambda e, dst=dst, src=src: e.dma_start(out=dst, in_=src),
                       f"wb{s}", writes=[("wb", s)])
                if cb_ == b:
                    srcs = wblk[s][:, 0:nel]
                    dsts = wscr_rows(n)[:, 0:nel]
                    tr.dma("sp", lambda e, dsts=dsts, srcs=srcs: e.dma_start(out=dsts, in_=srcs),
                           f"wst{s}", reads=[("wb", s)], writes=[("ws", n)])
            return s

        def inproj_fm(wd, col0, T, nbanks_out=None):
            bk = psalloc(2)
            nkh = KD // KH
            for kh in range(nkh):
                s = wload(wd[kh * KH * 128:(kh + 1) * KH * 128, col0:col0 + 256], KH, 256)
                for fi in range(2):
                    for kc in range(KH):
                        last = (kh == nkh - 1 and kc == KH - 1)
                        lhsT = wblk[s][:, kc * 256 + fi * 128: kc * 256 + (fi + 1) * 128]
                        rhs = hT[:, kh * KH + kc, 0:T]
                        out = ps[bk[fi]][:, 0:T]
                        tr.op("pe", lambda e, out=out, lhsT=lhsT, rhs=rhs, st=(kh == 0 and kc == 0), sp=last:
                              e.matmul(out, lhsT=lhsT, rhs=rhs, start=st, stop=sp),
                              reads=[("wb", s), "hT"], writes=[("ps", bk[fi])],
                              sig=(last or kc == KH - 1))
            return bk

        def outproj(wd, row0, yt, ytk, nt):
            for cb in range(NCB):
                s = wload(wd[row0:row0 + 1024, cb * 512:(cb + 1) * 512], 8, 512)
                bk = psalloc(nt)
                for i in range(nt):
                    for kc in range(8):
                        lhsT = yt[:, kc, i * 128:(i + 1) * 128]
                        rhs = wblk[s][:, kc * 512:(kc + 1) * 512]
                        out = ps[bk[i]][:, 0:512]
                        tr.op("pe", lambda e, out=out, lhsT=lhsT, rhs=rhs, st=(kc == 0), sp=(kc == 7):
                              e.matmul(out, lhsT=lhsT, rhs=rhs, start=st, stop=sp),
                              reads=[("wb", s), ytk], writes=[("ps", bk[i])], sig=(kc == 7))
                for i in range(nt):
                    xs = xres[:, i, cb * 512:(cb + 1) * 512]
                    pin = ps[bk[i]][:, 0:512]
                    tr.op("dve", lambda e, xs=xs, pin=pin: e.tensor_tensor(out=xs, in0=pin, in1=xs, op=ALU.add),
                          reads=[("ps", bk[i]), ("xres", i)], writes=[("xres", i)])

        def rmsnorm_to_hT(i, gc0):
            ssc = small[:, 0:NCB]
            for cb in range(NCB):
                tr.op("act", lambda e, cb=cb: e.activation(out=junk[:, :], in_=xres[:, i, cb * 512:(cb + 1) * 512],
                                                       func=AF.Square, accum_out=small[:, cb:cb + 1]),
                      reads=[("xres", i)], writes=["junk", "ssc"])
            tr.op("dve", lambda e: e.reduce_sum(out=small[:, 16:17], in_=ssc, axis=AX.X),
                  reads=["ssc"], writes=["ss"])
            tr.op("dve", lambda e: e.tensor_scalar(out=small[:, 17:18], in0=small[:, 16:17], scalar1=1.0 / D,
                                                   scalar2=RMS_EPS, op0=ALU.mult, op1=ALU.add),
                  reads=["ss"], writes=["ss2"])
            tr.op("act", lambda e: e.activation(out=small[:, 19:20], in_=small[:, 17:18], func=AF.Sqrt),
                  reads=["ss2"], writes=["ss3"])
            tr.op("dve", lambda e: e.reciprocal(out=small[:, 18:19], in_=small[:, 19:20]),
                  reads=["ss3"], writes=["rstd"])
            tr.op("act", lambda e: e.activation(out=vstage[:, :], in_=xres[:, i, :], func=AF.Copy,
                                                scale=small[:, 18:19]),
                  reads=[("xres", i), "rstd"], writes=["vstage"])
            for k4 in range(KD // 4):
                b = psalloc(1)[0]
                for q in range(4):
                    kc = k4 * 4 + q
                    tr.op("pe", lambda e, kc=kc, q=q, b=b: e.transpose(out=ps[b][:, q * 128:(q + 1) * 128],
                                                                     in_=vstage[:, kc * 128:(kc + 1) * 128],
                                                                     identity=ident[:, :]),
                          reads=["vstage", "ident"], writes=[("ps", b)], sig=(q == 3))
                for q in range(4):
                    kc = k4 * 4 + q
                    eng = "act" if q % 2 == 0 else "dve"
                    o = hT[:, kc, i * 128:(i + 1) * 128]
                    pin = ps[b][:, q * 128:(q + 1) * 128]
                    gcol = cols[:, gc0 + kc:gc0 + kc + 1]
                    if eng == "act":
                        tr.op("act", lambda e, o=o, pin=pin, gcol=gcol: e.activation(out=o, in_=pin, func=AF.Copy, scale=gcol),
                              reads=[("ps", b), "cols"], writes=["hT"])
                    else:
                        tr.op("dve", lambda e, o=o, pin=pin, gcol=gcol: e.tensor_scalar(out=o, in0=pin, scalar1=gcol, scalar2=None, op0=ALU.mult),
                              reads=[("ps", b), "cols"], writes=["hT"])

        tr.dma("sp", lambda e: e.dma_start(out=cols[:, :], in_=cols_d[:, :]), "c0a", writes=["cols"])
        tr.dma("sp", lambda e: e.dma_start(out=ident[:, :], in_=ident_d[:, :]), "c0b", writes=["ident"])
        tr.dma("sp", lambda e: e.dma_start(out=posv[:, :], in_=posv_d[:, :]), "c0c", writes=["posv"])
        tr.op("dve", lambda e: e.memset(ones_f[:, :], 1.0), writes=["ones_f"])
        tr.op("dve", lambda e: e.memset(ones_b[:, :], 1.0), writes=["ones_b"])
        tr.op("dve", lambda e: e.memset(carry_c[:, :, :], 0.0), writes=["carry_c"])
        tr.op("dve", lambda e: e.memset(carry_x[:, :, :], 0.0), writes=["carry_x"])
        for g, w in enumerate(POOLW):
            tr.op("dve", lambda e, g=g, w=w: e.tensor_scalar(out=invc[:, g, :], in0=posv[:, :], scalar1=1.0, scalar2=float(w),
                                                           op0=ALU.add, op1=ALU.min),
                  reads=["posv"], writes=["invc"])
        tr.op("dve", lambda e: e.reciprocal(out=invc[:, :, :], in_=invc[:, :, :]), reads=["invc"], writes=["invc"])

        def load_type_consts(ty):
            tr.dma("sp", lambda e: e.dma_start(out=vstage[:, 0:NHEAD * 128], in_=sgu_d[ty * 128:(ty + 1) * 128, :]),
                   "c1a", writes=["vstage"])
            tr.dma("sp", lambda e: e.dma_start(out=maskt[:, :], in_=mask_d[ty * 128:(ty + 1) * 128, :]),
                   "c1b", writes=["maskt"])
            tr.dma("sp", lambda e: e.dma_start(out=sb_bc[:, :], in_=sgub_d[ty:ty + 1, :].partition_broadcast(128)),
                   "c1c", writes=["sb_bc"])
            for h in range(NHEAD):
                tr.op("dve", lambda e, h=h: e.tensor_tensor(out=wTm[:, h * 128:(h + 1) * 128],
                                                          in0=vstage[:, h * 128:(h + 1) * 128], in1=maskt[:, :], op=ALU.mult),
                      reads=["vstage", "maskt"], writes=["wTm"])
            for hh in range(NHEAD * 128 // 512):
                b = psalloc(1)[0]
                tr.op("pe", lambda e, b=b, hh=hh: e.matmul(ps[b][:, 0:512], lhsT=ones_b[:, :], rhs=wTm[:, hh * 512:(hh + 1) * 512],
                                                         start=True, stop=True),
                      reads=["ones_b", "wTm"], writes=[("ps", b)])
                tr.op("act", lambda e, b=b, hh=hh: e.activation(out=rs_bc[:, hh * 512:(hh + 1) * 512], in_=ps[b][:, 0:512], func=AF.Copy),
                      reads=[("ps", b)], writes=["rs_bc"])

        tiles = list(range(NPT + 1))
        blocks = [("p", tiles[i:i + BLK]) for i in range(0, len(tiles), BLK)]
        blocks.append(("s", [NPT + 1]))
        cur_type = [None]

        for bi, (bty, btiles) in enumerate(blocks):
            cur_blk[0] = bi
            wn[0] = 0
            nt = len(btiles)
            T = nt * 128
            is_s = (bty == "s")
            last_p = (not is_s) and (btiles[-1] == NPT)
            ty = 1 if is_s else 0

            for i, tl in enumerate(btiles):
                tr.dma("sp", lambda e, i=i, tl=tl: e.dma_start(out=xres[:, i, :], in_=xin[tl * 128:(tl + 1) * 128, :]),
                       f"xin{i}", writes=[("xres", i)])
            if cur_type[0] != ty:
                load_type_consts(ty)
                cur_type[0] = ty
            for i in range(nt):
                rmsnorm_to_hT(i, C_G0)

            tr.alias(["vbf"], ["RG"])
            for cb in range(NCB):
                bk = psalloc(nt)
                nq = KD // 8
                for q in range(nq):
                    s = wload(w_in_e[q * 1024:(q + 1) * 1024, D + cb * 512: D + (cb + 1) * 512], 8, 512)
                    for i in range(nt):
                        for kc in range(8):
                            last = (q == nq - 1 and kc == 7)
                            lhsT = hT[:, q * 8 + kc, i * 128:(i + 1) * 128]
                            rhs = wblk[s][:, kc * 512:(kc + 1) * 512]
                            out = ps[bk[i]][:, 0:512]
                            tr.op("pe", lambda e, out=out, lhsT=lhsT, rhs=rhs, st=(q == 0 and kc == 0), sp=last:
                                  e.matmul(out, lhsT=lhsT, rhs=rhs, start=st, stop=sp),
                                  reads=[("wb", s), "hT"], writes=[("ps", bk[i])], sig=(kc == 7))
                for i in range(nt):
                    pin = ps[bk[i]][:, 0:512]
                    if is_s:
                        o = vstage[:, cb * 512:(cb + 1) * 512]
                        tr.op("act", lambda e, o=o, pin=pin: e.activation(out=o, in_=pin, func=AF.Gelu),
                              reads=[("ps", bk[i])], writes=["vstage"])
                        tr.op("dve", lambda e, o=o, i=i, cb=cb: e.bn_stats(out=bnst[:, i, cb, :], in_=o),
                              reads=["vstage"], writes=["bnst"])
                        ob = vbf[:, i * D + cb * 512: i * D + (cb + 1) * 512]
                        tr.op("dve", lambda e, o=o, ob=ob: e.tensor_copy(out=ob, in_=o),
                              reads=["vstage"], writes=["vbf"])
                    else:
                        ob = vbf[:, i * D + cb * 512: i * D + (cb + 1) * 512]
                        tr.op("act", lambda e, ob=ob, pin=pin: e.activation(out=ob, in_=pin, func=AF.Gelu),
                              reads=[("ps", bk[i])], writes=["vbf"])
                        tr.op("dve", lambda e, ob=ob, i=i, cb=cb: e.bn_stats(out=bnst[:, i, cb, :], in_=ob),
                              reads=["vbf"], writes=["bnst"])
            for i in range(nt):
                mv = small[:, 20 + 4 * i: 22 + 4 * i]
                rs_ = small[:, 22 + 4 * i: 23 + 4 * i]
                ngm = small[:, 23 + 4 * i: 24 + 4 * i]
                tr.op("dve", lambda e, mv=mv, i=i: e.bn_aggr(out=mv, in_=bnst[:, i, 0:NCB, :]),
                      reads=["bnst"], writes=[("mv", i)])
                tr.op("dve", lambda e, mv=mv, rs_=rs_: e.tensor_scalar(out=rs_, in0=mv[:, 1:2], scalar1=LN_EPS, scalar2=None, op0=ALU.add),
                      reads=[("mv", i)], writes=[("mv", i)])
                tr.op("act", lambda e, rs_=rs_: e.activation(out=rs_, in_=rs_, func=AF.Sqrt),
                      reads=[("mv", i)], writes=[("mv", i)])
                tr.op("dve", lambda e, rs_=rs_: e.reciprocal(out=rs_, in_=rs_),
                      reads=[("mv", i)], writes=[("mv", i)])
                tr.op("dve", lambda e, mv=mv, ngm=ngm: e.tensor_scalar(out=ngm, in0=mv[:, 0:1], scalar1=-1.0, scalar2=None, op0=ALU.mult),
                      reads=[("mv", i)], writes=[("mv", i)])
                tr.op("dve", lambda e, i=i, rs_=rs_: e.tensor_scalar(out=wTs[:, i, :], in0=wTm[:, :], scalar1=rs_, scalar2=None, op0=ALU.mult),
                      reads=[("mv", i), "wTm"], writes=["wTs"])
                tr.op("dve", lambda e, i=i, ngm=ngm: e.tensor_scalar(out=nm[:, i, :], in0=ones_f[:, :], scalar1=ngm, scalar2=None, op0=ALU.mult),
                      reads=[("mv", i), "ones_f"], writes=["nm"])
            if is_s:
                mv = small[:, 20:22]
                rs_ = small[:, 22:23]
                for cb in range(NCB):
                    for hf in range(2):
                        c0 = cb * 512 + hf * 256
                        gt = ug[hf][:, 0:256]
                        bt = sg[hf][:, 0:256]
                        tr.dma("sp", lambda e, gt=gt, c0=c0: e.dma_start(out=gt, in_=vg_d[0:1, c0:c0 + 256].partition_broadcast(128)),
                               f"vg{hf}", writes=[("ug", hf)])
                        tr.dma("sp", lambda e, bt=bt, c0=c0: e.dma_start(out=bt, in_=vb_d[0:1, c0:c0 + 256].partition_broadcast(128)),
                               f"vb{hf}", writes=[("sg", hf)])
                        vs = vstage[:, c0:c0 + 256]
                        tr.op("dve", lambda e, vs=vs, mv=mv, rs_=rs_: e.tensor_scalar(out=vs, in0=vs, scalar1=mv[:, 0:1], scalar2=rs_,
                                                                                  op0=ALU.subtract, op1=ALU.mult),
                              reads=["vstage", ("mv", 0)], writes=["vstage"])
                        tr.op("dve", lambda e, vs=vs, gt=gt: e.tensor_tensor(out=vs, in0=vs, in1=gt, op=ALU.mult),
                              reads=["vstage", ("ug", hf)], writes=["vstage"])
                        tr.op("dve", lambda e, vs=vs, bt=bt: e.tensor_tensor(out=vs, in0=vs, in1=bt, op=ALU.add),
                              reads=["vstage", ("sg", hf)], writes=["vstage"])
                tr.dma("sp", lambda e: e.dma_start(out=vout_d[:, :], in_=vstage[:, :]), "vout", reads=["vstage"])

            pend = []
            ygi = [0]

            def flush_outproj(wd):
                while pend:
                    g_, bufi = pend.pop(0)
                    outproj(wd, g_ * 1024, yT[bufi], ("yT", bufi), nt)

            for g in range(KD // 8):
                bufi = ygi[0] % 2
                ygi[0] += 1
                for pr in range(4):
                    j0 = g * 8 + 2 * pr
                    bu = inproj_fm(w_in_e, j0 * 128, T)
                    for fi in range(2):
                        tr.op("act", lambda e, fi=fi, bu=bu: e.activation(out=ug[fi][:, 0:T], in_=ps[bu[fi]][:, 0:T], func=AF.Gelu),
                              reads=[("ps", bu[fi])], writes=[("ug", fi)])
                    bg = inproj_fm(w_in_e, 2 * D + j0 * 128, T)
                    for fi in range(2):
                        tr.op("act", lambda e, fi=fi, bg=bg: e.activation(out=sg[fi][:, 0:T], in_=ps[bg[fi]][:, 0:T], func=AF.Silu),
                              reads=[("ps", bg[fi])], writes=[("sg", fi)])
                    for fi in range(2):
                        j = j0 + fi
                        h = j // HT
                        bm = psalloc(1)[0]
                        for i in range(nt):
                            out = ps[bm][:, i * 128:(i + 1) * 128]
                            tr.op("pe", lambda e, out=out, i=i, j=j, h=h: e.matmul(out, lhsT=vbf[:, i * D + j * 128: i * D + (j + 1) * 128],
                                                                               rhs=wTs[:, i, h * 128:(h + 1) * 128], start=True, stop=False),
                                  reads=["vbf", "wTs"], writes=[("ps", bm)], sig=False)
                            tr.op("pe", lambda e, out=out, i=i, h=h: e.matmul(out, lhsT=nm[:, i, :], rhs=wTs[:, i, h * 128:(h + 1) * 128],
                                                                          start=False, stop=True),
                                  reads=["nm", "wTs"], writes=[("ps", bm)], sig=(i == nt - 1))
                        tr.op("dve", lambda e, j=j, h=h: e.scalar_tensor_tensor(out=At[:, :], in0=rs_bc[:, h * 128:(h + 1) * 128],
                                                                              scalar=cols[:, C_VB + j:C_VB + j + 1],
                                                                              in1=sb_bc[:, h * 128:(h + 1) * 128], op0=ALU.mult, op1=ALU.add),
                              reads=["rs_bc", "sb_bc", "cols"], writes=["At"])
                        for i in range(nt):
                            tr.op("dve", lambda e, i=i, j=j, bm=bm: e.scalar_tensor_tensor(out=Et[:, i * 128:(i + 1) * 128],
                                                                                       in0=ps[bm][:, i * 128:(i + 1) * 128],
                                                                                       scalar=cols[:, C_VG + j:C_VG + j + 1],
                                                                                       in1=At[:, :], op0=ALU.mult, op1=ALU.add),
                                  reads=[("ps", bm), "At", "cols"], writes=["Et"])
                        tr.op("dve", lambda e, fi=fi: e.tensor_tensor(out=Et[:, 0:T], in0=Et[:, 0:T], in1=ug[fi][:, 0:T], op=ALU.mult),
                              reads=["Et", ("ug", fi)], writes=["Et"])
                        jj = 2 * pr + fi
                        tr.op("dve", lambda e, fi=fi, jj=jj, bufi=bufi: e.tensor_tensor(out=yT[bufi][:, jj, 0:T], in0=Et[:, 0:T],
                                                                                    in1=sg[fi][:, 0:T], op=ALU.mult),
                              reads=["Et", ("sg", fi)], writes=[("yT", bufi)])
                flush_outproj(w_out_e)
                pend.append((g, bufi))

            tr.alias(["lnrstd"], ["vbf", "RG"])
            P3K = [("cb", q) for q in range(HT)] + [("cn", 0), ("cn", 1), ("sgb", 0), ("sgb", 1), ("sig", 0), ("sig", 1),
                                                     ("cvb", 0), ("cvb", 1), ("sq", 0), ("sq", 1), "mean", "msq", "cvs", "p3"]
            for k_ in P3K:
                tr.alias([k_], ["vbf", "RG"])
            Tp = T

            def fm(a, n):
                return RG[:, a:a + n]

            for g in range(KD // 8):
                bufi = ygi[0] % 2
                ygi[0] += 1
                for hd in range(8 // HT):
                    j_h0 = g * 8 + hd * HT
                    for pr in range(HT // 2):
                        j0 = j_h0 + 2 * pr
                        ba = inproj_fm(w_in_e, 3 * D + j0 * 128, T)
                        bb = inproj_fm(w_in_e, 4 * D + j0 * 128, T)
                        for fi in range(2):
                            j = j0 + fi
                            sgt = fm(sigb[fi], T)
                            tr.op("act", lambda e, sgt=sgt, fi=fi, bb=bb: e.activation(out=sgt, in_=ps[bb[fi]][:, 0:T], func=AF.Sigmoid),
                                  reads=[("ps", bb[fi])], writes=[("sig", fi)])
                            cacc = fm(cbuf[2 * pr + fi], T)
                            if not is_s:
                                ext = fm(cvb[fi], LC)
                                tr.op("act", lambda e, ext=ext, j=j: e.activation(out=ext[:, 0:NPRE], in_=carry_c[:, j, :], func=AF.Copy),
                                      reads=["carry_c"], writes=[("cvb", fi)])
                                tr.op("dve", lambda e, ext=ext, sgt=sgt, fi=fi, ba=ba: e.tensor_tensor(out=ext[:, NPRE:NPRE + T], in0=ps[ba[fi]][:, 0:T],
                                                                                                    in1=sgt, op=ALU.mult),
                                      reads=[("ps", ba[fi]), ("sig", fi)], writes=[("cvb", fi)])
                                tr.op("act", lambda e, ext=ext, j=j: e.activation(out=carry_c[:, j, :], in_=ext[:, T:T + NPRE], func=AF.Copy),
                                      reads=[("cvb", fi)], writes=["carry_c"])
                                for k in range(CONV_W):
                                    wk = cols[:, C_CW + j * CONV_W + k: C_CW + j * CONV_W + k + 1]
                                    if k == 0:
                                        tr.op("dve", lambda e, cacc=cacc, ext=ext, wk=wk, j=j: e.tensor_scalar(
                                            out=cacc, in0=ext[:, 0:T], scalar1=wk, scalar2=cols[:, C_CB + j:C_CB + j + 1],
                                            op0=ALU.mult, op1=ALU.add),
                                              reads=[("cvb", fi), "cols"], writes=[("cb", 2 * pr + fi)])
                                    else:
                                        tr.op("dve", lambda e, cacc=cacc, ext=ext, wk=wk, k=k: e.scalar_tensor_tensor(
                                            out=cacc, in0=ext[:, k:k + T], scalar=wk, in1=cacc, op0=ALU.mult, op1=ALU.add),
                                              reads=[("cvb", fi), "cols", ("cb", 2 * pr + fi)], writes=[("cb", 2 * pr + fi)])
                            else:
                                LS = NPRE + DS
                                ext3 = RG[:, cvs:cvs + NQ * LS].rearrange("p (q l) -> p q l", q=NQ)
                                tr.dma("sp", lambda e, ext3=ext3, j=j: e.dma_start(
                                    out=ext3[:, :, 0:NPRE], in_=stcT_d[j * 128:(j + 1) * 128, :].rearrange("p (q l) -> p q l", q=NQ)),
                                    "stc", writes=["cvs"])
                                pa3 = ps[ba[fi]][:, 0:T].rearrange("p (q l) -> p q l", q=NQ)
                                sg3 = sgt.rearrange("p (q l) -> p q l", q=NQ)
                                tr.op("dve", lambda e, ext3=ext3, pa3=pa3, sg3=sg3: e.tensor_tensor(out=ext3[:, :, NPRE:LS], in0=pa3, in1=sg3, op=ALU.mult),
                                      reads=[("ps", ba[fi]), ("sig", fi)], writes=["cvs"])
                                gq = fm(cnb[fi], T)
                                gq3 = gq.rearrange("p (q l) -> p q l", q=NQ)
                                tr.op("act", lambda e, gq3=gq3, ext3=ext3: e.activation(out=gq3, in_=ext3[:, :, NPRE:LS], func=AF.Copy),
                                      reads=["cvs"], writes=[("cn", fi)])
                                tr.dma("sp", lambda e, gq=gq, j=j: e.dma_start(out=gluTs_d[j * 128:(j + 1) * 128, :], in_=gq),
                                       f"glo{fi}", reads=[("cn", fi)])
                                c3 = cacc.rearrange("p (q l) -> p q l", q=NQ)
                                for k in range(CONV_W):
                                    wk = cols[:, C_CW + j * CONV_W + k: C_CW + j * CONV_W + k + 1]
                                    if k == 0:
                                        tr.op("dve", lambda e, c3=c3, ext3=ext3, wk=wk, j=j: e.tensor_scalar(
                                            out=c3, in0=ext3[:, :, 0:DS], scalar1=wk, scalar2=cols[:, C_CB + j:C_CB + j + 1],
                                            op0=ALU.mult, op1=ALU.add),
                                              reads=["cvs", "cols"], writes=[("cb", 2 * pr + fi)])
                                    else:
                                        tr.op("dve", lambda e, c3=c3, ext3=ext3, wk=wk, k=k: e.scalar_tensor_tensor(
                                            out=c3, in0=ext3[:, :, k:k + DS], scalar=wk, in1=c3, op0=ALU.mult, op1=ALU.add),
                                              reads=["cvs", "cols", ("cb", 2 * pr + fi)], writes=[("cb", 2 * pr + fi)])
                    b1, b2 = psalloc(2)
                    for q in range(HT):
                        cq = fm(cbuf[q], T)
                        sq = fm(sqb[q % 2], T)
                        tr.op("act", lambda e, cq=cq, sq=sq: e.activation(out=sq, in_=cq, func=AF.Square),
                              reads=[("cb", q)], writes=[("sq", q % 2)])
                        tr.op("pe", lambda e, cq=cq, q=q, b1=b1: e.matmul(ps[b1][:, 0:T], lhsT=ones_f[:, :], rhs=cq, start=(q == 0), stop=(q == HT - 1)),
                              reads=[("cb", q), "ones_f"], writes=[("ps", b1)], sig=(q == HT - 1))
                        tr.op("pe", lambda e, sq=sq, q=q, b2=b2: e.matmul(ps[b2][:, 0:T], lhsT=ones_f[:, :], rhs=sq, start=(q == 0), stop=(q == HT - 1)),
                              reads=[("sq", q % 2), "ones_f"], writes=[("ps", b2)], sig=True)
                    mean = fm(meanb, T)
                    rstd = fm(rstdb, T)
                    msq = fm(msqb, T)
                    nfe = float(HT * 128)
                    tr.op("dve", lambda e, mean=mean, b1=b1: e.tensor_scalar(out=mean, in0=ps[b1][:, 0:T], scalar1=1.0 / nfe, scalar2=None, op0=ALU.mult),
                          reads=[("ps", b1)], writes=["mean"])
                    tr.op("dve", lambda e, mean=mean, msq=msq: e.tensor_tensor(out=msq, in0=mean, in1=mean, op=ALU.mult),
                          reads=["mean"], writes=["msq"])
                    tr.op("dve", lambda e, rstd=rstd, msq=msq, b2=b2: e.scalar_tensor_tensor(out=rstd, in0=ps[b2][:, 0:T], scalar=1.0 / nfe, in1=msq,
                                                                                        op0=ALU.mult, op1=ALU.subtract),
                          reads=[("ps", b2), "msq"], writes=["lnrstd"])
                    tr.op("dve", lambda e, rstd=rstd: e.tensor_scalar(out=rstd, in0=rstd, scalar1=LN_EPS, scalar2=None, op0=ALU.add),
                          reads=["lnrstd"], writes=["lnrstd"])
                    tr.op("act", lambda e, rstd=rstd: e.activation(out=rstd, in_=rstd, func=AF.Sqrt),
                          reads=["lnrstd"], writes=["lnrstd"])
                    tr.op("dve", lambda e, rstd=rstd: e.reciprocal(out=rstd, in_=rstd),
                          reads=["lnrstd"], writes=["lnrstd"])
                    for pr in range(HT // 2):
                        j0 = j_h0 + 2 * pr
                        bgb = inproj_fm(w_in_e, 5 * D + j0 * 128, T)
                        for fi in range(2):
                            j = j0 + fi
                            q = 2 * pr + fi
                            sgbt = fm(sgbb[fi], T)
                            tr.op("act", lambda e, sgbt=sgbt, fi=fi, bgb=bgb: e.activation(out=sgbt, in_=ps[bgb[fi]][:, 0:T], func=AF.Silu),
                                  reads=[("ps", bgb[fi])], writes=[("sgb", fi)])
                            cq = fm(cbuf[q], T)
                            cn = fm(cnb[fi], T)
                            tr.op("dve", lambda e, cq=cq, cn=cn, mean=mean: e.tensor_tensor(out=cn, in0=cq, in1=mean, op=ALU.subtract),
                                  reads=[("cb", q), "mean"], writes=[("cn", fi)])
                            tr.op("dve", lambda e, cn=cn, rstd=rstd: e.tensor_tensor(out=cn, in0=cn, in1=rstd, op=ALU.mult),
                                  reads=[("cn", fi), "lnrstd"], writes=[("cn", fi)])
                            tr.op("act", lambda e, cn=cn, j=j: e.activation(out=cn, in_=cn, func=AF.Silu,
                                                                          scale=cols[:, C_CNG + j:C_CNG + j + 1],
                                                                          bias=cols[:, C_CNB + j:C_CNB + j + 1]),
                                  reads=[("cn", fi), "cols"], writes=[("cn", fi)])
                            jj = hd * HT + q
                            tr.op("dve", lambda e, cn=cn, sgbt=sgbt, jj=jj, bufi=bufi: e.tensor_tensor(out=yT[bufi][:, jj, 0:T], in0=cn, in1=sgbt, op=ALU.mult),
                                  reads=[("cn", fi), ("sgb", fi)], writes=[("yT", bufi)])
                flush_outproj(w_out_e)
                pend.append((KD // 8 + g, bufi))
            flush_outproj(w_out_e)

            if last_p:
                tr.dma("sp", lambda e: e.dma_start(out=gluTp_d.rearrange("(k p) t -> p k t", p=128), in_=carry_c[:, :, :]),
                       "glp", reads=["carry_c"])

            for i in range(nt):
                rmsnorm_to_hT(i, C_G1)

            tr.alias(["p5", "dT"], P3K + ["RG", "lnrstd"])
            for k_ in ([("xcb", 0), ("xcb", 1), ("sA", 0), ("sA", 1), ("sB", 0), ("sB", 1), ("sgc", 0), ("sgc", 1), "t16"]):
                tr.alias([k_], ["p5"])
            for gp, w in enumerate(POOLW):
                nstep = gp + 1
                for pr in range(CGT // 2):
                    cj0 = gp * CGT + 2 * pr
                    bx = inproj_fm(w_in_o, cj0 * 128, T)
                    for fi in range(2):
                        cj = cj0 + fi
                        jl = 2 * pr + fi
                        if not is_s:
                            ext = fm(xcb[fi], LX)
                            tr.op("act", lambda e, ext=ext, cj=cj: e.activation(out=ext[:, 0:PPRE], in_=carry_x[:, cj, :], func=AF.Copy),
                                  reads=["carry_x"], writes=[("xcb", fi)])
                            tr.op("act", lambda e, ext=ext, fi=fi, bx=bx: e.activation(out=ext[:, PPRE:PPRE + T], in_=ps[bx[fi]][:, 0:T], func=AF.Copy),
                                  reads=[("ps", bx[fi])], writes=[("xcb", fi)])
                            tr.op("act", lambda e, ext=ext, cj=cj: e.activation(out=carry_x[:, cj, :], in_=ext[:, T:T + PPRE], func=AF.Copy),
                                  reads=[("xcb", fi)], writes=["carry_x"])
                            L = PPRE + T
                            cur, curk = ext, ("xcb", fi)
                            for m in range(nstep):
                                sh = 1 << m
                                vs_ = (1 << (m + 1)) - 1
                                nxt = fm((sA if m % 2 == 0 else sB)[fi], LX)
                                nk = (("sA", fi) if m % 2 == 0 else ("sB", fi))
                                tr.op("dve", lambda e, nxt=nxt, cur=cur, vs_=vs_, sh=sh, L=L: e.tensor_tensor(
                                    out=nxt[:, vs_:L], in0=cur[:, vs_:L], in1=cur[:, vs_ - sh:L - sh], op=ALU.add),
                                      reads=[curk], writes=[nk])
                                cur, curk = nxt, nk
                            dsl = dT[:, jl * TMAX: jl * TMAX + T]
                            tr.op("dve", lambda e, dsl=dsl, cur=cur, ext=ext, w=w: e.scalar_tensor_tensor(
                                out=dsl, in0=cur[:, PPRE:PPRE + T], scalar=1.0 / w, in1=ext[:, PPRE:PPRE + T],
                                op0=ALU.mult, op1=ALU.subtract),
                                  reads=[curk, ("xcb", fi)], writes=["dT"])
                            if bi == 0:
                                o0 = 128
                                t16v = RG[:, t16:t16 + 16]
                                tr.op("dve", lambda e, t16v=t16v, cur=cur, gp=gp, o0=o0: e.tensor_tensor(
                                    out=t16v, in0=cur[:, PPRE + o0:PPRE + o0 + 16], in1=invc[:, gp, :], op=ALU.mult),
                                      reads=[curk, "invc"], writes=["t16"])
                                tr.op("dve", lambda e, t16v=t16v, ext=ext, dsl=dsl, o0=o0: e.tensor_tensor(
                                    out=dsl[:, o0:o0 + 16], in0=t16v, in1=ext[:, PPRE + o0:PPRE + o0 + 16], op=ALU.subtract),
                                      reads=["t16", ("xcb", fi)], writes=["dT"])
                        else:
                            LS = PPRE + DS
                            ext3 = RG[:, xcb[fi]:xcb[fi] + NQ * LS].rearrange("p (q l) -> p q l", q=NQ)
                            tr.dma("sp", lambda e, ext3=ext3, cj=cj: e.dma_start(
                                out=ext3[:, :, 0:PPRE], in_=stpT_d[cj * 128:(cj + 1) * 128, :].rearrange("p (q l) -> p q l", q=NQ)),
                                f"stp{fi}", writes=[("xcb", fi)])
                            px3 = ps[bx[fi]][:, 0:T].rearrange("p (q l) -> p q l", q=NQ)
                            tr.op("act", lambda e, ext3=ext3, px3=px3: e.activation(out=ext3[:, :, PPRE:LS], in_=px3, func=AF.Copy),
                                  reads=[("ps", bx[fi])], writes=[("xcb", fi)])
                            xo = fm(sgcb[fi], T)
                            tr.op("act", lambda e, xo=xo, fi=fi, bx=bx: e.activation(out=xo, in_=ps[bx[fi]][:, 0:T], func=AF.Copy),
                                  reads=[("ps", bx[fi])], writes=[("sgc", fi)])
                            tr.dma("sp", lambda e, xo=xo, cj=cj: e.dma_start(out=xcTs_d[cj * 128:(cj + 1) * 128, :], in_=xo),
                                   f"xco{fi}", reads=[("sgc", fi)])
                            cur, curk = ext3, ("xcb", fi)
                            for m in range(nstep):
                                sh = 1 << m
                                vs_ = (1 << (m + 1)) - 1
                                a_ = (sA if m % 2 == 0 else sB)[fi]
                                nxt = RG[:, a_:a_ + NQ * LS].rearrange("p (q l) -> p q l", q=NQ)
                                nk = (("sA", fi) if m % 2 == 0 else ("sB", fi))
                                tr.op("dve", lambda e, nxt=nxt, cur=cur, vs_=vs_, sh=sh, LS=LS: e.tensor_tensor(
                                    out=nxt[:, :, vs_:LS], in0=cur[:, :, vs_:LS], in1=cur[:, :, vs_ - sh:LS - sh], op=ALU.add),
                                      reads=[curk], writes=[nk])
                                cur, curk = nxt, nk
                            dsl3 = dT[:, jl * TMAX: jl * TMAX + T].rearrange("p (q l) -> p q l", q=NQ)
                            tr.op("dve", lambda e, dsl3=dsl3, cur=cur, ext3=ext3, w=w, LS=LS: e.scalar_tensor_tensor(
                                out=dsl3, in0=cur[:, :, PPRE:LS], scalar=1.0 / w, in1=ext3[:, :, PPRE:LS],
                                op0=ALU.mult, op1=ALU.subtract),
                                  reads=[curk, ("xcb", fi)], writes=["dT"])
                for half in range(CGT // 8):
                    bufi = ygi[0] % 2
                    ygi[0] += 1
                    for pr in range(4):
                        dj0 = half * 8 + 2 * pr
                        bk = psalloc(2)
                        s = wload(pool_w[gp * CGT * 128:(gp + 1) * CGT * 128, dj0 * 128: dj0 * 128 + 256], CGT, 256)
                        for fi in range(2):
                            for kc in range(CGT):
                                lhsT = wblk[s][:, kc * 256 + fi * 128: kc * 256 + (fi + 1) * 128]
                                rhs = dT[:, kc * TMAX: kc * TMAX + T]
                                out = ps[bk[fi]][:, 0:T]
                                tr.op("pe", lambda e, out=out, lhsT=lhsT, rhs=rhs, st=(kc == 0), sp=(kc == CGT - 1):
                                      e.matmul(out, lhsT=lhsT, rhs=rhs, start=st, stop=sp),
                                      reads=[("wb", s), "dT"], writes=[("ps", bk[fi])], sig=(kc == CGT - 1))
                        bgc = inproj_fm(w_in_o, 2 * D + (gp * CGT + dj0) * 128, T)
                        for fi in range(2):
                            fgl = gp * CGT + dj0 + fi
                            sgct = fm(sgcb[fi], T)
                            tr.op("act", lambda e, sgct=sgct, fi=fi, bgc=bgc: e.activation(out=sgct, in_=ps[bgc[fi]][:, 0:T], func=AF.Silu),
                                  reads=[("ps", bgc[fi])], writes=[("sgc", fi)])
                            jj = 2 * pr + fi
                            tr.op("dve", lambda e, sgct=sgct, fi=fi, bk=bk, fgl=fgl, jj=jj, bufi=bufi: e.scalar_tensor_tensor(
                                out=yT[bufi][:, jj, 0:T], in0=ps[bk[fi]][:, 0:T], scalar=cols[:, C_PS + fgl:C_PS + fgl + 1],
                                in1=sgct, op0=ALU.mult, op1=ALU.mult),
                                  reads=[("ps", bk[fi]), ("sgc", fi), "cols"], writes=[("yT", bufi)])
                    flush_outproj(w_out_o)
                    pend.append(((gp * CGT) // 8 + half, bufi))
            flush_outproj(w_out_o)
            if last_p:
                tr.dma("sp", lambda e: e.dma_start(out=xcTp_d.rearrange("(k p) t -> p k t", p=128), in_=carry_x[:, :, :]),
                       "xcp", reads=["carry_x"])
            tr.alias(["RG"], ["p5", "dT", ("xcb", 0), ("xcb", 1), ("sA", 0), ("sA", 1), ("sB", 0), ("sB", 1), ("sgc", 0), ("sgc", 1), "t16"])

            tr.dma("sp", lambda e: e.dma_start(out=vstage[:, :], in_=nf_d[0:1, :].partition_broadcast(128)), "nfb", writes=["vstage"])
            for i, tl in enumerate(btiles):
                if tl == 0:
                    continue
                for cb in range(NCB):
                    tr.op("act", lambda e, cb=cb, i=i: e.activation(out=junk[:, :], in_=xres[:, i, cb * 512:(cb + 1) * 512],
                                                                  func=AF.Square, accum_out=small[:, cb:cb + 1]),
                          reads=[("xres", i)], writes=["junk", "ssc"])
                tr.op("dve", lambda e: e.reduce_sum(out=small[:, 16:17], in_=small[:, 0:NCB], axis=AX.X),
                      reads=["ssc"], writes=["ss"])
                tr.op("dve", lambda e: e.tensor_scalar(out=small[:, 17:18], in0=small[:, 16:17], scalar1=1.0 / D,
                                                       scalar2=RMS_EPS, op0=ALU.mult, op1=ALU.add),
                      reads=["ss"], writes=["ss2"])
                tr.op("act", lambda e: e.activation(out=small[:, 19:20], in_=small[:, 17:18], func=AF.Sqrt),
                      reads=["ss2"], writes=["ss3"])
                tr.op("dve", lambda e: e.reciprocal(out=small[:, 18:19], in_=small[:, 19:20]),
                      reads=["ss3"], writes=["rstd"])
                for cb in range(NCB):
                    xs = xres[:, i, cb * 512:(cb + 1) * 512]
                    tr.op("dve", lambda e, xs=xs, cb=cb: e.scalar_tensor_tensor(out=xs, in0=xs, scalar=small[:, 18:19],
                                                                              in1=vstage[:, cb * 512:(cb + 1) * 512],
                                                                              op0=ALU.mult, op1=ALU.mult),
                          reads=[("xres", i), "rstd", "vstage"], writes=[("xres", i)])
                ot = tl - 1
                tr.dma("sp", lambda e, i=i, ot=ot: e.dma_start(out=y_d[ot * 128:(ot + 1) * 128, :], in_=xres[:, i, :]),
                       f"yo{i}", reads=[("xres", i)])

        tr.dma("sp", lambda e: e.dma_start(out=convs_old[:, :], in_=stc_tm[:, DS * D:NPRE * D]), "cso")
        tr.dma("sp", lambda e: e.dma_start(out=pools_old[:, :], in_=stp_tm[:, DS * 2 * D:PPRE * 2 * D]), "pso")

        tr.final_wait("sp")
        print("bass ops emitted:", tr.nops)
    return nc


_PROG_CACHE = {}


def _prep_inputs(inp):
    xp = np.asarray(inp["x_prompt"], np.float32)
    xs = np.asarray(inp["x_sample"], np.float32)
    NB, SEQ, D = xp.shape
    DB, DS, _ = xs.shape
    KD = D // 128
    KC = 2 * KD
    half = SEQ // 2
    NPT = half // 128
    NQ = DB // NCORES
    assert NB * 2 == NCORES and NQ * DS == 128

    def colmaj(v):
        v = np.asarray(v, np.float32).reshape(-1, 128)
        return np.ascontiguousarray(v.T)

    cw = np.asarray(inp["conv_w"], np.float32)[0]
    cw_cols = np.ascontiguousarray(cw.T.reshape(KD, 128, CONV_W).transpose(1, 0, 2).reshape(128, KD * CONV_W))
    cols = np.concatenate([
        colmaj(inp["norm_in"][0]), colmaj(inp["norm_in"][1]), colmaj(inp["v_norm_g"][0]), colmaj(inp["v_norm_b"][0]),
        colmaj(inp["conv_b"][0]), colmaj(inp["conv_norm_g"][0]), colmaj(inp["conv_norm_b"][0]),
        colmaj(inp["pool_scale"][0]), np.zeros((128, KD), np.float32), cw_cols], axis=1)
    cols = np.ascontiguousarray(cols, np.float32)

    sgw = np.asarray(inp["sgu_w"], np.float32)[0]
    sgb = np.asarray(inp["sgu_b"], np.float32)[0]
    wT_p = np.ascontiguousarray(sgw.transpose(2, 0, 1).reshape(128, NHEAD * 128))
    wT_s = np.zeros((128, NHEAD, 128), np.float32)
    for q in range(NQ):
        wT_s[q * DS:(q + 1) * DS, :, q * DS:(q + 1) * DS] = sgw[:, :DS, :DS].transpose(2, 0, 1)
    sgu = np.concatenate([wT_p, wT_s.reshape(128, NHEAD * 128)], axis=0)
    s_i = np.arange(128)[:, None]
    t_i = np.arange(128)[None, :]
    mask_p = (s_i <= t_i).astype(np.float32)
    mask_s = ((s_i // DS == t_i // DS) & (s_i <= t_i)).astype(np.float32)
    mask = np.concatenate([mask_p, mask_s], axis=0)
    sgub = np.stack([sgb.reshape(-1), np.tile(sgb[:, :DS], (1, NQ)).reshape(-1)], axis=0).astype(np.float32)
    ident = np.eye(128, dtype=np.float32)

    shared = {
        "w_in_e": np.asarray(inp["w_in_even"], np.float32)[0],
        "w_out_e": np.asarray(inp["w_out_even"], np.float32)[0],
        "w_in_o": np.asarray(inp["w_in_odd"], np.float32)[0],
        "pool_w": np.asarray(inp["pool_w"], np.float32)[0].reshape(4 * (2 * D // 4), 2 * D // 4),
        "w_out_o": np.asarray(inp["w_out_odd"], np.float32)[0],
        "cols": cols, "nf": np.asarray(inp["norm_f"], np.float32).reshape(1, D),
        "vg": np.asarray(inp["v_norm_g"], np.float32).reshape(1, D),
        "vb": np.asarray(inp["v_norm_b"], np.float32).reshape(1, D),
        "sgub": sgub, "sgu": sgu, "mask": mask, "ident": ident,
    }
    stc = np.asarray(inp["state_conv"], np.float32)[0]
    stp = np.asarray(inp["state_pool"], np.float32)[0]
    in_maps = []
    for c in range(NCORES):
        b, hf = c // 2, c % 2
        halo = xp[b, half - 128:half] if hf == 1 else np.zeros((128, D), np.float32)
        xin = np.concatenate([halo, xp[b, hf * half:(hf + 1) * half], xs[c * NQ:(c + 1) * NQ].reshape(128, D)], axis=0)
        sc = stc[c * NQ:(c + 1) * NQ]
        sp_ = stp[c * NQ:(c + 1) * NQ]
        m = dict(shared)
        m["xin"] = np.ascontiguousarray(xin)
        m["posv"] = np.ascontiguousarray(np.broadcast_to((hf * half + np.arange(16, dtype=np.float32))[None, :], (128, 16)))
        m["stcT"] = np.ascontiguousarray(sc.transpose(2, 0, 1).reshape(D, NQ * NPRE))
        m["stpT"] = np.ascontiguousarray(sp_.transpose(2, 0, 1).reshape(2 * D, NQ * PPRE))
        m["stc_tm"] = np.ascontiguousarray(sc.reshape(NQ, NPRE * D))
        m["stp_tm"] = np.ascontiguousarray(sp_.reshape(NQ, PPRE * 2 * D))
        in_maps.append(m)
    return in_maps, (NB, SEQ, D, DB, DS, NPT, NQ)


def kernel(**inputs):
    in_maps, (NB, SEQ, D, DB, DS, NPT, NQ) = _prep_inputs(inputs)
    key = (D, NPT)
    if key not in _PROG_CACHE:
        _PROG_CACHE[key] = build_program(D, NPT)
    nc = _PROG_CACHE[key]
    res = run_bass_kernel_spmd(nc, in_maps, core_ids=list(range(NCORES))).results
    half = SEQ // 2
    y_prompt = np.zeros((NB, SEQ, D), np.float32)
    y_sample = np.zeros((DB, DS, D), np.float32)
    new_v = np.zeros((1, DB, DS, D), np.float32)
    conv_p = np.zeros((1, NB, NPRE, D), np.float32)
    conv_s = np.zeros((1, DB, NPRE, D), np.float32)
    pool_p = np.zeros((1, NB, PPRE, 2 * D), np.float32)
    pool_s = np.zeros((1, DB, PPRE, 2 * D), np.float32)
    for c in range(NCORES):
        r = res[c]
        b, hf = c // 2, c % 2
        y = r["y"].reshape(NPT + 1, 128, D)
        y_prompt[b, hf * half:(hf + 1) * half] = y[:NPT].reshape(half, D)
        qs = slice(c * NQ, (c + 1) * NQ)
        y_sample[qs] = y[NPT].reshape(NQ, DS, D)
        new_v[0, qs] = r["vout"].reshape(NQ, DS, D)
        conv_s[0, qs, :NPRE - DS] = r["convs_old"].reshape(NQ, NPRE - DS, D)
        conv_s[0, qs, NPRE - DS:] = r["gluTs"].T.reshape(NQ, DS, D)
        pool_s[0, qs, :PPRE - DS] = r["pools_old"].reshape(NQ, PPRE - DS, 2 * D)
        pool_s[0, qs, PPRE - DS:] = r["xcTs"].T.reshape(NQ, DS, 2 * D)
        if hf == 1:
            conv_p[0, b] = r["gluTp"].T
            pool_p[0, b] = r["xcTp"].T
    return (y_prompt, y_sample, new_v, conv_p, conv_s, pool_p, pool_s)
```

```python
import numpy as np
from contextlib import ExitStack
import concourse.bass as bass
import concourse.mybir as mybir
from concourse.bass_utils import run_bass_kernel_spmd

F32 = mybir.dt.float32
BF16 = mybir.dt.bfloat16
AF = mybir.ActivationFunctionType
ALU = mybir.AluOpType
AX = mybir.AxisListType

CONV_W = 31
NPRE = CONV_W - 1
POOLW = (2, 4, 8, 16)
PPRE = 15
RMS_EPS = 1e-6
LN_EPS = 1e-5
NHEAD = 8
BLK = 3
NCORES = 8
ENGS = ("pe", "act", "dve", "pool", "sp")


class Tr:
    def __init__(self, nc, es):
        self.nc = nc
        self.es = es
        self.E = {"pe": nc.tensor, "act": nc.scalar, "dve": nc.vector, "pool": nc.gpsimd, "sp": nc.sync}
        self.nops = 0
        self.cnt = {}
        self.sem = {}
        self.seen = {e: {} for e in ENGS}
        self.W = {}
        self.R = {}
        for e in ("pe", "act", "dve", "pool"):
            self._mksem(e)

    def _mksem(self, name):
        if name not in self.sem:
            self.sem[name] = self.es.enter_context(self.nc.semaphore("s_" + name))
            self.cnt[name] = 0

    def _deps(self, eng, reads, writes):
        need = {}
        for k in reads:
            for s, t in self.W.get(k, {}).items():
                need[s] = max(need.get(s, 0), t)
        for k in writes:
            for s, t in self.W.get(k, {}).items():
                need[s] = max(need.get(s, 0), t)
            for s, t in self.R.get(k, {}).items():
                need[s] = max(need.get(s, 0), t)
        for s, t in need.items():
            if s == "pe" and eng == "pe":
                continue
            if self.seen[eng].get(s, 0) >= t:
                continue
            self.seen[eng][s] = t
            self.E[eng].wait_ge(self.sem[s], t)

    def _mark(self, s, tick, reads, writes):
        for k in reads:
            d = self.R.setdefault(k, {})
            d[s] = max(d.get(s, 0), tick)
        for k in writes:
            self.W[k] = {s: tick}
            self.R[k] = {}

    def op(self, eng, fn, reads=(), writes=(), sig=True):
        self._deps(eng, reads, writes)
        if sig:
            self.cnt[eng] += 1
            tick = self.cnt[eng]
            fn(self.E[eng]).then_inc(self.sem[eng], 1)
        else:
            tick = self.cnt[eng] + 1
            fn(self.E[eng])
        self.nops += 1
        self._mark(eng, tick, reads, writes)

    def dma(self, eng, fn, slot, reads=(), writes=()):
        self._mksem(slot)
        self._deps(eng, reads, writes)
        self.cnt[slot] += 16
        fn(self.E[eng]).then_inc(self.sem[slot], 16)
        self.nops += 1
        self._mark(slot, self.cnt[slot], reads, writes)

    def alias(self, new_keys, old_keys):
        w, r = {}, {}
        for k in old_keys:
            for s, t in self.W.get(k, {}).items():
                w[s] = max(w.get(s, 0), t)
            for s, t in self.R.get(k, {}).items():
                r[s] = max(r.get(s, 0), t)
        for k in new_keys:
            self.W[k] = dict(w)
            self.R[k] = dict(r)

    def final_wait(self, eng):
        for s, c in self.cnt.items():
            if c > 0 and self.seen[eng].get(s, 0) < c:
                self.E[eng].wait_ge(self.sem[s], c)
                self.seen[eng][s] = c

    def emit(self, block):
        nc = self.nc

        def run(e, items):
            for it in items:
                if it[0] == "w":
                    e.wait_ge(self.sem[it[1]], it[2])
                else:
                    ins = it[1](e)
                    if it[2] is not None:
                        ins.then_inc(self.sem[it[2]], it[3])

        q = self.q

        @block.tensor
        def _(e):
            run(e, q["pe"])

        @block.scalar
        def _(e):
            run(e, q["act"])

        @block.vector
        def _(e):
            run(e, q["dve"])

        @block.gpsimd
        def _(e):
            run(e, q["pool"])

        @block.sync
        def _(e):
            run(e, q["sp"])


def build_program(D, NPT):
    KD = D // 128
    HT = KD // NHEAD
    KC = 2 * KD
    CGT = KC // 4
    NCB = D // 512
    assert HT >= 2 and HT % 2 == 0 and KD % 16 == 0
    NTIN = NPT + 2
    NTOUT = NPT + 1
    TMAX = BLK * 128
    KH = 16
    NQ = 16
    DS = 8

    nc = bass.Bass("TRN2", target_bir_lowering=False)

    def din(name, shape):
        return nc.dram_tensor(name, list(shape), F32, kind="ExternalInput").ap()

    def dout(name, shape):
        return nc.dram_tensor(name, list(shape), F32, kind="ExternalOutput").ap()

    xin = din("xin", [NTIN * 128, D])
    w_in_e = din("w_in_e", [D, 6 * D])
    w_out_e = din("w_out_e", [2 * D, D])
    w_in_o = din("w_in_o", [D, 4 * D])
    pool_w = din("pool_w", [4 * 2 * KD // 4 * 128 * 1, CGT * 128])
    w_out_o = din("w_out_o", [2 * D, D])
    NCOL = 8 * KD + KC + KD * CONV_W
    cols_d = din("cols", [128, NCOL])
    nf_d = din("nf", [1, D])
    vg_d = din("vg", [1, D])
    vb_d = din("vb", [1, D])
    sgub_d = din("sgub", [2, NHEAD * 128])
    sgu_d = din("sgu", [2 * 128, NHEAD * 128])
    mask_d = din("mask", [2 * 128, 128])
    ident_d = din("ident", [128, 128])
    posv_d = din("posv", [128, 16])
    stcT_d = din("stcT", [D, NQ * NPRE])
    stpT_d = din("stpT", [2 * D, NQ * PPRE])
    stc_tm = din("stc_tm", [NQ, NPRE * D])
    stp_tm = din("stp_tm", [NQ, PPRE * 2 * D])

    y_d = dout("y", [NTOUT * 128, D])
    vout_d = dout("vout", [128, D])
    gluTs_d = dout("gluTs", [D, 128])
    xcTs_d = dout("xcTs", [2 * D, 128])
    gluTp_d = dout("gluTp", [D, NPRE])
    xcTp_d = dout("xcTp", [2 * D, PPRE])
    convs_old = dout("convs_old", [NQ, (NPRE - DS) * D])
    pools_old = dout("pools_old", [NQ, (PPRE - DS) * 2 * D])

    C_G0, C_G1, C_VG, C_VB, C_CB, C_CNG, C_CNB = [i * KD for i in range(7)]
    C_PS = 7 * KD
    C_INVW = C_PS + KC
    C_CW = 8 * KD + KC

    with ExitStack() as es:
        def sb(name, shape, dt=F32):
            return es.enter_context(nc.sbuf_tensor("sb_" + name, list(shape), dt))

        tr = Tr(nc, es)
        ps = [es.enter_context(nc.psum_tensor(f"ps{b}", [128, 512], F32)) for b in range(8)]
        ps_rr = [0]

        def psalloc(n):
            r = [(ps_rr[0] + i) % 8 for i in range(n)]
            ps_rr[0] = (ps_rr[0] + n) % 8
            return r

        RGN = 8192
        xres = sb("xres", [128, BLK, D])
        hT = sb("hT", [128, KD, TMAX], BF16)
        RG = sb("RG", [128, RGN])
        vstage = sb("vstage", [128, D])
        yT = [sb(f"yT{i}", [128, 8, TMAX], BF16) for i in range(2)]
        NWB = 4
        wblk = [sb(f"wb{i}", [128, 4096], BF16) for i in range(NWB)]
        cols = sb("cols", [128, NCOL])
        rs_bc = sb("rs_bc", [128, NHEAD * 128])
        sb_bc = sb("sb_bc", [128, NHEAD * 128])
        wTm = sb("wTm", [128, NHEAD * 128], BF16)
        wTs = sb("wTs", [128, BLK, NHEAD * 128], BF16)
        nm = sb("nm", [128, BLK, 128], BF16)
        ident = sb("ident", [128, 128])
        ones_f = sb("ones_f", [128, 128])
        ones_b = sb("ones_b", [128, 128], BF16)
        maskt = sb("maskt", [128, 128])
        carry_c = sb("carry_c", [128, KD, NPRE])
        carry_x = sb("carry_x", [128, KC, PPRE])
        invc = sb("invc", [128, 4, 16])
        posv = sb("posv", [128, 16])
        ug = [sb(f"ug{i}", [128, TMAX]) for i in range(2)]
        sg = [sb(f"sg{i}", [128, TMAX]) for i in range(2)]
        Et = sb("Et", [128, TMAX])
        At = sb("At", [128, 128])
        small = sb("small", [128, 64])
        bnst = sb("bnst", [128, BLK, 8, 6])
        junk = sb("junk", [128, 512])

        vbf = RG[:, 0:BLK * D // 2].bitcast(BF16)
        off = [0]

        def rg(n):
            a = off[0]
            off[0] += n
            assert off[0] <= RGN, off[0]
            return a

        off[0] = 0
        LC = NPRE + TMAX
        cbuf = [[rg(TMAX) for _ in range(HT)] for _ in range(2)]
        cvb = [rg(LC) for _ in range(2)]
        sqb = [rg(TMAX) for _ in range(2)]
        meanb, rstdb, msqb = rg(TMAX), rg(TMAX), rg(TMAX)
        sgbb = [rg(TMAX) for _ in range(2)]
        sigb = [rg(TMAX) for _ in range(2)]
        cnb = [rg(TMAX) for _ in range(2)]
        cvs = cbuf[1][0]
        assert NQ * (NPRE + DS) <= HT * TMAX
        p3_end = off[0]
        off[0] = 0
        dT_off = rg(CGT * TMAX // 2)
        LX = PPRE + TMAX
        xcb = [rg(LX) for _ in range(2)]
        sA = [rg(LX) for _ in range(2)]
        sB = [rg(LX) for _ in range(2)]
        sgcb = [rg(TMAX) for _ in range(2)]
        t16 = rg(16)
        p5_end = off[0]
        dT = RG[:, dT_off:dT_off + CGT * TMAX // 2].bitcast(BF16)

        RGK = ["RG"]

        wb_i = [0]

        def wload(src_ap, nk, width):
            s = wb_i[0] % NWB
            wb_i[0] += 1
            dst = wblk[s][:, 0:nk * width].rearrange("p (k c) -> p k c", k=nk)
            src = src_ap.rearrange("(k p) c -> p k c", p=128)
            tr.dma("pool", lambda e, dst=dst, src=src: e.dma_start(out=dst, in_=src),
                   f"wb{s}", writes=[("wb", s)])
            return s

        def inproj_fm(wd, col0, T, nbanks_out=None):
            bk = psalloc(2)
            nkh = KD // KH
            for kh in range(nkh):
                s = wload(wd[kh * KH * 128:(kh + 1) * KH * 128, col0:col0 + 256], KH, 256)
                for fi in range(2):
                    for kc in range(KH):
                        last = (kh == nkh - 1 and kc == KH - 1)
                        lhsT = wblk[s][:, kc * 256 + fi * 128: kc * 256 + (fi + 1) * 128]
                        rhs = hT[:, kh * KH + kc, 0:T]
                        out = ps[bk[fi]][:, 0:T]
                        tr.op("pe", lambda e, out=out, lhsT=lhsT, rhs=rhs, st=(kh == 0 and kc == 0), sp=last:
                              e.matmul(out, lhsT=lhsT, rhs=rhs, start=st, stop=sp),
                              reads=[("wb", s), "hT"], writes=[("ps", bk[fi])],
                              sig=(last or kc == KH - 1))
            return bk

        def outproj(wd, row0, yt, ytk, nt):
            for cb in range(NCB):
                s = wload(wd[row0:row0 + 1024, cb * 512:(cb + 1) * 512], 8, 512)
                bk = psalloc(nt)
                for i in range(nt):
                    for kc in range(8):
                        lhsT = yt[:, kc, i * 128:(i + 1) * 128]
                        rhs = wblk[s][:, kc * 512:(kc + 1) * 512]
                        out = ps[bk[i]][:, 0:512]
                        tr.op("pe", lambda e, out=out, lhsT=lhsT, rhs=rhs, st=(kc == 0), sp=(kc == 7):
                              e.matmul(out, lhsT=lhsT, rhs=rhs, start=st, stop=sp),
                              reads=[("wb", s), ytk], writes=[("ps", bk[i])], sig=(kc == 7))
                for i in range(nt):
                    xs = xres[:, i, cb * 512:(cb + 1) * 512]
                    pin = ps[bk[i]][:, 0:512]
                    tr.op("dve", lambda e, xs=xs, pin=pin: e.tensor_tensor(out=xs, in0=pin, in1=xs, op=ALU.add),
                          reads=[("ps", bk[i]), ("xres", i)], writes=[("xres", i)])

        def rmsnorm_to_hT(i, gc0):
            ssc = small[:, 0:NCB]
            for cb in range(NCB):
                tr.op("act", lambda e, cb=cb: e.activation(out=junk[:, :], in_=xres[:, i, cb * 512:(cb + 1) * 512],
                                                       func=AF.Square, accum_out=small[:, cb:cb + 1]),
                      reads=[("xres", i)], writes=["junk", "ssc"])
            tr.op("dve", lambda e: e.reduce_sum(out=small[:, 16:17], in_=ssc, axis=AX.X),
                  reads=["ssc"], writes=["ss"])
            tr.op("dve", lambda e: e.tensor_scalar(out=small[:, 17:18], in0=small[:, 16:17], scalar1=1.0 / D,
                                                   scalar2=RMS_EPS, op0=ALU.mult, op1=ALU.add),
                  reads=["ss"], writes=["ss2"])
            tr.op("act", lambda e: e.activation(out=small[:, 19:20], in_=small[:, 17:18], func=AF.Sqrt),
                  reads=["ss2"], writes=["ss3"])
            tr.op("dve", lambda e: e.reciprocal(out=small[:, 18:19], in_=small[:, 19:20]),
                  reads=["ss3"], writes=["rstd"])
            tr.op("act", lambda e: e.activation(out=vstage[:, :], in_=xres[:, i, :], func=AF.Copy,
                                                scale=small[:, 18:19]),
                  reads=[("xres", i), "rstd"], writes=["vstage"])
            for k4 in range(KD // 4):
                b = psalloc(1)[0]
                for q in range(4):
                    kc = k4 * 4 + q
                    tr.op("pe", lambda e, kc=kc, q=q, b=b: e.transpose(out=ps[b][:, q * 128:(q + 1) * 128],
                                                                     in_=vstage[:, kc * 128:(kc + 1) * 128],
                                                                     identity=ident[:, :]),
                          reads=["vstage", "ident"], writes=[("ps", b)], sig=(q == 3))
                for q in range(4):
                    kc = k4 * 4 + q
                    eng = "act" if q % 2 == 0 else "dve"
                    o = hT[:, kc, i * 128:(i + 1) * 128]
                    pin = ps[b][:, q * 128:(q + 1) * 128]
                    gcol = cols[:, gc0 + kc:gc0 + kc + 1]
                    if eng == "act":
                        tr.op("act", lambda e, o=o, pin=pin, gcol=gcol: e.activation(out=o, in_=pin, func=AF.Copy, scale=gcol),
                              reads=[("ps", b), "cols"], writes=["hT"])
                    else:
                        tr.op("dve", lambda e, o=o, pin=pin, gcol=gcol: e.tensor_scalar(out=o, in0=pin, scalar1=gcol, scalar2=None, op0=ALU.mult),
                              reads=[("ps", b), "cols"], writes=["hT"])

        tr.dma("sp", lambda e: e.dma_start(out=cols[:, :], in_=cols_d[:, :]), "c0a", writes=["cols"])
        tr.dma("sp", lambda e: e.dma_start(out=ident[:, :], in_=ident_d[:, :]), "c0b", writes=["ident"])
        tr.dma("sp", lambda e: e.dma_start(out=posv[:, :], in_=posv_d[:, :]), "c0c", writes=["posv"])
        tr.op("dve", lambda e: e.memset(ones_f[:, :], 1.0), writes=["ones_f"])
        tr.op("dve", lambda e: e.memset(ones_b[:, :], 1.0), writes=["ones_b"])
        tr.op("dve", lambda e: e.memset(carry_c[:, :, :], 0.0), writes=["carry_c"])
        tr.op("dve", lambda e: e.memset(carry_x[:, :, :], 0.0), writes=["carry_x"])
        for g, w in enumerate(POOLW):
            tr.op("dve", lambda e, g=g, w=w: e.tensor_scalar(out=invc[:, g, :], in0=posv[:, :], scalar1=1.0, scalar2=float(w),
                                                           op0=ALU.add, op1=ALU.min),
                  reads=["posv"], writes=["invc"])
        tr.op("dve", lambda e: e.reciprocal(out=invc[:, :, :], in_=invc[:, :, :]), reads=["invc"], writes=["invc"])

        def load_type_consts(ty):
            tr.dma("sp", lambda e: e.dma_start(out=vstage[:, 0:NHEAD * 128], in_=sgu_d[ty * 128:(ty + 1) * 128, :]),
                   "c1a", writes=["vstage"])
            tr.dma("sp", lambda e: e.dma_start(out=maskt[:, :], in_=mask_d[ty * 128:(ty + 1) * 128, :]),
                   "c1b", writes=["maskt"])
            tr.dma("sp", lambda e: e.dma_start(out=sb_bc[:, :], in_=sgub_d[ty:ty + 1, :].partition_broadcast(128)),
                   "c1c", writes=["sb_bc"])
            for h in range(NHEAD):
                tr.op("dve", lambda e, h=h: e.tensor_tensor(out=wTm[:, h * 128:(h + 1) * 128],
                                                          in0=vstage[:, h * 128:(h + 1) * 128], in1=maskt[:, :], op=ALU.mult),
                      reads=["vstage", "maskt"], writes=["wTm"])
            for hh in range(NHEAD * 128 // 512):
                b = psalloc(1)[0]
                tr.op("pe", lambda e, b=b, hh=hh: e.matmul(ps[b][:, 0:512], lhsT=ones_b[:, :], rhs=wTm[:, hh * 512:(hh + 1) * 512],
                                                         start=True, stop=True),
                      reads=["ones_b", "wTm"], writes=[("ps", b)])
                tr.op("act", lambda e, b=b, hh=hh: e.activation(out=rs_bc[:, hh * 512:(hh + 1) * 512], in_=ps[b][:, 0:512], func=AF.Copy),
                      reads=[("ps", b)], writes=["rs_bc"])

        tiles = list(range(NPT + 1))
        blocks = [("p", tiles[i:i + BLK]) for i in range(0, len(tiles), BLK)]
        blocks.append(("s", [NPT + 1]))
        cur_type = [None]

        for bi, (bty, btiles) in enumerate(blocks):
            nt = len(btiles)
            T = nt * 128
            is_s = (bty == "s")
            last_p = (not is_s) and (btiles[-1] == NPT)
            ty = 1 if is_s else 0

            for i, tl in enumerate(btiles):
                tr.dma("sp", lambda e, i=i, tl=tl: e.dma_start(out=xres[:, i, :], in_=xin[tl * 128:(tl + 1) * 128, :]),
                       f"xin{i}", writes=[("xres", i)])
            if cur_type[0] != ty:
                load_type_consts(ty)
                cur_type[0] = ty
            for i in range(nt):
                rmsnorm_to_hT(i, C_G0)

            tr.alias(["vbf"], ["RG"])
            for cb in range(NCB):
                bk = psalloc(nt)
                nq = KD // 8
                for q in range(nq):
                    s = wload(w_in_e[q * 1024:(q + 1) * 1024, D + cb * 512: D + (cb + 1) * 512], 8, 512)
                    for i in range(nt):
                        for kc in range(8):
                            last = (q == nq - 1 and kc == 7)
                            lhsT = hT[:, q * 8 + kc, i * 128:(i + 1) * 128]
                            rhs = wblk[s][:, kc * 512:(kc + 1) * 512]
                            out = ps[bk[i]][:, 0:512]
                            tr.op("pe", lambda e, out=out, lhsT=lhsT, rhs=rhs, st=(q == 0 and kc == 0), sp=last:
                                  e.matmul(out, lhsT=lhsT, rhs=rhs, start=st, stop=sp),
                                  reads=[("wb", s), "hT"], writes=[("ps", bk[i])], sig=(kc == 7))
                for i in range(nt):
                    pin = ps[bk[i]][:, 0:512]
                    if is_s:
                        o = vstage[:, cb * 512:(cb + 1) * 512]
                        tr.op("act", lambda e, o=o, pin=pin: e.activation(out=o, in_=pin, func=AF.Gelu),
                              reads=[("ps", bk[i])], writes=["vstage"])
                        tr.op("dve", lambda e, o=o, i=i, cb=cb: e.bn_stats(out=bnst[:, i, cb, :], in_=o),
                              reads=["vstage"], writes=["bnst"])
                        ob = vbf[:, i * D + cb * 512: i * D + (cb + 1) * 512]
                        tr.op("dve", lambda e, o=o, ob=ob: e.tensor_copy(out=ob, in_=o),
                              reads=["vstage"], writes=["vbf"])
                    else:
                        ob = vbf[:, i * D + cb * 512: i * D + (cb + 1) * 512]
                        tr.op("act", lambda e, ob=ob, pin=pin: e.activation(out=ob, in_=pin, func=AF.Gelu),
                              reads=[("ps", bk[i])], writes=["vbf"])
                        tr.op("dve", lambda e, ob=ob, i=i, cb=cb: e.bn_stats(out=bnst[:, i, cb, :], in_=ob),
                              reads=["vbf"], writes=["bnst"])
            for i in range(nt):
                mv = small[:, 20 + 4 * i: 22 + 4 * i]
                rs_ = small[:, 22 + 4 * i: 23 + 4 * i]
                ngm = small[:, 23 + 4 * i: 24 + 4 * i]
                tr.op("dve", lambda e, mv=mv, i=i: e.bn_aggr(out=mv, in_=bnst[:, i, 0:NCB, :]),
                      reads=["bnst"], writes=[("mv", i)])
                tr.op("dve", lambda e, mv=mv, rs_=rs_: e.tensor_scalar(out=rs_, in0=mv[:, 1:2], scalar1=LN_EPS, scalar2=None, op0=ALU.add),
                      reads=[("mv", i)], writes=[("mv", i)])
                tr.op("act", lambda e, rs_=rs_: e.activation(out=rs_, in_=rs_, func=AF.Sqrt),
                      reads=[("mv", i)], writes=[("mv", i)])
                tr.op("dve", lambda e, rs_=rs_: e.reciprocal(out=rs_, in_=rs_),
                      reads=[("mv", i)], writes=[("mv", i)])
                tr.op("dve", lambda e, mv=mv, ngm=ngm: e.tensor_scalar(out=ngm, in0=mv[:, 0:1], scalar1=-1.0, scalar2=None, op0=ALU.mult),
                      reads=[("mv", i)], writes=[("mv", i)])
                tr.op("dve", lambda e, i=i, rs_=rs_: e.tensor_scalar(out=wTs[:, i, :], in0=wTm[:, :], scalar1=rs_, scalar2=None, op0=ALU.mult),
                      reads=[("mv", i), "wTm"], writes=["wTs"])
                tr.op("dve", lambda e, i=i, ngm=ngm: e.tensor_scalar(out=nm[:, i, :], in0=ones_f[:, :], scalar1=ngm, scalar2=None, op0=ALU.mult),
                      reads=[("mv", i), "ones_f"], writes=["nm"])
            if is_s:
                mv = small[:, 20:22]
                rs_ = small[:, 22:23]
                for cb in range(NCB):
                    for hf in range(2):
                        c0 = cb * 512 + hf * 256
                        gt = ug[hf][:, 0:256]
                        bt = sg[hf][:, 0:256]
                        tr.dma("sp", lambda e, gt=gt, c0=c0: e.dma_start(out=gt, in_=vg_d[0:1, c0:c0 + 256].partition_broadcast(128)),
                               f"vg{hf}", writes=[("ug", hf)])
                        tr.dma("sp", lambda e, bt=bt, c0=c0: e.dma_start(out=bt, in_=vb_d[0:1, c0:c0 + 256].partition_broadcast(128)),
                               f"vb{hf}", writes=[("sg", hf)])
                        vs = vstage[:, c0:c0 + 256]
                        tr.op("dve", lambda e, vs=vs, mv=mv, rs_=rs_: e.tensor_scalar(out=vs, in0=vs, scalar1=mv[:, 0:1], scalar2=rs_,
                                                                                  op0=ALU.subtract, op1=ALU.mult),
                              reads=["vstage", ("mv", 0)], writes=["vstage"])
                        tr.op("dve", lambda e, vs=vs, gt=gt: e.tensor_tensor(out=vs, in0=vs, in1=gt, op=ALU.mult),
                              reads=["vstage", ("ug", hf)], writes=["vstage"])
                        tr.op("dve", lambda e, vs=vs, bt=bt: e.tensor_tensor(out=vs, in0=vs, in1=bt, op=ALU.add),
                              reads=["vstage", ("sg", hf)], writes=["vstage"])
                tr.dma("sp", lambda e: e.dma_start(out=vout_d[:, :], in_=vstage[:, :]), "vout", reads=["vstage"])

            pend = []
            ygi = [0]

            def flush_outproj(wd):
                while pend:
                    g_, bufi = pend.pop(0)
                    outproj(wd, g_ * 1024, yT[bufi], ("yT", bufi), nt)

            for g in range(KD // 8):
                bufi = ygi[0] % 2
                ygi[0] += 1
                for pr in range(4):
                    j0 = g * 8 + 2 * pr
                    bu = inproj_fm(w_in_e, j0 * 128, T)
                    for fi in range(2):
                        tr.op("act", lambda e, fi=fi, bu=bu: e.activation(out=ug[fi][:, 0:T], in_=ps[bu[fi]][:, 0:T], func=AF.Gelu),
                              reads=[("ps", bu[fi])], writes=[("ug", fi)])
                    bg = inproj_fm(w_in_e, 2 * D + j0 * 128, T)
                    for fi in range(2):
                        tr.op("act", lambda e, fi=fi, bg=bg: e.activation(out=sg[fi][:, 0:T], in_=ps[bg[fi]][:, 0:T], func=AF.Silu),
                              reads=[("ps", bg[fi])], writes=[("sg", fi)])
                    for fi in range(2):
                        j = j0 + fi
                        h = j // HT
                        bm = psalloc(1)[0]
                        for i in range(nt):
                            out = ps[bm][:, i * 128:(i + 1) * 128]
                            tr.op("pe", lambda e, out=out, i=i, j=j, h=h: e.matmul(out, lhsT=vbf[:, i * D + j * 128: i * D + (j + 1) * 128],
                                                                               rhs=wTs[:, i, h * 128:(h + 1) * 128], start=True, stop=False),
                                  reads=["vbf", "wTs"], writes=[("ps", bm)], sig=False)
                            tr.op("pe", lambda e, out=out, i=i, h=h: e.matmul(out, lhsT=nm[:, i, :], rhs=wTs[:, i, h * 128:(h + 1) * 128],
                                                                          start=False, stop=True),
                                  reads=["nm", "wTs"], writes=[("ps", bm)], sig=(i == nt - 1))
                        tr.op("dve", lambda e, j=j, h=h: e.scalar_tensor_tensor(out=At[:, :], in0=rs_bc[:, h * 128:(h + 1) * 128],
                                                                              scalar=cols[:, C_VB + j:C_VB + j + 1],
                                                                              in1=sb_bc[:, h * 128:(h + 1) * 128], op0=ALU.mult, op1=ALU.add),
                              reads=["rs_bc", "sb_bc", "cols"], writes=["At"])
                        for i in range(nt):
                            tr.op("dve", lambda e, i=i, j=j, bm=bm: e.scalar_tensor_tensor(out=Et[:, i * 128:(i + 1) * 128],
                                                                                       in0=ps[bm][:, i * 128:(i + 1) * 128],
                                                                                       scalar=cols[:, C_VG + j:C_VG + j + 1],
                                                                                       in1=At[:, :], op0=ALU.mult, op1=ALU.add),
                                  reads=[("ps", bm), "At", "cols"], writes=["Et"])
                        tr.op("dve", lambda e, fi=fi: e.tensor_tensor(out=Et[:, 0:T], in0=Et[:, 0:T], in1=ug[fi][:, 0:T], op=ALU.mult),
                              reads=["Et", ("ug", fi)], writes=["Et"])
                        jj = 2 * pr + fi
                        tr.op("dve", lambda e, fi=fi, jj=jj, bufi=bufi: e.tensor_tensor(out=yT[bufi][:, jj, 0:T], in0=Et[:, 0:T],
                                                                                    in1=sg[fi][:, 0:T], op=ALU.mult),
                              reads=["Et", ("sg", fi)], writes=[("yT", bufi)])
                flush_outproj(w_out_e)
                pend.append((g, bufi))

            tr.alias(["lnrstd"], ["vbf", "RG"])
            P3K = [("cb", p_, q) for p_ in range(2) for q in range(HT)] + [("cn", 0), ("cn", 1), ("sgb", 0), ("sgb", 1), ("sig", 0), ("sig", 1),
                                                     ("cvb", 0), ("cvb", 1), ("sq", 0), ("sq", 1), "mean", "msq", "cvs", "p3"]
            for k_ in P3K:
                tr.alias([k_], ["vbf", "RG"])
            Tp = T

            def fm(a, n):
                return RG[:, a:a + n]

            def stage1(g, hd, par):
                j_h0 = g * 8 + hd * HT
                for pr in range(HT // 2):
                    j0 = j_h0 + 2 * pr
                    ba = inproj_fm(w_in_e, 3 * D + j0 * 128, T)
                    bb = inproj_fm(w_in_e, 4 * D + j0 * 128, T)
                    for fi in range(2):
                        j = j0 + fi
                        sgt = fm(sigb[fi], T)
                        tr.op("act", lambda e, sgt=sgt, fi=fi, bb=bb: e.activation(out=sgt, in_=ps[bb[fi]][:, 0:T], func=AF.Sigmoid),
                              reads=[("ps", bb[fi])], writes=[("sig", fi)])
                        cacc = fm(cbuf[par][2 * pr + fi], T)
                        if not is_s:
                            ext = fm(cvb[fi], LC)
                            tr.op("act", lambda e, ext=ext, j=j: e.activation(out=ext[:, 0:NPRE], in_=carry_c[:, j, :], func=AF.Copy),
                                  reads=["carry_c"], writes=[("cvb", fi)])
                            tr.op("dve", lambda e, ext=ext, sgt=sgt, fi=fi, ba=ba: e.tensor_tensor(out=ext[:, NPRE:NPRE + T], in0=ps[ba[fi]][:, 0:T],
                                                                                                in1=sgt, op=ALU.mult),
                                  reads=[("ps", ba[fi]), ("sig", fi)], writes=[("cvb", fi)])
                            tr.op("act", lambda e, ext=ext, j=j: e.activation(out=carry_c[:, j, :], in_=ext[:, T:T + NPRE], func=AF.Copy),
                                  reads=[("cvb", fi)], writes=["carry_c"])
                            for k in range(CONV_W):
                                wk = cols[:, C_CW + j * CONV_W + k: C_CW + j * CONV_W + k + 1]
                                if k == 0:
                                    tr.op("dve", lambda e, cacc=cacc, ext=ext, wk=wk, j=j: e.tensor_scalar(
                                        out=cacc, in0=ext[:, 0:T], scalar1=wk, scalar2=cols[:, C_CB + j:C_CB + j + 1],
                                        op0=ALU.mult, op1=ALU.add),
                                          reads=[("cvb", fi), "cols"], writes=[("cb", par, 2 * pr + fi)])
                                else:
                                    tr.op("dve", lambda e, cacc=cacc, ext=ext, wk=wk, k=k: e.scalar_tensor_tensor(
                                        out=cacc, in0=ext[:, k:k + T], scalar=wk, in1=cacc, op0=ALU.mult, op1=ALU.add),
                                          reads=[("cvb", fi), "cols", ("cb", par, 2 * pr + fi)], writes=[("cb", par, 2 * pr + fi)])
                        else:
                            LS = NPRE + DS
                            ext3 = RG[:, cvs:cvs + NQ * LS].rearrange("p (q l) -> p q l", q=NQ)
                            tr.dma("sp", lambda e, ext3=ext3, j=j: e.dma_start(
                                out=ext3[:, :, 0:NPRE], in_=stcT_d[j * 128:(j + 1) * 128, :].rearrange("p (q l) -> p q l", q=NQ)),
                                "stc", writes=["cvs"])
                            pa3 = ps[ba[fi]][:, 0:T].rearrange("p (q l) -> p q l", q=NQ)
                            sg3 = sgt.rearrange("p (q l) -> p q l", q=NQ)
                            tr.op("dve", lambda e, ext3=ext3, pa3=pa3, sg3=sg3: e.tensor_tensor(out=ext3[:, :, NPRE:LS], in0=pa3, in1=sg3, op=ALU.mult),
                                  reads=[("ps", ba[fi]), ("sig", fi)], writes=["cvs"])
                            gq = fm(cnb[fi], T)
                            gq3 = gq.rearrange("p (q l) -> p q l", q=NQ)
                            tr.op("act", lambda e, gq3=gq3, ext3=ext3: e.activation(out=gq3, in_=ext3[:, :, NPRE:LS], func=AF.Copy),
                                  reads=["cvs"], writes=[("cn", fi)])
                            tr.dma("sp", lambda e, gq=gq, j=j: e.dma_start(out=gluTs_d[j * 128:(j + 1) * 128, :], in_=gq),
                                   f"glo{fi}", reads=[("cn", fi)])
                            c3 = cacc.rearrange("p (q l) -> p q l", q=NQ)
                            for k in range(CONV_W):
                                wk = cols[:, C_CW + j * CONV_W + k: C_CW + j * CONV_W + k + 1]
                                if k == 0:
                                    tr.op("dve", lambda e, c3=c3, ext3=ext3, wk=wk, j=j: e.tensor_scalar(
                                        out=c3, in0=ext3[:, :, 0:DS], scalar1=wk, scalar2=cols[:, C_CB + j:C_CB + j + 1],
                                        op0=ALU.mult, op1=ALU.add),
                                          reads=["cvs", "cols"], writes=[("cb", par, 2 * pr + fi)])
                                else:
                                    tr.op("dve", lambda e, c3=c3, ext3=ext3, wk=wk, k=k: e.scalar_tensor_tensor(
                                        out=c3, in0=ext3[:, :, k:k + DS], scalar=wk, in1=c3, op0=ALU.mult, op1=ALU.add),
                                          reads=["cvs", "cols", ("cb", par, 2 * pr + fi)], writes=[("cb", par, 2 * pr + fi)])

            def stage2(g, hd, par):
                j_h0 = g * 8 + hd * HT
                if hd == 0:
                    bufi_g[g] = ygi[0] % 2
                    ygi[0] += 1
                bufi = bufi_g[g]
                b1, b2 = psalloc(2)
                for q in range(HT):
                    cq = fm(cbuf[par][q], T)
                    sq = fm(sqb[q % 2], T)
                    tr.op("act", lambda e, cq=cq, sq=sq: e.activation(out=sq, in_=cq, func=AF.Square),
                          reads=[("cb", par, q)], writes=[("sq", q % 2)])
                    tr.op("pe", lambda e, cq=cq, q=q, b1=b1: e.matmul(ps[b1][:, 0:T], lhsT=ones_f[:, :], rhs=cq, start=(q == 0), stop=(q == HT - 1)),
                          reads=[("cb", par, q), "ones_f"], writes=[("ps", b1)], sig=(q == HT - 1))
                    tr.op("pe", lambda e, sq=sq, q=q, b2=b2: e.matmul(ps[b2][:, 0:T], lhsT=ones_f[:, :], rhs=sq, start=(q == 0), stop=(q == HT - 1)),
                          reads=[("sq", q % 2), "ones_f"], writes=[("ps", b2)], sig=True)
                mean = fm(meanb, T)
                rstd = fm(rstdb, T)
                msq = fm(msqb, T)
                nfe = float(HT * 128)
                tr.op("dve", lambda e, mean=mean, b1=b1: e.tensor_scalar(out=mean, in0=ps[b1][:, 0:T], scalar1=1.0 / nfe, scalar2=None, op0=ALU.mult),
                      reads=[("ps", b1)], writes=["mean"])
                tr.op("dve", lambda e, mean=mean, msq=msq: e.tensor_tensor(out=msq, in0=mean, in1=mean, op=ALU.mult),
                      reads=["mean"], writes=["msq"])
                tr.op("dve", lambda e, rstd=rstd, msq=msq, b2=b2: e.scalar_tensor_tensor(out=rstd, in0=ps[b2][:, 0:T], scalar=1.0 / nfe, in1=msq,
                                                                                    op0=ALU.mult, op1=ALU.subtract),
                      reads=[("ps", b2), "msq"], writes=["lnrstd"])
                tr.op("dve", lambda e, rstd=rstd: e.tensor_scalar(out=rstd, in0=rstd, scalar1=LN_EPS, scalar2=None, op0=ALU.add),
                      reads=["lnrstd"], writes=["lnrstd"])
                tr.op("act", lambda e, rstd=rstd: e.activation(out=rstd, in_=rstd, func=AF.Sqrt),
                      reads=["lnrstd"], writes=["lnrstd"])
                tr.op("dve", lambda e, rstd=rstd: e.reciprocal(out=rstd, in_=rstd),
                      reads=["lnrstd"], writes=["lnrstd"])
                for pr in range(HT // 2):
                    j0 = j_h0 + 2 * pr
                    bgb = inproj_fm(w_in_e, 5 * D + j0 * 128, T)
                    for fi in range(2):
                        j = j0 + fi
                        q = 2 * pr + fi
                        sgbt = fm(sgbb[fi], T)
                        tr.op("act", lambda e, sgbt=sgbt, fi=fi, bgb=bgb: e.activation(out=sgbt, in_=ps[bgb[fi]][:, 0:T], func=AF.Silu),
                              reads=[("ps", bgb[fi])], writes=[("sgb", fi)])
                        cq = fm(cbuf[par][q], T)
                        cn = fm(cnb[fi], T)
                        tr.op("dve", lambda e, cq=cq, cn=cn, mean=mean: e.tensor_tensor(out=cn, in0=cq, in1=mean, op=ALU.subtract),
                              reads=[("cb", par, q), "mean"], writes=[("cn", fi)])
                        tr.op("dve", lambda e, cn=cn, rstd=rstd: e.tensor_tensor(out=cn, in0=cn, in1=rstd, op=ALU.mult),
                              reads=[("cn", fi), "lnrstd"], writes=[("cn", fi)])
                        tr.op("act", lambda e, cn=cn, j=j: e.activation(out=cn, in_=cn, func=AF.Silu,
                                                                      scale=cols[:, C_CNG + j:C_CNG + j + 1],
                                                                      bias=cols[:, C_CNB + j:C_CNB + j + 1]),
                              reads=[("cn", fi), "cols"], writes=[("cn", fi)])
                        jj = hd * HT + q
                        tr.op("dve", lambda e, cn=cn, sgbt=sgbt, jj=jj, bufi=bufi: e.tensor_tensor(out=yT[bufi][:, jj, 0:T], in0=cn, in1=sgbt, op=ALU.mult),
                              reads=[("cn", fi), ("sgb", fi)], writes=[("yT", bufi)])
                if hd == 8 // HT - 1:
                    flush_outproj(w_out_e)
                    pend.append((KD // 8 + g, bufi))

            bufi_g = {}
            heads = [(g, hd) for g in range(KD // 8) for hd in range(8 // HT)]
            if is_s:
                for (g, hd) in heads:
                    stage1(g, hd, 0)
                    stage2(g, hd, 0)
            else:
                for idx, (g, hd) in enumerate(heads):
                    stage1(g, hd, idx % 2)
                    if idx >= 1:
                        stage2(heads[idx - 1][0], heads[idx - 1][1], (idx - 1) % 2)
                stage2(heads[-1][0], heads[-1][1], (len(heads) - 1) % 2)
            flush_outproj(w_out_e)

            if last_p:
                tr.dma("sp", lambda e: e.dma_start(out=gluTp_d.rearrange("(k p) t -> p k t", p=128), in_=carry_c[:, :, :]),
                       "glp", reads=["carry_c"])

            for i in range(nt):
                rmsnorm_to_hT(i, C_G1)

            tr.alias(["p5", "dT"], P3K + ["RG", "lnrstd"])
            for k_ in ([("xcb", 0), ("xcb", 1), ("sA", 0), ("sA", 1), ("sB", 0), ("sB", 1), ("sgc", 0), ("sgc", 1), "t16"]):
                tr.alias([k_], ["p5"])
            for gp, w in enumerate(POOLW):
                nstep = gp + 1
                for pr in range(CGT // 2):
                    cj0 = gp * CGT + 2 * pr
                    bx = inproj_fm(w_in_o, cj0 * 128, T)
                    for fi in range(2):
                        cj = cj0 + fi
                        jl = 2 * pr + fi
                        if not is_s:
                            ext = fm(xcb[fi], LX)
                            tr.op("act", lambda e, ext=ext, cj=cj: e.activation(out=ext[:, 0:PPRE], in_=carry_x[:, cj, :], func=AF.Copy),
                                  reads=["carry_x"], writes=[("xcb", fi)])
                            tr.op("act", lambda e, ext=ext, fi=fi, bx=bx: e.activation(out=ext[:, PPRE:PPRE + T], in_=ps[bx[fi]][:, 0:T], func=AF.Copy),
                                  reads=[("ps", bx[fi])], writes=[("xcb", fi)])
                            tr.op("act", lambda e, ext=ext, cj=cj: e.activation(out=carry_x[:, cj, :], in_=ext[:, T:T + PPRE], func=AF.Copy),
                                  reads=[("xcb", fi)], writes=["carry_x"])
                            L = PPRE + T
                            cur, curk = ext, ("xcb", fi)
                            for m in range(nstep):
                                sh = 1 << m
                                vs_ = (1 << (m + 1)) - 1
                                nxt = fm((sA if m % 2 == 0 else sB)[fi], LX)
                                nk = (("sA", fi) if m % 2 == 0 else ("sB", fi))
                                tr.op("dve", lambda e, nxt=nxt, cur=cur, vs_=vs_, sh=sh, L=L: e.tensor_tensor(
                                    out=nxt[:, vs_:L], in0=cur[:, vs_:L], in1=cur[:, vs_ - sh:L - sh], op=ALU.add),
                                      reads=[curk], writes=[nk])
                                cur, curk = nxt, nk
                            dsl = dT[:, jl * TMAX: jl * TMAX + T]
                            tr.op("dve", lambda e, dsl=dsl, cur=cur, ext=ext, w=w: e.scalar_tensor_tensor(
                                out=dsl, in0=cur[:, PPRE:PPRE + T], scalar=1.0 / w, in1=ext[:, PPRE:PPRE + T],
                                op0=ALU.mult, op1=ALU.subtract),
                                  reads=[curk, ("xcb", fi)], writes=["dT"])
                            if bi == 0:
                                o0 = 128
                                t16v = RG[:, t16:t16 + 16]
                                tr.op("dve", lambda e, t16v=t16v, cur=cur, gp=gp, o0=o0: e.tensor_tensor(
                                    out=t16v, in0=cur[:, PPRE + o0:PPRE + o0 + 16], in1=invc[:, gp, :], op=ALU.mult),
                                      reads=[curk, "invc"], writes=["t16"])
                                tr.op("dve", lambda e, t16v=t16v, ext=ext, dsl=dsl, o0=o0: e.tensor_tensor(
                                    out=dsl[:, o0:o0 + 16], in0=t16v, in1=ext[:, PPRE + o0:PPRE + o0 + 16], op=ALU.subtract),
                                      reads=["t16", ("xcb", fi)], writes=["dT"])
                        else:
                            LS = PPRE + DS
                            ext3 = RG[:, xcb[fi]:xcb[fi] + NQ * LS].rearrange("p (q l) -> p q l", q=NQ)
                            tr.dma("sp", lambda e, ext3=ext3, cj=cj: e.dma_start(
                                out=ext3[:, :, 0:PPRE], in_=stpT_d[cj * 128:(cj + 1) * 128, :].rearrange("p (q l) -> p q l", q=NQ)),
                                f"stp{fi}", writes=[("xcb", fi)])
                            px3 = ps[bx[fi]][:, 0:T].rearrange("p (q l) -> p q l", q=NQ)
                            tr.op("act", lambda e, ext3=ext3, px3=px3: e.activation(out=ext3[:, :, PPRE:LS], in_=px3, func=AF.Copy),
                                  reads=[("ps", bx[fi])], writes=[("xcb", fi)])
                            xo = fm(sgcb[fi], T)
                            tr.op("act", lambda e, xo=xo, fi=fi, bx=bx: e.activation(out=xo, in_=ps[bx[fi]][:, 0:T], func=AF.Copy),
                                  reads=[("ps", bx[fi])], writes=[("sgc", fi)])
                            tr.dma("sp", lambda e, xo=xo, cj=cj: e.dma_start(out=xcTs_d[cj * 128:(cj + 1) * 128, :], in_=xo),
                                   f"xco{fi}", reads=[("sgc", fi)])
                            cur, curk = ext3, ("xcb", fi)
                            for m in range(nstep):
                                sh = 1 << m
                                vs_ = (1 << (m + 1)) - 1
                                a_ = (sA if m % 2 == 0 else sB)[fi]
                                nxt = RG[:, a_:a_ + NQ * LS].rearrange("p (q l) -> p q l", q=NQ)
                                nk = (("sA", fi) if m % 2 == 0 else ("sB", fi))
                                tr.op("dve", lambda e, nxt=nxt, cur=cur, vs_=vs_, sh=sh, LS=LS: e.tensor_tensor(
                                    out=nxt[:, :, vs_:LS], in0=cur[:, :, vs_:LS], in1=cur[:, :, vs_ - sh:LS - sh], op=ALU.add),
                                      reads=[curk], writes=[nk])
                                cur, curk = nxt, nk
                            dsl3 = dT[:, jl * TMAX: jl * TMAX + T].rearrange("p (q l) -> p q l", q=NQ)
                            tr.op("dve", lambda e, dsl3=dsl3, cur=cur, ext3=ext3, w=w, LS=LS: e.scalar_tensor_tensor(
                                out=dsl3, in0=cur[:, :, PPRE:LS], scalar=1.0 / w, in1=ext3[:, :, PPRE:LS],
                                op0=ALU.mult, op1=ALU.subtract),
                                  reads=[curk, ("xcb", fi)], writes=["dT"])
                for half in range(CGT // 8):
                    bufi = ygi[0] % 2
                    ygi[0] += 1
                    for pr in range(4):
                        dj0 = half * 8 + 2 * pr
                        bk = psalloc(2)
                        s = wload(pool_w[gp * CGT * 128:(gp + 1) * CGT * 128, dj0 * 128: dj0 * 128 + 256], CGT, 256)
                        for fi in range(2):
                            for kc in range(CGT):
                                lhsT = wblk[s][:, kc * 256 + fi * 128: kc * 256 + (fi + 1) * 128]
                                rhs = dT[:, kc * TMAX: kc * TMAX + T]
                                out = ps[bk[fi]][:, 0:T]
                                tr.op("pe", lambda e, out=out, lhsT=lhsT, rhs=rhs, st=(kc == 0), sp=(kc == CGT - 1):
                                      e.matmul(out, lhsT=lhsT, rhs=rhs, start=st, stop=sp),
                                      reads=[("wb", s), "dT"], writes=[("ps", bk[fi])], sig=(kc == CGT - 1))
                        bgc = inproj_fm(w_in_o, 2 * D + (gp * CGT + dj0) * 128, T)
                        for fi in range(2):
                            fgl = gp * CGT + dj0 + fi
                            sgct = fm(sgcb[fi], T)
                            tr.op("act", lambda e, sgct=sgct, fi=fi, bgc=bgc: e.activation(out=sgct, in_=ps[bgc[fi]][:, 0:T], func=AF.Silu),
                                  reads=[("ps", bgc[fi])], writes=[("sgc", fi)])
                            jj = 2 * pr + fi
                            tr.op("dve", lambda e, sgct=sgct, fi=fi, bk=bk, fgl=fgl, jj=jj, bufi=bufi: e.scalar_tensor_tensor(
                                out=yT[bufi][:, jj, 0:T], in0=ps[bk[fi]][:, 0:T], scalar=cols[:, C_PS + fgl:C_PS + fgl + 1],
                                in1=sgct, op0=ALU.mult, op1=ALU.mult),
                                  reads=[("ps", bk[fi]), ("sgc", fi), "cols"], writes=[("yT", bufi)])
                    flush_outproj(w_out_o)
                    pend.append(((gp * CGT) // 8 + half, bufi))
            flush_outproj(w_out_o)
            if last_p:
                tr.dma("sp", lambda e: e.dma_start(out=xcTp_d.rearrange("(k p) t -> p k t", p=128), in_=carry_x[:, :, :]),
                       "xcp", reads=["carry_x"])
            tr.alias(["RG"], ["p5", "dT", ("xcb", 0), ("xcb", 1), ("sA", 0), ("sA", 1), ("sB", 0), ("sB", 1), ("sgc", 0), ("sgc", 1), "t16"])

            tr.dma("sp", lambda e: e.dma_start(out=vstage[:, :], in_=nf_d[0:1, :].partition_broadcast(128)), "nfb", writes=["vstage"])
            for i, tl in enumerate(btiles):
                if tl == 0:
                    continue
                for cb in range(NCB):
                    tr.op("act", lambda e, cb=cb, i=i: e.activation(out=junk[:, :], in_=xres[:, i, cb * 512:(cb + 1) * 512],
                                                                  func=AF.Square, accum_out=small[:, cb:cb + 1]),
                          reads=[("xres", i)], writes=["junk", "ssc"])
                tr.op("dve", lambda e: e.reduce_sum(out=small[:, 16:17], in_=small[:, 0:NCB], axis=AX.X),
                      reads=["ssc"], writes=["ss"])
                tr.op("dve", lambda e: e.tensor_scalar(out=small[:, 17:18], in0=small[:, 16:17], scalar1=1.0 / D,
                                                       scalar2=RMS_EPS, op0=ALU.mult, op1=ALU.add),
                      reads=["ss"], writes=["ss2"])
                tr.op("act", lambda e: e.activation(out=small[:, 19:20], in_=small[:, 17:18], func=AF.Sqrt),
                      reads=["ss2"], writes=["ss3"])
                tr.op("dve", lambda e: e.reciprocal(out=small[:, 18:19], in_=small[:, 19:20]),
                      reads=["ss3"], writes=["rstd"])
                for cb in range(NCB):
                    xs = xres[:, i, cb * 512:(cb + 1) * 512]
                    tr.op("dve", lambda e, xs=xs, cb=cb: e.scalar_tensor_tensor(out=xs, in0=xs, scalar=small[:, 18:19],
                                                                              in1=vstage[:, cb * 512:(cb + 1) * 512],
                                                                              op0=ALU.mult, op1=ALU.mult),
                          reads=[("xres", i), "rstd", "vstage"], writes=[("xres", i)])
                ot = tl - 1
                tr.dma("sp", lambda e, i=i, ot=ot: e.dma_start(out=y_d[ot * 128:(ot + 1) * 128, :], in_=xres[:, i, :]),
                       f"yo{i}", reads=[("xres", i)])

        tr.dma("sp", lambda e: e.dma_start(out=convs_old[:, :], in_=stc_tm[:, DS * D:NPRE * D]), "cso")
        tr.dma("sp", lambda e: e.dma_start(out=pools_old[:, :], in_=stp_tm[:, DS * 2 * D:PPRE * 2 * D]), "pso")

        tr.final_wait("sp")
        print("bass ops emitted:", tr.nops)
    return nc


_PROG_CACHE = {}


def _prep_inputs(inp):
    xp = np.asarray(inp["x_prompt"], np.float32)
    xs = np.asarray(inp["x_sample"], np.float32)
    NB, SEQ, D = xp.shape
    DB, DS, _ = xs.shape
    KD = D // 128
    KC = 2 * KD
    half = SEQ // 2
    NPT = half // 128
    NQ = DB // NCORES
    assert NB * 2 == NCORES and NQ * DS == 128

    def colmaj(v):
        v = np.asarray(v, np.float32).reshape(-1, 128)
        return np.ascontiguousarray(v.T)

    cw = np.asarray(inp["conv_w"], np.float32)[0]
    cw_cols = np.ascontiguousarray(cw.T.reshape(KD, 128, CONV_W).transpose(1, 0, 2).reshape(128, KD * CONV_W))
    cols = np.concatenate([
        colmaj(inp["norm_in"][0]), colmaj(inp["norm_in"][1]), colmaj(inp["v_norm_g"][0]), colmaj(inp["v_norm_b"][0]),
        colmaj(inp["conv_b"][0]), colmaj(inp["conv_norm_g"][0]), colmaj(inp["conv_norm_b"][0]),
        colmaj(inp["pool_scale"][0]), np.zeros((128, KD), np.float32), cw_cols], axis=1)
    cols = np.ascontiguousarray(cols, np.float32)

    sgw = np.asarray(inp["sgu_w"], np.float32)[0]
    sgb = np.asarray(inp["sgu_b"], np.float32)[0]
    wT_p = np.ascontiguousarray(sgw.transpose(2, 0, 1).reshape(128, NHEAD * 128))
    wT_s = np.zeros((128, NHEAD, 128), np.float32)
    for q in range(NQ):
        wT_s[q * DS:(q + 1) * DS, :, q * DS:(q + 1) * DS] = sgw[:, :DS, :DS].transpose(2, 0, 1)
    sgu = np.concatenate([wT_p, wT_s.reshape(128, NHEAD * 128)], axis=0)
    s_i = np.arange(128)[:, None]
    t_i = np.arange(128)[None, :]
    mask_p = (s_i <= t_i).astype(np.float32)
    mask_s = ((s_i // DS == t_i // DS) & (s_i <= t_i)).astype(np.float32)
    mask = np.concatenate([mask_p, mask_s], axis=0)
    sgub = np.stack([sgb.reshape(-1), np.tile(sgb[:, :DS], (1, NQ)).reshape(-1)], axis=0).astype(np.float32)
    ident = np.eye(128, dtype=np.float32)

    shared = {
        "w_in_e": np.asarray(inp["w_in_even"], np.float32)[0],
        "w_out_e": np.asarray(inp["w_out_even"], np.float32)[0],
        "w_in_o": np.asarray(inp["w_in_odd"], np.float32)[0],
        "pool_w": np.asarray(inp["pool_w"], np.float32)[0].reshape(4 * (2 * D // 4), 2 * D // 4),
        "w_out_o": np.asarray(inp["w_out_odd"], np.float32)[0],
        "cols": cols, "nf": np.asarray(inp["norm_f"], np.float32).reshape(1, D),
        "vg": np.asarray(inp["v_norm_g"], np.float32).reshape(1, D),
        "vb": np.asarray(inp["v_norm_b"], np.float32).reshape(1, D),
        "sgub": sgub, "sgu": sgu, "mask": mask, "ident": ident,
    }
    stc = np.asarray(inp["state_conv"], np.float32)[0]
    stp = np.asarray(inp["state_pool"], np.float32)[0]
    in_maps = []
    for c in range(NCORES):
        b, hf = c // 2, c % 2
        halo = xp[b, half - 128:half] if hf == 1 else np.zeros((128, D), np.float32)
        xin = np.concatenate([halo, xp[b, hf * half:(hf + 1) * half], xs[c * NQ:(c + 1) * NQ].reshape(128, D)], axis=0)
        sc = stc[c * NQ:(c + 1) * NQ]
        sp_ = stp[c * NQ:(c + 1) * NQ]
        m = dict(shared)
        m["xin"] = np.ascontiguousarray(xin)
        m["posv"] = np.ascontiguousarray(np.broadcast_to((hf * half + np.arange(16, dtype=np.float32))[None, :], (128, 16)))
        m["stcT"] = np.ascontiguousarray(sc.transpose(2, 0, 1).reshape(D, NQ * NPRE))
        m["stpT"] = np.ascontiguousarray(sp_.transpose(2, 0, 1).reshape(2 * D, NQ * PPRE))
        m["stc_tm"] = np.ascontiguousarray(sc.reshape(NQ, NPRE * D))
        m["stp_tm"] = np.ascontiguousarray(sp_.reshape(NQ, PPRE * 2 * D))
        in_maps.append(m)
    return in_maps, (NB, SEQ, D, DB, DS, NPT, NQ)


def kernel(**inputs):
    in_maps, (NB, SEQ, D, DB, DS, NPT, NQ) = _prep_inputs(inputs)
    key = (D, NPT)
    if key not in _PROG_CACHE:
        _PROG_CACHE[key] = build_program(D, NPT)
    nc = _PROG_CACHE[key]
    res = run_bass_kernel_spmd(nc, in_maps, core_ids=list(range(NCORES))).results
    half = SEQ // 2
    y_prompt = np.zeros((NB, SEQ, D), np.float32)
    y_sample = np.zeros((DB, DS, D), np.float32)
    new_v = np.zeros((1, DB, DS, D), np.float32)
    conv_p = np.zeros((1, NB, NPRE, D), np.float32)
    conv_s = np.zeros((1, DB, NPRE, D), np.float32)
    pool_p = np.zeros((1, NB, PPRE, 2 * D), np.float32)
    pool_s = np.zeros((1, DB, PPRE, 2 * D), np.float32)
    for c in range(NCORES):
        r = res[c]
        b, hf = c // 2, c % 2
        y = r["y"].reshape(NPT + 1, 128, D)
        y_prompt[b, hf * half:(hf + 1) * half] = y[:NPT].reshape(half, D)
        qs = slice(c * NQ, (c + 1) * NQ)
        y_sample[qs] = y[NPT].reshape(NQ, DS, D)
        new_v[0, qs] = r["vout"].reshape(NQ, DS, D)
        conv_s[0, qs, :NPRE - DS] = r["convs_old"].reshape(NQ, NPRE - DS, D)
        conv_s[0, qs, NPRE - DS:] = r["gluTs"].T.reshape(NQ, DS, D)
        pool_s[0, qs, :PPRE - DS] = r["pools_old"].reshape(NQ, PPRE - DS, 2 * D)
        pool_s[0, qs, PPRE - DS:] = r["xcTs"].T.reshape(NQ, DS, 2 * D)
        if hf == 1:
            conv_p[0, b] = r["gluTp"].T
            pool_p[0, b] = r["xcTp"].T
    return (y_prompt, y_sample, new_v, conv_p, conv_s, pool_p, pool_s)
```

```python
import numpy as np
from contextlib import ExitStack
import concourse.bass as bass
import concourse.mybir as mybir
from concourse.bass_utils import run_bass_kernel_spmd

F32 = mybir.dt.float32
BF16 = mybir.dt.bfloat16
AF = mybir.ActivationFunctionType
ALU = mybir.AluOpType
AX = mybir.AxisListType

CONV_W = 31
NPRE = CONV_W - 1
POOLW = (2, 4, 8, 16)
PPRE = 15
RMS_EPS = 1e-6
LN_EPS = 1e-5
NHEAD = 8
BLK = 4
NCORES = 8
ENGS = ("pe", "act", "dve", "pool", "sp")


class Tr:
    def __init__(self, nc, es):
        self.nc = nc
        self.es = es
        self.E = {"pe": nc.tensor, "act": nc.scalar, "dve": nc.vector, "pool": nc.gpsimd, "sp": nc.sync}
        self.nops = 0
        self.cnt = {}
        self.sem = {}
        self.seen = {e: {} for e in ENGS}
        self.W = {}
        self.R = {}
        for e in ("pe", "act", "dve", "pool"):
            self._mksem(e)

    def _mksem(self, name):
        if name not in self.sem:
            self.sem[name] = self.es.enter_context(self.nc.semaphore("s_" + name))
            self.cnt[name] = 0

    def _deps(self, eng, reads, writes):
        need = {}
        for k in reads:
            for s, t in self.W.get(k, {}).items():
                need[s] = max(need.get(s, 0), t)
        for k in writes:
            for s, t in self.W.get(k, {}).items():
                need[s] = max(need.get(s, 0), t)
            for s, t in self.R.get(k, {}).items():
                need[s] = max(need.get(s, 0), t)
        for s, t in need.items():
            if s == "pe" and eng == "pe":
                continue
            if self.seen[eng].get(s, 0) >= t:
                continue
            self.seen[eng][s] = t
            self.E[eng].wait_ge(self.sem[s], t)

    def _mark(self, s, tick, reads, writes):
        for k in reads:
            d = self.R.setdefault(k, {})
            d[s] = max(d.get(s, 0), tick)
        for k in writes:
            self.W[k] = {s: tick}
            self.R[k] = {}

    def op(self, eng, fn, reads=(), writes=(), sig=True):
        self._deps(eng, reads, writes)
        if sig:
            self.cnt[eng] += 1
            tick = self.cnt[eng]
            fn(self.E[eng]).then_inc(self.sem[eng], 1)
        else:
            tick = self.cnt[eng] + 1
            fn(self.E[eng])
        self.nops += 1
        self._mark(eng, tick, reads, writes)

    def dma(self, eng, fn, slot, reads=(), writes=()):
        self._mksem(slot)
        self._deps(eng, reads, writes)
        self.cnt[slot] += 16
        fn(self.E[eng]).then_inc(self.sem[slot], 16)
        self.nops += 1
        self._mark(slot, self.cnt[slot], reads, writes)

    def alias(self, new_keys, old_keys):
        w, r = {}, {}
        for k in old_keys:
            for s, t in self.W.get(k, {}).items():
                w[s] = max(w.get(s, 0), t)
            for s, t in self.R.get(k, {}).items():
                r[s] = max(r.get(s, 0), t)
        for k in new_keys:
            self.W[k] = dict(w)
            self.R[k] = dict(r)

    def final_wait(self, eng):
        for s, c in self.cnt.items():
            if c > 0 and self.seen[eng].get(s, 0) < c:
                self.E[eng].wait_ge(self.sem[s], c)
                self.seen[eng][s] = c

    def emit(self, block):
        nc = self.nc

        def run(e, items):
            for it in items:
                if it[0] == "w":
                    e.wait_ge(self.sem[it[1]], it[2])
                else:
                    ins = it[1](e)
                    if it[2] is not None:
                        ins.then_inc(self.sem[it[2]], it[3])

        q = self.q

        @block.tensor
        def _(e):
            run(e, q["pe"])

        @block.scalar
        def _(e):
            run(e, q["act"])

        @block.vector
        def _(e):
            run(e, q["dve"])

        @block.gpsimd
        def _(e):
            run(e, q["pool"])

        @block.sync
        def _(e):
            run(e, q["sp"])


def build_program(D, NPT):
    KD = D // 128
    HT = KD // NHEAD
    KC = 2 * KD
    CGT = KC // 4
    NCB = D // 512
    assert HT >= 2 and HT % 2 == 0 and KD % 16 == 0
    NTIN = NPT + 2
    NTOUT = NPT + 1
    TMAX = BLK * 128
    KH = 16
    NQ = 16
    DS = 8

    nc = bass.Bass("TRN2", target_bir_lowering=False)

    def din(name, shape):
        return nc.dram_tensor(name, list(shape), F32, kind="ExternalInput").ap()

    def dout(name, shape):
        return nc.dram_tensor(name, list(shape), F32, kind="ExternalOutput").ap()

    xin = din("xin", [NTIN * 128, D])
    w_in_e = din("w_in_e", [D, 6 * D])
    w_out_e = din("w_out_e", [2 * D, D])
    w_in_o = din("w_in_o", [D, 4 * D])
    pool_w = din("pool_w", [4 * CGT * 128, CGT * 128])
    w_out_o = din("w_out_o", [2 * D, D])
    NCOL = 8 * KD + KC + KD * CONV_W
    cols_d = din("cols", [128, NCOL])
    nf_d = din("nf", [1, D])
    vg_d = din("vg", [1, D])
    vb_d = din("vb", [1, D])
    sgub_d = din("sgub", [2, NHEAD * 128])
    sgu_d = din("sgu", [2 * 128, NHEAD * 128])
    mask_d = din("mask", [2 * 128, 128])
    ident_d = din("ident", [128, 128])
    posv_d = din("posv", [128, 16])
    stcT_d = din("stcT", [D, NQ * NPRE])
    stpT_d = din("stpT", [2 * D, NQ * PPRE])
    stc_tm = din("stc_tm", [NQ, NPRE * D])
    stp_tm = din("stp_tm", [NQ, PPRE * 2 * D])

    y_d = dout("y", [NTOUT * 128, D])
    vout_d = dout("vout", [128, D])
    gluTs_d = dout("gluTs", [D, 128])
    xcTs_d = dout("xcTs", [2 * D, 128])
    gluTp_d = dout("gluTp", [D, NPRE])
    xcTp_d = dout("xcTp", [2 * D, PPRE])
    convs_old = dout("convs_old", [NQ, (NPRE - DS) * D])
    pools_old = dout("pools_old", [NQ, (PPRE - DS) * 2 * D])

    C_G0, C_G1, C_VG, C_VB, C_CB, C_CNG, C_CNB = [i * KD for i in range(7)]
    C_PS = 7 * KD
    C_CW = 8 * KD + KC
    HW = NHEAD * 128

    with ExitStack() as es:
        def sb(name, shape, dt=F32):
            return es.enter_context(nc.sbuf_tensor("sb_" + name, list(shape), dt))

        tr = Tr(nc, es)
        ps = [es.enter_context(nc.psum_tensor(f"ps{b}", [128, 512], F32)) for b in range(8)]
        ps_rr = [0]

        def psalloc(n):
            r = [(ps_rr[0] + i) % 8 for i in range(n)]
            ps_rr[0] = (ps_rr[0] + n) % 8
            return r

        RGN = max(8192, 3 * D // 2 + 5 * (NHEAD * 128) // 2)
        xres = sb("xres", [128, BLK, D])
        hT = sb("hT", [128, KD, TMAX], BF16)
        RG = sb("RG", [128, RGN])
        yT = sb("yT", [128, 8, TMAX], BF16)
        NWB = 3
        wblk = [sb(f"wb{i}", [128, 4096], BF16) for i in range(NWB)]
        cols = sb("cols", [128, NCOL])
        rs_p = sb("rs_p", [128, HW])
        sb_p = sb("sb_p", [128, HW])
        wTm_p = sb("wTm_p", [128, HW], BF16)
        wTs = sb("wTs", [128, BLK, HW], BF16)
        nm = sb("nm", [128, BLK, 128], BF16)
        ident = sb("ident", [128, 128])
        ones_f = sb("ones_f", [128, 128])
        ones_b = sb("ones_b", [128, 128], BF16)
        maskt = sb("maskt", [128, 128])
        carry_c = sb("carry_c", [128, KD, NPRE])
        carry_x = sb("carry_x", [128, KC, PPRE])
        invc = sb("invc", [128, 4, 16])
        posv = sb("posv", [128, 16])
        ug = sb("ug", [128, 512])
        sg = sb("sg", [128, 512])
        Et = sb("Et", [128, 512])
        At = sb("At", [128, 2, 128])
        small = sb("small", [128, 64])
        bnst = sb("bnst", [128, BLK, 8, 6])

        vbf = RG[:, 0:BLK * D // 2].bitcast(BF16)
        T1O = 3 * D // 2
        rs_s = RG[:, T1O:T1O + HW]
        sb_s = RG[:, T1O + HW:T1O + 2 * HW]
        wTm_s = RG[:, T1O + 2 * HW:T1O + 2 * HW + HW // 2].bitcast(BF16)
        assert T1O + 2 * HW + HW // 2 <= RGN
        off = [0]

        def rg(n):
            a = off[0]
            off[0] += n
            assert off[0] <= RGN, off[0]
            return a

        LC = NPRE + TMAX
        cbuf = [rg(TMAX) for _ in range(HT)]
        cvb = rg(LC)
        sqb = rg(TMAX)
        meanb, rstdb, msqb = rg(TMAX), rg(TMAX), rg(TMAX)
        sgbb = [rg(TMAX) for _ in range(2)]
        sigb = rg(TMAX)
        cnb = [rg(TMAX) for _ in range(2)]
        cvs = [rg(NQ * (NPRE + DS)) for _ in range(2)]
        off[0] = 0
        dT_off = rg(CGT * TMAX // 2)
        LX = PPRE + TMAX
        xcb, sAb, sBb = rg(LX), rg(LX), rg(LX)
        sgcb = [rg(TMAX) for _ in range(2)]
        t16 = rg(16)
        LSX = NQ * (PPRE + DS)
        xcs = [rg(LSX) for _ in range(2)]
        sAs, sBs = rg(LSX), rg(LSX)
        dT = RG[:, dT_off:dT_off + CGT * TMAX // 2].bitcast(BF16)

        def fm(a, n):
            return RG[:, a:a + n]

        P3K = [("cb", q) for q in range(HT)] + [("cn", 0), ("cn", 1), ("sgb", 0), ("sgb", 1), "sig", "cvb", "sq",
                                               "mean", "msq", "lnrstd", ("cvsP", 0), ("cvsP", 1), ("cvsN", 0), ("cvsN", 1)]
        P5K = ["dT", "xcb", "sA", "sB", ("sgc", 0), ("sgc", 1), "t16", ("xcsP", 0), ("xcsP", 1), ("xcsN", 0), ("xcsN", 1), "sAs", "sBs"]

        wb_i = [0]

        def wload(src_ap, nk, width):
            s = wb_i[0] % NWB
            wb_i[0] += 1
            dst = wblk[s][:, 0:nk * width].rearrange("p (k c) -> p k c", k=nk)
            src = src_ap.rearrange("(k p) c -> p k c", p=128)
            tr.dma("pool", lambda e: e.dma_start(out=dst, in_=src), f"wb{s}", writes=[("wb", s)])
            return s

        def inproj_fm(wd, col0, T):
            bk = psalloc(2)
            nkh = KD // KH
            for kh in range(nkh):
                s = wload(wd[kh * KH * 128:(kh + 1) * KH * 128, col0:col0 + 256], KH, 256)
                for fi in range(2):
                    for kc in range(KH):
                        last = (kh == nkh - 1 and kc == KH - 1)
                        lhsT = wblk[s][:, kc * 256 + fi * 128: kc * 256 + (fi + 1) * 128]
                        rhs = hT[:, kh * KH + kc, 0:T]
                        out = ps[bk[fi]][:, 0:T]
                        st = (kh == 0 and kc == 0)
                        tr.op("pe", lambda e: e.matmul(out, lhsT=lhsT, rhs=rhs, start=st, stop=last),
                              reads=[("wb", s), "hT"], writes=[("ps", bk[fi])], sig=(last or kc == KH - 1))
            return bk

        def outproj(wd, row0, nt):
            for cb in range(NCB):
                s = wload(wd[row0:row0 + 1024, cb * 512:(cb + 1) * 512], 8, 512)
                bk = psalloc(nt)
                for i in range(nt):
                    for kc in range(8):
                        lhsT = yT[:, kc, i * 128:(i + 1) * 128]
                        rhs = wblk[s][:, kc * 512:(kc + 1) * 512]
                        out = ps[bk[i]][:, 0:512]
                        tr.op("pe", lambda e: e.matmul(out, lhsT=lhsT, rhs=rhs, start=(kc == 0), stop=(kc == 7)),
                              reads=[("wb", s), "yT"], writes=[("ps", bk[i])], sig=(kc == 7))
                for i in range(nt):
                    xs = xres[:, i, cb * 512:(cb + 1) * 512]
                    pin = ps[bk[i]][:, 0:512]
                    tr.op("dve", lambda e: e.tensor_tensor(out=xs, in0=pin, in1=xs, op=ALU.add),
                          reads=[("ps", bk[i]), ("xres", i)], writes=[("xres", i)])

        def row_rstd(i, dst_col):
            for cb in range(NCB):
                tr.op("act", lambda e: e.activation(out=ug[:, :], in_=xres[:, i, cb * 512:(cb + 1) * 512],
                                                    func=AF.Square, accum_out=small[:, cb:cb + 1]),
                      reads=[("xres", i)], writes=["ug", "ssc"])
            tr.op("dve", lambda e: e.reduce_sum(out=small[:, 16:17], in_=small[:, 0:NCB], axis=AX.X),
                  reads=["ssc"], writes=["ss"])
            tr.op("dve", lambda e: e.tensor_scalar(out=small[:, 17:18], in0=small[:, 16:17], scalar1=1.0 / D,
                                                   scalar2=RMS_EPS, op0=ALU.mult, op1=ALU.add),
                  reads=["ss"], writes=["ss2"])
            tr.op("act", lambda e: e.activation(out=small[:, 19:20], in_=small[:, 17:18], func=AF.Sqrt),
                  reads=["ss2"], writes=["ss3"])
            tr.op("dve", lambda e: e.reciprocal(out=small[:, dst_col:dst_col + 1], in_=small[:, 19:20]),
                  reads=["ss3"], writes=[("rstd", dst_col)])

        def rmsnorm_to_hT(i, gc0):
            row_rstd(i, 18)
            for cb in range(NCB):
                xb, xk = (sg, "sg") if cb % 2 == 0 else (Et, "Et")
                tr.op("act", lambda e: e.activation(out=xb[:, :], in_=xres[:, i, cb * 512:(cb + 1) * 512], func=AF.Copy,
                                                    scale=small[:, 18:19]),
                      reads=[("xres", i), ("rstd", 18)], writes=[xk])
                b = psalloc(1)[0]
                for q in range(4):
                    tr.op("pe", lambda e: e.transpose(out=ps[b][:, q * 128:(q + 1) * 128], in_=xb[:, q * 128:(q + 1) * 128],
                                                      identity=ident[:, :]),
                          reads=[xk, "ident"], writes=[("ps", b)], sig=(q == 3))
                for q in range(4):
                    kc = cb * 4 + q
                    o = hT[:, kc, i * 128:(i + 1) * 128]
                    pin = ps[b][:, q * 128:(q + 1) * 128]
                    gcol = cols[:, gc0 + kc:gc0 + kc + 1]
                    if q % 2 == 0:
                        tr.op("act", lambda e: e.activation(out=o, in_=pin, func=AF.Copy, scale=gcol),
                              reads=[("ps", b), "cols"], writes=["hT"])
                    else:
                        tr.op("dve", lambda e: e.tensor_scalar(out=o, in0=pin, scalar1=gcol, scalar2=None, op0=ALU.mult),
                              reads=[("ps", b), "cols"], writes=["hT"])

        tr.dma("sp", lambda e: e.dma_start(out=cols[:, :], in_=cols_d[:, :]), "c0a", writes=["cols"])
        tr.dma("sp", lambda e: e.dma_start(out=ident[:, :], in_=ident_d[:, :]), "c0b", writes=["ident"])
        tr.dma("sp", lambda e: e.dma_start(out=posv[:, :], in_=posv_d[:, :]), "c0c", writes=["posv"])
        tr.op("dve", lambda e: e.memset(ones_f[:, :], 1.0), writes=["ones_f"])
        tr.op("dve", lambda e: e.memset(ones_b[:, :], 1.0), writes=["ones_b"])
        tr.op("dve", lambda e: e.memset(carry_c[:, :, :], 0.0), writes=["carry_c"])
        tr.op("dve", lambda e: e.memset(carry_x[:, :, :], 0.0), writes=["carry_x"])
        for g, w in enumerate(POOLW):
            tr.op("dve", lambda e: e.tensor_scalar(out=invc[:, g, :], in0=posv[:, :], scalar1=1.0, scalar2=float(w),
                                                   op0=ALU.add, op1=ALU.min),
                  reads=["posv"], writes=["invc"])
        tr.op("dve", lambda e: e.reciprocal(out=invc[:, :, :], in_=invc[:, :, :]), reads=["invc"], writes=["invc"])

        def load_type_consts(ty, wTm_, rs_, sb_, kw, kr, ks):
            tr.dma("sp", lambda e: e.dma_start(out=maskt[:, :], in_=mask_d[ty * 128:(ty + 1) * 128, :]),
                   "c1b", writes=["maskt"])
            tr.dma("sp", lambda e: e.dma_start(out=sb_, in_=sgub_d[ty:ty + 1, :].partition_broadcast(128)),
                   "c1c", writes=[ks])
            for hh in range(HW // 512):
                tr.dma("sp", lambda e: e.dma_start(out=Et[:, :], in_=sgu_d[ty * 128:(ty + 1) * 128, hh * 512:(hh + 1) * 512]),
                       "c1a", writes=["Et"])
                for h4 in range(4):
                    h = hh * 4 + h4
                    tr.op("dve", lambda e: e.tensor_tensor(out=wTm_[:, h * 128:(h + 1) * 128],
                                                           in0=Et[:, h4 * 128:(h4 + 1) * 128], in1=maskt[:, :], op=ALU.mult),
                          reads=["Et", "maskt"], writes=[kw])
            for hh in range(HW // 512):
                b = psalloc(1)[0]
                tr.op("pe", lambda e: e.matmul(ps[b][:, 0:512], lhsT=ones_b[:, :], rhs=wTm_[:, hh * 512:(hh + 1) * 512],
                                               start=True, stop=True),
                      reads=["ones_b", kw], writes=[("ps", b)])
                tr.op("act", lambda e: e.activation(out=rs_[:, hh * 512:(hh + 1) * 512], in_=ps[b][:, 0:512], func=AF.Copy),
                      reads=[("ps", b)], writes=[kr])

        load_type_consts(0, wTm_p[:, :], rs_p[:, :], sb_p[:, :], "wTm_p", "rs_p", "sb_p")

        ptiles = list(range(NPT + 1))
        blocks = [[(t, 0) for t in ptiles[0:BLK]]]
        rest = ptiles[BLK:]
        blocks += [[(t, 0) for t in rest[i:i + 3]] for i in range(0, len(rest), 3)]
        if len(blocks[-1]) <= 2:
            blocks[-1].append((NPT + 1, 1))
        else:
            blocks.append([(NPT + 1, 1)])
        for blk in blocks:
            assert not (any(t == 1 for _, t in blk) and len(blk) > 3)
        print("token blocks:", blocks)

        for bi, blk in enumerate(blocks):
            nt = len(blk)
            T = nt * 128
            has_s = blk[-1][1] == 1
            npp = nt - 1 if has_s else nt
            Tp = npp * 128
            last_p = npp > 0 and blk[npp - 1][0] == NPT
            tr.alias(["vbf", "t1c"], ["RG"] + P5K)

            for i, (tl, ty) in enumerate(blk):
                tr.dma("sp", lambda e: e.dma_start(out=xres[:, i, :], in_=xin[tl * 128:(tl + 1) * 128, :]),
                       f"xin{i}", writes=[("xres", i)])
            if has_s:
                load_type_consts(1, wTm_s, rs_s, sb_s, "t1c", "t1c", "t1c")
            for i in range(nt):
                rmsnorm_to_hT(i, C_G0)

            for cb in range(NCB):
                bk = psalloc(nt)
                nq = KD // 8
                for q in range(nq):
                    s = wload(w_in_e[q * 1024:(q + 1) * 1024, D + cb * 512: D + (cb + 1) * 512], 8, 512)
                    for i in range(nt):
                        for kc in range(8):
                            last = (q == nq - 1 and kc == 7)
                            lhsT = hT[:, q * 8 + kc, i * 128:(i + 1) * 128]
                            rhs = wblk[s][:, kc * 512:(kc + 1) * 512]
                            out = ps[bk[i]][:, 0:512]
                            tr.op("pe", lambda e: e.matmul(out, lhsT=lhsT, rhs=rhs, start=(q == 0 and kc == 0), stop=last),
                                  reads=[("wb", s), "hT"], writes=[("ps", bk[i])], sig=(kc == 7))
                for i in range(nt):
                    pin = ps[bk[i]][:, 0:512]
                    ob = vbf[:, i * D + cb * 512: i * D + (cb + 1) * 512]
                    if blk[i][1] == 1:
                        tr.op("act", lambda e: e.activation(out=Et[:, :], in_=pin, func=AF.Gelu),
                              reads=[("ps", bk[i])], writes=["Et"])
                        tr.op("dve", lambda e: e.bn_stats(out=bnst[:, i, cb, :], in_=Et[:, :]),
                              reads=["Et"], writes=["bnst"])
                        tr.op("dve", lambda e: e.tensor_copy(out=ob, in_=Et[:, :]), reads=["Et"], writes=["vbf"])
                        tr.dma("sp", lambda e: e.dma_start(out=vout_d[:, cb * 512:(cb + 1) * 512], in_=Et[:, :]),
                               "vpark", reads=["Et"], writes=["voutd"])
                    else:
                        tr.op("act", lambda e: e.activation(out=ob, in_=pin, func=AF.Gelu),
                              reads=[("ps", bk[i])], writes=["vbf"])
                        tr.op("dve", lambda e: e.bn_stats(out=bnst[:, i, cb, :], in_=ob),
                              reads=["vbf"], writes=["bnst"])
            for i in range(nt):
                mv = small[:, 20 + 4 * i: 22 + 4 * i]
                rs_ = small[:, 22 + 4 * i: 23 + 4 * i]
                ngm = small[:, 23 + 4 * i: 24 + 4 * i]
                wsrc, wk = (wTm_s, "t1c") if blk[i][1] == 1 else (wTm_p[:, :], "wTm_p")
                tr.op("dve", lambda e: e.bn_aggr(out=mv, in_=bnst[:, i, 0:NCB, :]), reads=["bnst"], writes=[("mv", i)])
                tr.op("dve", lambda e: e.tensor_scalar(out=rs_, in0=mv[:, 1:2], scalar1=LN_EPS, scalar2=None, op0=ALU.add),
                      reads=[("mv", i)], writes=[("mv", i)])
                tr.op("act", lambda e: e.activation(out=rs_, in_=rs_, func=AF.Sqrt), reads=[("mv", i)], writes=[("mv", i)])
                tr.op("dve", lambda e: e.reciprocal(out=rs_, in_=rs_), reads=[("mv", i)], writes=[("mv", i)])
                tr.op("dve", lambda e: e.tensor_scalar(out=ngm, in0=mv[:, 0:1], scalar1=-1.0, scalar2=None, op0=ALU.mult),
                      reads=[("mv", i)], writes=[("mv", i)])
                tr.op("dve", lambda e: e.tensor_scalar(out=wTs[:, i, :], in0=wsrc, scalar1=rs_, scalar2=None, op0=ALU.mult),
                      reads=[("mv", i), wk], writes=["wTs"])
                tr.op("dve", lambda e: e.tensor_scalar(out=nm[:, i, :], in0=ones_f[:, :], scalar1=ngm, scalar2=None, op0=ALU.mult),
                      reads=[("mv", i), "ones_f"], writes=["nm"])
            if has_s:
                i = nt - 1
                mv = small[:, 20 + 4 * i: 22 + 4 * i]
                rs_ = small[:, 22 + 4 * i: 23 + 4 * i]
                for cb in range(NCB):
                    c0 = cb * 512
                    tr.dma("sp", lambda e: e.dma_start(out=Et[:, :], in_=vout_d[:, c0:c0 + 512]), "vback", reads=["voutd"], writes=["Et"])
                    tr.dma("sp", lambda e: e.dma_start(out=ug[:, :], in_=vg_d[0:1, c0:c0 + 512].partition_broadcast(128)),
                           "vg0", writes=["ug"])
                    tr.dma("sp", lambda e: e.dma_start(out=sg[:, :], in_=vb_d[0:1, c0:c0 + 512].partition_broadcast(128)),
                           "vb0", writes=["sg"])
                    tr.op("dve", lambda e: e.tensor_scalar(out=Et[:, :], in0=Et[:, :], scalar1=mv[:, 0:1], scalar2=rs_,
                                                           op0=ALU.subtract, op1=ALU.mult),
                          reads=["Et", ("mv", i)], writes=["Et"])
                    tr.op("dve", lambda e: e.tensor_tensor(out=Et[:, :], in0=Et[:, :], in1=ug[:, :], op=ALU.mult),
                          reads=["Et", "ug"], writes=["Et"])
                    tr.op("dve", lambda e: e.tensor_tensor(out=Et[:, :], in0=Et[:, :], in1=sg[:, :], op=ALU.add),
                          reads=["Et", "sg"], writes=["Et"])
                    tr.dma("sp", lambda e: e.dma_start(out=vout_d[:, c0:c0 + 512], in_=Et[:, :]), "vout", reads=["Et"], writes=["voutd"])

            for g in range(KD // 8):
                for pr in range(4):
                    j0 = g * 8 + 2 * pr
                    bu = inproj_fm(w_in_e, j0 * 128, T)
                    bg = inproj_fm(w_in_e, 2 * D + j0 * 128, T)
                    for fi in range(2):
                        j = j0 + fi
                        h = j // HT
                        tr.op("act", lambda e: e.activation(out=ug[:, 0:T], in_=ps[bu[fi]][:, 0:T], func=AF.Gelu),
                              reads=[("ps", bu[fi])], writes=["ug"])
                        tr.op("act", lambda e: e.activation(out=sg[:, 0:T], in_=ps[bg[fi]][:, 0:T], func=AF.Silu),
                              reads=[("ps", bg[fi])], writes=["sg"])
                        bm = psalloc(1)[0]
                        for i in range(nt):
                            out = ps[bm][:, i * 128:(i + 1) * 128]
                            tr.op("pe", lambda e: e.matmul(out, lhsT=vbf[:, i * D + j * 128: i * D + (j + 1) * 128],
                                                           rhs=wTs[:, i, h * 128:(h + 1) * 128], start=True, stop=False),
                                  reads=["vbf", "wTs"], writes=[("ps", bm)], sig=False)
                            tr.op("pe", lambda e: e.matmul(out, lhsT=nm[:, i, :], rhs=wTs[:, i, h * 128:(h + 1) * 128],
                                                           start=False, stop=True),
                                  reads=["nm", "wTs"], writes=[("ps", bm)], sig=(i == nt - 1))
                        for ty in ([0] if npp > 0 else []) + ([1] if has_s else []):
                            r_, s_, kk = (rs_s, sb_s, "t1c") if ty == 1 else (rs_p[:, :], sb_p[:, :], "rs_p")
                            tr.op("dve", lambda e: e.scalar_tensor_tensor(out=At[:, ty, :], in0=r_[:, h * 128:(h + 1) * 128],
                                                                          scalar=cols[:, C_VB + j:C_VB + j + 1],
                                                                          in1=s_[:, h * 128:(h + 1) * 128], op0=ALU.mult, op1=ALU.add),
                                  reads=[kk, "sb_p", "cols"], writes=["At"])
                        for i in range(nt):
                            ty = blk[i][1]
                            tr.op("dve", lambda e: e.scalar_tensor_tensor(out=Et[:, i * 128:(i + 1) * 128],
                                                                          in0=ps[bm][:, i * 128:(i + 1) * 128],
                                                                          scalar=cols[:, C_VG + j:C_VG + j + 1],
                                                                          in1=At[:, ty, :], op0=ALU.mult, op1=ALU.add),
                                  reads=[("ps", bm), "At", "cols"], writes=["Et"])
                        tr.op("dve", lambda e: e.tensor_tensor(out=Et[:, 0:T], in0=Et[:, 0:T], in1=ug[:, 0:T], op=ALU.mult),
                              reads=["Et", "ug"], writes=["Et"])
                        jj = 2 * pr + fi
                        tr.op("dve", lambda e: e.tensor_tensor(out=yT[:, jj, 0:T], in0=Et[:, 0:T], in1=sg[:, 0:T], op=ALU.mult),
                              reads=["Et", "sg"], writes=["yT"])
                outproj(w_out_e, g * 1024, nt)

            for k_ in P3K:
                tr.alias([k_], ["vbf", "t1c"])
            for g in range(KD // 8):
                for hd in range(8 // HT):
                    j_h0 = g * 8 + hd * HT
                    for pr in range(HT // 2):
                        j0 = j_h0 + 2 * pr
                        ba = inproj_fm(w_in_e, 3 * D + j0 * 128, T)
                        bb = inproj_fm(w_in_e, 4 * D + j0 * 128, T)
                        for fi in range(2):
                            j = j0 + fi
                            q = 2 * pr + fi
                            sgt = fm(sigb, T)
                            tr.op("act", lambda e: e.activation(out=sgt, in_=ps[bb[fi]][:, 0:T], func=AF.Sigmoid),
                                  reads=[("ps", bb[fi])], writes=["sig"])
                            cacc = fm(cbuf[q], T)
                            wk0 = cols[:, C_CW + j * CONV_W: C_CW + j * CONV_W + 1]
                            cbias = cols[:, C_CB + j:C_CB + j + 1]
                            if npp > 0:
                                ext = fm(cvb, LC)
                                tr.op("act", lambda e: e.activation(out=ext[:, 0:NPRE], in_=carry_c[:, j, :], func=AF.Copy),
                                      reads=["carry_c"], writes=["cvb"])
                                tr.op("dve", lambda e: e.tensor_tensor(out=ext[:, NPRE:NPRE + Tp], in0=ps[ba[fi]][:, 0:Tp],
                                                                       in1=sgt[:, 0:Tp], op=ALU.mult),
                                      reads=[("ps", ba[fi]), "sig"], writes=["cvb"])
                                tr.op("act", lambda e: e.activation(out=carry_c[:, j, :], in_=ext[:, Tp:Tp + NPRE], func=AF.Copy),
                                      reads=["cvb"], writes=["carry_c"])
                                tr.op("dve", lambda e: e.tensor_scalar(out=cacc[:, 0:Tp], in0=ext[:, 0:Tp], scalar1=wk0, scalar2=cbias,
                                                                       op0=ALU.mult, op1=ALU.add),
                                      reads=["cvb", "cols"], writes=[("cb", q)])
                                for k in range(1, CONV_W):
                                    wk = cols[:, C_CW + j * CONV_W + k: C_CW + j * CONV_W + k + 1]
                                    tr.op("dve", lambda e: e.scalar_tensor_tensor(out=cacc[:, 0:Tp], in0=ext[:, k:k + Tp], scalar=wk,
                                                                                  in1=cacc[:, 0:Tp], op0=ALU.mult, op1=ALU.add),
                                          reads=["cvb", "cols", ("cb", q)], writes=[("cb", q)])
                            if has_s:
                                LS = NPRE + DS
                                ext3 = RG[:, cvs[fi]:cvs[fi] + NQ * LS].rearrange("p (q l) -> p q l", q=NQ)
                                tr.dma("sp", lambda e: e.dma_start(
                                    out=ext3[:, :, 0:NPRE], in_=stcT_d[j * 128:(j + 1) * 128, :].rearrange("p (q l) -> p q l", q=NQ)),
                                    f"stc{fi}", writes=[("cvsP", fi)])
                                pa3 = ps[ba[fi]][:, Tp:T].rearrange("p (q l) -> p q l", q=NQ)
                                sg3 = sgt[:, Tp:T].rearrange("p (q l) -> p q l", q=NQ)
                                tr.op("dve", lambda e: e.tensor_tensor(out=ext3[:, :, NPRE:LS], in0=pa3, in1=sg3, op=ALU.mult),
                                      reads=[("ps", ba[fi]), "sig"], writes=[("cvsN", fi)])
                                gq = fm(cnb[fi], 128)
                                gq3 = gq.rearrange("p (q l) -> p q l", q=NQ)
                                tr.op("act", lambda e: e.activation(out=gq3, in_=ext3[:, :, NPRE:LS], func=AF.Copy),
                                      reads=[("cvsN", fi)], writes=[("cn", fi)])
                                tr.dma("sp", lambda e: e.dma_start(out=gluTs_d[j * 128:(j + 1) * 128, :], in_=gq),
                                       f"glo{fi}", reads=[("cn", fi)])
                                c3 = cacc[:, Tp:T].rearrange("p (q l) -> p q l", q=NQ)
                                tr.op("dve", lambda e: e.tensor_scalar(out=c3, in0=ext3[:, :, 0:DS], scalar1=wk0, scalar2=cbias,
                                                                       op0=ALU.mult, op1=ALU.add),
                                      reads=[("cvsP", fi), ("cvsN", fi), "cols", ("cb", q)], writes=[("cb", q)])
                                for k in range(1, CONV_W):
                                    wk = cols[:, C_CW + j * CONV_W + k: C_CW + j * CONV_W + k + 1]
                                    tr.op("dve", lambda e: e.scalar_tensor_tensor(out=c3, in0=ext3[:, :, k:k + DS], scalar=wk, in1=c3,
                                                                                  op0=ALU.mult, op1=ALU.add),
                                          reads=[("cvsP", fi), ("cvsN", fi), "cols", ("cb", q)], writes=[("cb", q)])
                    sgb_of = {}
                    bgbs = []
                    for pr in range(HT // 2):
                        bgbs.append(inproj_fm(w_in_e, 5 * D + (j_h0 + 2 * pr) * 128, T))
                    b1, b2 = psalloc(2)
                    for q in range(HT):
                        cq = fm(cbuf[q], T)
                        sq = fm(sqb, T)
                        tr.op("act", lambda e: e.activation(out=sq, in_=cq, func=AF.Square), reads=[("cb", q)], writes=["sq"])
                        tr.op("pe", lambda e: e.matmul(ps[b1][:, 0:T], lhsT=ones_f[:, :], rhs=cq, start=(q == 0), stop=(q == HT - 1)),
                              reads=[("cb", q), "ones_f"], writes=[("ps", b1)], sig=(q == HT - 1))
                        tr.op("pe", lambda e: e.matmul(ps[b2][:, 0:T], lhsT=ones_f[:, :], rhs=sq, start=(q == 0), stop=(q == HT - 1)),
                              reads=["sq", "ones_f"], writes=[("ps", b2)], sig=True)
                    mean, rstd, msq = fm(meanb, T), fm(rstdb, T), fm(msqb, T)
                    nfe = float(HT * 128)
                    tr.op("dve", lambda e: e.tensor_scalar(out=mean, in0=ps[b1][:, 0:T], scalar1=1.0 / nfe, scalar2=None, op0=ALU.mult),
                          reads=[("ps", b1)], writes=["mean"])
                    tr.op("dve", lambda e: e.tensor_tensor(out=msq, in0=mean, in1=mean, op=ALU.mult), reads=["mean"], writes=["msq"])
                    tr.op("dve", lambda e: e.scalar_tensor_tensor(out=rstd, in0=ps[b2][:, 0:T], scalar=1.0 / nfe, in1=msq,
                                                                  op0=ALU.mult, op1=ALU.subtract),
                          reads=[("ps", b2), "msq"], writes=["lnrstd"])
                    tr.op("dve", lambda e: e.tensor_scalar(out=rstd, in0=rstd, scalar1=LN_EPS, scalar2=None, op0=ALU.add),
                          reads=["lnrstd"], writes=["lnrstd"])
                    tr.op("act", lambda e: e.activation(out=rstd, in_=rstd, func=AF.Sqrt), reads=["lnrstd"], writes=["lnrstd"])
                    tr.op("dve", lambda e: e.reciprocal(out=rstd, in_=rstd), reads=["lnrstd"], writes=["lnrstd"])
                    for pr in range(HT // 2):
                        bgb = bgbs[pr]
                        for fi in range(2):
                            j = j_h0 + 2 * pr + fi
                            q = 2 * pr + fi
                            sgbt = fm(sgbb[fi], T)
                            tr.op("act", lambda e: e.activation(out=sgbt, in_=ps[bgb[fi]][:, 0:T], func=AF.Silu),
                                  reads=[("ps", bgb[fi])], writes=[("sgb", fi)])
                            cq = fm(cbuf[q], T)
                            cn = fm(cnb[fi], T)
                            tr.op("dve", lambda e: e.tensor_tensor(out=cn, in0=cq, in1=mean, op=ALU.subtract),
                                  reads=[("cb", q), "mean"], writes=[("cn", fi)])
                            tr.op("dve", lambda e: e.tensor_tensor(out=cn, in0=cn, in1=rstd, op=ALU.mult),
                                  reads=[("cn", fi), "lnrstd"], writes=[("cn", fi)])
                            tr.op("act", lambda e: e.activation(out=cn, in_=cn, func=AF.Silu,
                                                                scale=cols[:, C_CNG + j:C_CNG + j + 1],
                                                                bias=cols[:, C_CNB + j:C_CNB + j + 1]),
                                  reads=[("cn", fi), "cols"], writes=[("cn", fi)])
                            jj = hd * HT + q
                            tr.op("dve", lambda e: e.tensor_tensor(out=yT[:, jj, 0:T], in0=cn, in1=sgbt, op=ALU.mult),
                                  reads=[("cn", fi), ("sgb", fi)], writes=["yT"])
                outproj(w_out_e, (KD // 8 + g) * 1024, nt)

            if last_p:
                tr.dma("sp", lambda e: e.dma_start(out=gluTp_d.rearrange("(k p) t -> p k t", p=128), in_=carry_c[:, :, :]),
                       "glp", reads=["carry_c"])

            for i in range(nt):
                rmsnorm_to_hT(i, C_G1)

            for k_ in P5K:
                tr.alias([k_], P3K + ["vbf", "t1c"])
            for gp, w in enumerate(POOLW):
                nstep = gp + 1
                for pr in range(CGT // 2):
                    cj0 = gp * CGT + 2 * pr
                    bx = inproj_fm(w_in_o, cj0 * 128, T)
                    for fi in range(2):
                        cj = cj0 + fi
                        jl = 2 * pr + fi
                        if npp > 0:
                            ext = fm(xcb, LX)
                            tr.op("act", lambda e: e.activation(out=ext[:, 0:PPRE], in_=carry_x[:, cj, :], func=AF.Copy),
                                  reads=["carry_x"], writes=["xcb"])
                            tr.op("act", lambda e: e.activation(out=ext[:, PPRE:PPRE + Tp], in_=ps[bx[fi]][:, 0:Tp], func=AF.Copy),
                                  reads=[("ps", bx[fi])], writes=["xcb"])
                            tr.op("act", lambda e: e.activation(out=carry_x[:, cj, :], in_=ext[:, Tp:Tp + PPRE], func=AF.Copy),
                                  reads=["xcb"], writes=["carry_x"])
                            L = PPRE + Tp
                            cur, curk = ext, "xcb"
                            for m in range(nstep):
                                sh = 1 << m
                                vs_ = (1 << (m + 1)) - 1
                                nxt = fm(sAb if m % 2 == 0 else sBb, LX)
                                nk = "sA" if m % 2 == 0 else "sB"
                                c_, n_ = cur, nxt
                                tr.op("dve", lambda e: e.tensor_tensor(out=n_[:, vs_:L], in0=c_[:, vs_:L], in1=c_[:, vs_ - sh:L - sh], op=ALU.add),
                                      reads=[curk], writes=[nk])
                                cur, curk = nxt, nk
                            dsl = dT[:, jl * TMAX: jl * TMAX + Tp]
                            c_ = cur
                            tr.op("dve", lambda e: e.scalar_tensor_tensor(out=dsl, in0=c_[:, PPRE:PPRE + Tp], scalar=1.0 / w,
                                                                          in1=ext[:, PPRE:PPRE + Tp], op0=ALU.mult, op1=ALU.subtract),
                                  reads=[curk, "xcb"], writes=["dT"])
                            if bi == 0:
                                o0 = 128
                                t16v = RG[:, t16:t16 + 16]
                                tr.op("dve", lambda e: e.tensor_tensor(out=t16v, in0=c_[:, PPRE + o0:PPRE + o0 + 16], in1=invc[:, gp, :], op=ALU.mult),
                                      reads=[curk, "invc"], writes=["t16"])
                                tr.op("dve", lambda e: e.tensor_tensor(out=dsl[:, o0:o0 + 16], in0=t16v, in1=ext[:, PPRE + o0:PPRE + o0 + 16],
                                                                       op=ALU.subtract),
                                      reads=["t16", "xcb"], writes=["dT"])
                        if has_s:
                            LS = PPRE + DS
                            ext3 = RG[:, xcs[fi]:xcs[fi] + NQ * LS].rearrange("p (q l) -> p q l", q=NQ)
                            tr.dma("sp", lambda e: e.dma_start(
                                out=ext3[:, :, 0:PPRE], in_=stpT_d[cj * 128:(cj + 1) * 128, :].rearrange("p (q l) -> p q l", q=NQ)),
                                f"stp{fi}", writes=[("xcsP", fi)])
                            px3 = ps[bx[fi]][:, Tp:T].rearrange("p (q l) -> p q l", q=NQ)
                            tr.op("act", lambda e: e.activation(out=ext3[:, :, PPRE:LS], in_=px3, func=AF.Copy),
                                  reads=[("ps", bx[fi])], writes=[("xcsN", fi)])
                            xo = fm(sgcb[fi], 128)
                            tr.op("act", lambda e: e.activation(out=xo, in_=ps[bx[fi]][:, Tp:T], func=AF.Copy),
                                  reads=[("ps", bx[fi])], writes=[("sgc", fi)])
                            tr.dma("sp", lambda e: e.dma_start(out=xcTs_d[cj * 128:(cj + 1) * 128, :], in_=xo),
                                   f"xco{fi}", reads=[("sgc", fi)])
                            cur, curk = ext3, [("xcsP", fi), ("xcsN", fi)]
                            for m in range(nstep):
                                sh = 1 << m
                                vs_ = (1 << (m + 1)) - 1
                                a_ = sAs if m % 2 == 0 else sBs
                                nxt = RG[:, a_:a_ + NQ * LS].rearrange("p (q l) -> p q l", q=NQ)
                                nk = "sAs" if m % 2 == 0 else "sBs"
                                c_, n_ = cur, nxt
                                tr.op("dve", lambda e: e.tensor_tensor(out=n_[:, :, vs_:LS], in0=c_[:, :, vs_:LS], in1=c_[:, :, vs_ - sh:LS - sh], op=ALU.add),
                                      reads=curk, writes=[nk])
                                cur, curk = nxt, [nk]
                            dsl3 = dT[:, jl * TMAX + Tp: jl * TMAX + T].rearrange("p (q l) -> p q l", q=NQ)
                            c_ = cur
                            tr.op("dve", lambda e: e.scalar_tensor_tensor(out=dsl3, in0=c_[:, :, PPRE:LS], scalar=1.0 / w, in1=ext3[:, :, PPRE:LS],
                                                                          op0=ALU.mult, op1=ALU.subtract),
                                  reads=curk + [("xcsP", fi), ("xcsN", fi)], writes=["dT"])
                for half in range(CGT // 8):
                    for pr in range(4):
                        dj0 = half * 8 + 2 * pr
                        bk = psalloc(2)
                        s = wload(pool_w[gp * CGT * 128:(gp + 1) * CGT * 128, dj0 * 128: dj0 * 128 + 256], CGT, 256)
                        for fi in range(2):
                            for kc in range(CGT):
                                lhsT = wblk[s][:, kc * 256 + fi * 128: kc * 256 + (fi + 1) * 128]
                                rhs = dT[:, kc * TMAX: kc * TMAX + T]
                                out = ps[bk[fi]][:, 0:T]
                                tr.op("pe", lambda e: e.matmul(out, lhsT=lhsT, rhs=rhs, start=(kc == 0), stop=(kc == CGT - 1)),
                                      reads=[("wb", s), "dT"], writes=[("ps", bk[fi])], sig=(kc == CGT - 1))
                        bgc = inproj_fm(w_in_o, 2 * D + (gp * CGT + dj0) * 128, T)
                        for fi in range(2):
                            fgl = gp * CGT + dj0 + fi
                            sgct = fm(sgcb[fi], T)
                            tr.op("act", lambda e: e.activation(out=sgct, in_=ps[bgc[fi]][:, 0:T], func=AF.Silu),
                                  reads=[("ps", bgc[fi])], writes=[("sgc", fi)])
                            jj = 2 * pr + fi
                            tr.op("dve", lambda e: e.scalar_tensor_tensor(out=yT[:, jj, 0:T], in0=ps[bk[fi]][:, 0:T],
                                                                          scalar=cols[:, C_PS + fgl:C_PS + fgl + 1],
                                                                          in1=sgct, op0=ALU.mult, op1=ALU.mult),
                                  reads=[("ps", bk[fi]), ("sgc", fi), "cols"], writes=["yT"])
                    outproj(w_out_o, ((gp * CGT) // 8 + half) * 1024, nt)
            if last_p:
                tr.dma("sp", lambda e: e.dma_start(out=xcTp_d.rearrange("(k p) t -> p k t", p=128), in_=carry_x[:, :, :]),
                       "xcp", reads=["carry_x"])

            outs = [(i, tl) for i, (tl, ty) in enumerate(blk) if tl != 0]
            for i, tl in outs:
                row_rstd(i, 40 + i)
            for cb in range(NCB):
                gb, gk = (sg, "sg") if cb % 2 == 0 else (Et, "Et")
                tr.dma("sp", lambda e: e.dma_start(out=gb[:, :], in_=nf_d[0:1, cb * 512:(cb + 1) * 512].partition_broadcast(128)),
                       "nfb0" if cb % 2 == 0 else "nfb1", writes=[gk])
                for i, tl in outs:
                    xs = xres[:, i, cb * 512:(cb + 1) * 512]
                    tr.op("dve", lambda e: e.scalar_tensor_tensor(out=xs, in0=xs, scalar=small[:, 40 + i:41 + i], in1=gb[:, :],
                                                                  op0=ALU.mult, op1=ALU.mult),
                          reads=[("xres", i), ("rstd", 40 + i), gk], writes=[("xres", i)])
            for i, tl in outs:
                ot = tl - 1
                tr.dma("sp", lambda e: e.dma_start(out=y_d[ot * 128:(ot + 1) * 128, :], in_=xres[:, i, :]),
                       f"yo{i}", reads=[("xres", i)])
            tr.alias(["RG"], P5K + P3K + ["vbf", "t1c"])

        tr.dma("sp", lambda e: e.dma_start(out=convs_old[:, :], in_=stc_tm[:, DS * D:NPRE * D]), "cso")
        tr.dma("sp", lambda e: e.dma_start(out=pools_old[:, :], in_=stp_tm[:, DS * 2 * D:PPRE * 2 * D]), "pso")
        tr.final_wait("sp")
        print("bass ops emitted:", tr.nops)
    return nc


_PROG_CACHE = {}


def _prep_inputs(inp):
    xp = np.asarray(inp["x_prompt"], np.float32)
    xs = np.asarray(inp["x_sample"], np.float32)
    NB, SEQ, D = xp.shape
    DB, DS, _ = xs.shape
    KD = D // 128
    KC = 2 * KD
    half = SEQ // 2
    NPT = half // 128
    NQ = DB // NCORES
    assert NB * 2 == NCORES and NQ * DS == 128

    def colmaj(v):
        v = np.asarray(v, np.float32).reshape(-1, 128)
        return np.ascontiguousarray(v.T)

    cw = np.asarray(inp["conv_w"], np.float32)[0]
    cw_cols = np.ascontiguousarray(cw.T.reshape(KD, 128, CONV_W).transpose(1, 0, 2).reshape(128, KD * CONV_W))
    cols = np.concatenate([
        colmaj(inp["norm_in"][0]), colmaj(inp["norm_in"][1]), colmaj(inp["v_norm_g"][0]), colmaj(inp["v_norm_b"][0]),
        colmaj(inp["conv_b"][0]), colmaj(inp["conv_norm_g"][0]), colmaj(inp["conv_norm_b"][0]),
        colmaj(inp["pool_scale"][0]), np.zeros((128, KD), np.float32), cw_cols], axis=1)
    cols = np.ascontiguousarray(cols, np.float32)

    sgw = np.asarray(inp["sgu_w"], np.float32)[0]
    sgb = np.asarray(inp["sgu_b"], np.float32)[0]
    wT_p = np.ascontiguousarray(sgw.transpose(2, 0, 1).reshape(128, NHEAD * 128))
    wT_s = np.zeros((128, NHEAD, 128), np.float32)
    for q in range(NQ):
        wT_s[q * DS:(q + 1) * DS, :, q * DS:(q + 1) * DS] = sgw[:, :DS, :DS].transpose(2, 0, 1)
    sgu = np.concatenate([wT_p, wT_s.reshape(128, NHEAD * 128)], axis=0)
    s_i = np.arange(128)[:, None]
    t_i = np.arange(128)[None, :]
    mask_p = (s_i <= t_i).astype(np.float32)
    mask_s = ((s_i // DS == t_i // DS) & (s_i <= t_i)).astype(np.float32)
    mask = np.concatenate([mask_p, mask_s], axis=0)
    sgub = np.stack([sgb.reshape(-1), np.tile(sgb[:, :DS], (1, NQ)).reshape(-1)], axis=0).astype(np.float32)
    ident = np.eye(128, dtype=np.float32)

    shared = {
        "w_in_e": np.asarray(inp["w_in_even"], np.float32)[0],
        "w_out_e": np.asarray(inp["w_out_even"], np.float32)[0],
        "w_in_o": np.asarray(inp["w_in_odd"], np.float32)[0],
        "pool_w": np.asarray(inp["pool_w"], np.float32)[0].reshape(4 * (2 * D // 4), 2 * D // 4),
        "w_out_o": np.asarray(inp["w_out_odd"], np.float32)[0],
        "cols": cols, "nf": np.asarray(inp["norm_f"], np.float32).reshape(1, D),
        "vg": np.asarray(inp["v_norm_g"], np.float32).reshape(1, D),
        "vb": np.asarray(inp["v_norm_b"], np.float32).reshape(1, D),
        "sgub": sgub, "sgu": sgu, "mask": mask, "ident": ident,
    }
    stc = np.asarray(inp["state_conv"], np.float32)[0]
    stp = np.asarray(inp["state_pool"], np.float32)[0]
    in_maps = []
    for c in range(NCORES):
        b, hf = c // 2, c % 2
        halo = xp[b, half - 128:half] if hf == 1 else np.zeros((128, D), np.float32)
        xin = np.concatenate([halo, xp[b, hf * half:(hf + 1) * half], xs[c * NQ:(c + 1) * NQ].reshape(128, D)], axis=0)
        sc = stc[c * NQ:(c + 1) * NQ]
        sp_ = stp[c * NQ:(c + 1) * NQ]
        m = dict(shared)
        m["xin"] = np.ascontiguousarray(xin)
        m["posv"] = np.ascontiguousarray(np.broadcast_to((hf * half + np.arange(16, dtype=np.float32))[None, :], (128, 16)))
        m["stcT"] = np.ascontiguousarray(sc.transpose(2, 0, 1).reshape(D, NQ * NPRE))
        m["stpT"] = np.ascontiguousarray(sp_.transpose(2, 0, 1).reshape(2 * D, NQ * PPRE))
        m["stc_tm"] = np.ascontiguousarray(sc.reshape(NQ, NPRE * D))
        m["stp_tm"] = np.ascontiguousarray(sp_.reshape(NQ, PPRE * 2 * D))
        in_maps.append(m)
    return in_maps, (NB, SEQ, D, DB, DS, NPT, NQ)


def kernel(**inputs):
    in_maps, (NB, SEQ, D, DB, DS, NPT, NQ) = _prep_inputs(inputs)
    key = (D, NPT)
    if key not in _PROG_CACHE:
        _PROG_CACHE[key] = build_program(D, NPT)
    nc = _PROG_CACHE[key]
    res = run_bass_kernel_spmd(nc, in_maps, core_ids=list(range(NCORES))).results
    half = SEQ // 2
    y_prompt = np.zeros((NB, SEQ, D), np.float32)
    y_sample = np.zeros((DB, DS, D), np.float32)
    new_v = np.zeros((1, DB, DS, D), np.float32)
    conv_p = np.zeros((1, NB, NPRE, D), np.float32)
    conv_s = np.zeros((1, DB, NPRE, D), np.float32)
    pool_p = np.zeros((1, NB, PPRE, 2 * D), np.float32)
    pool_s = np.zeros((1, DB, PPRE, 2 * D), np.float32)
    for c in range(NCORES):
        r = res[c]
        b, hf = c // 2, c % 2
        y = r["y"].reshape(NPT + 1, 128, D)
        y_prompt[b, hf * half:(hf + 1) * half] = y[:NPT].reshape(half, D)
        qs = slice(c * NQ, (c + 1) * NQ)
        y_sample[qs] = y[NPT].reshape(NQ, DS, D)
        new_v[0, qs] = r["vout"].reshape(NQ, DS, D)
        conv_s[0, qs, :NPRE - DS] = r["convs_old"].reshape(NQ, NPRE - DS, D)
        conv_s[0, qs, NPRE - DS:] = r["gluTs"].T.reshape(NQ, DS, D)
        pool_s[0, qs, :PPRE - DS] = r["pools_old"].reshape(NQ, PPRE - DS, 2 * D)
        pool_s[0, qs, PPRE - DS:] = r["xcTs"].T.reshape(NQ, DS, 2 * D)
        if hf == 1:
            conv_p[0, b] = r["gluTp"].T
            pool_p[0, b] = r["xcTp"].T
    return (y_prompt, y_sample, new_v, conv_p, conv_s, pool_p, pool_s)
```

```python
import numpy as np
from contextlib import ExitStack
import concourse.bass as bass
import concourse.mybir as mybir
from concourse.bass_utils import run_bass_kernel_spmd

F32 = mybir.dt.float32
BF16 = mybir.dt.bfloat16
AF = mybir.ActivationFunctionType
ALU = mybir.AluOpType
AX = mybir.AxisListType

CONV_W = 31
NPRE = CONV_W - 1
POOLW = (2, 4, 8, 16)
PPRE = 15
RMS_EPS = 1e-6
LN_EPS = 1e-5
NHEAD = 8
BLK = 4
NCORES = 8
ENGS = ("pe", "act", "dve", "pool", "sp")


class Tr:
    def __init__(self, nc, es):
        self.nc = nc
        self.es = es
        self.E = {"pe": nc.tensor, "act": nc.scalar, "dve": nc.vector, "pool": nc.gpsimd, "sp": nc.sync}
        self.nops = 0
        self.cnt = {}
        self.sem = {}
        self.seen = {e: {} for e in ENGS}
        self.W = {}
        self.R = {}
        for e in ("pe", "act", "dve", "pool"):
            self._mksem(e)

    def _mksem(self, name):
        if name not in self.sem:
            self.sem[name] = self.es.enter_context(self.nc.semaphore("s_" + name))
            self.cnt[name] = 0

    def _deps(self, eng, reads, writes):
        need = {}
        for k in reads:
            for s, t in self.W.get(k, {}).items():
                need[s] = max(need.get(s, 0), t)
        for k in writes:
            for s, t in self.W.get(k, {}).items():
                need[s] = max(need.get(s, 0), t)
            for s, t in self.R.get(k, {}).items():
                need[s] = max(need.get(s, 0), t)
        for s, t in need.items():
            if s == "pe" and eng == "pe":
                continue
            if self.seen[eng].get(s, 0) >= t:
                continue
            self.seen[eng][s] = t
            self.E[eng].wait_ge(self.sem[s], t)

    def _mark(self, s, tick, reads, writes):
        for k in reads:
            d = self.R.setdefault(k, {})
            d[s] = max(d.get(s, 0), tick)
        for k in writes:
            self.W[k] = {s: tick}
            self.R[k] = {}

    def op(self, eng, fn, reads=(), writes=(), sig=True):
        self._deps(eng, reads, writes)
        if sig:
            self.cnt[eng] += 1
            tick = self.cnt[eng]
            fn(self.E[eng]).then_inc(self.sem[eng], 1)
        else:
            tick = self.cnt[eng] + 1
            fn(self.E[eng])
        self.nops += 1
        self._mark(eng, tick, reads, writes)

    def dma(self, eng, fn, slot, reads=(), writes=()):
        self._mksem(slot)
        self._deps(eng, reads, writes)
        self.cnt[slot] += 16
        fn(self.E[eng]).then_inc(self.sem[slot], 16)
        self.nops += 1
        self._mark(slot, self.cnt[slot], reads, writes)

    def alias(self, new_keys, old_keys):
        w, r = {}, {}
        for k in old_keys:
            for s, t in self.W.get(k, {}).items():
                w[s] = max(w.get(s, 0), t)
            for s, t in self.R.get(k, {}).items():
                r[s] = max(r.get(s, 0), t)
        for k in new_keys:
            self.W[k] = dict(w)
            self.R[k] = dict(r)

    def final_wait(self, eng):
        for s, c in self.cnt.items():
            if c > 0 and self.seen[eng].get(s, 0) < c:
                self.E[eng].wait_ge(self.sem[s], c)
                self.seen[eng][s] = c

    def emit(self, block):
        nc = self.nc

        def run(e, items):
            for it in items:
                if it[0] == "w":
                    e.wait_ge(self.sem[it[1]], it[2])
                else:
                    ins = it[1](e)
                    if it[2] is not None:
                        ins.then_inc(self.sem[it[2]], it[3])

        q = self.q

        @block.tensor
        def _(e):
            run(e, q["pe"])

        @block.scalar
        def _(e):
            run(e, q["act"])

        @block.vector
        def _(e):
            run(e, q["dve"])

        @block.gpsimd
        def _(e):
            run(e, q["pool"])

        @block.sync
        def _(e):
            run(e, q["sp"])


def build_program(D, NPT):
    KD = D // 128
    HT = KD // NHEAD
    KC = 2 * KD
    CGT = KC // 4
    NCB = D // 512
    assert HT >= 2 and HT % 2 == 0 and KD % 16 == 0
    NTIN = NPT + 2
    NTOUT = NPT + 1
    TMAX = BLK * 128
    KH = 16
    NQ = 16
    DS = 8

    nc = bass.Bass("TRN2", target_bir_lowering=False)

    def din(name, shape):
        return nc.dram_tensor(name, list(shape), F32, kind="ExternalInput").ap()

    def dout(name, shape):
        return nc.dram_tensor(name, list(shape), F32, kind="ExternalOutput").ap()

    xin = din("xin", [NTIN * 128, D])
    w_in_e = din("w_in_e", [D, 6 * D])
    w_out_e = din("w_out_e", [2 * D, D])
    w_in_o = din("w_in_o", [D, 4 * D])
    pool_w = din("pool_w", [4 * CGT * 128, CGT * 128])
    w_out_o = din("w_out_o", [2 * D, D])
    NCOL = 8 * KD + KC + KD * CONV_W
    cols_d = din("cols", [128, NCOL])
    nf_d = din("nf", [1, D])
    vg_d = din("vg", [1, D])
    vb_d = din("vb", [1, D])
    sgub_d = din("sgub", [2, NHEAD * 128])
    sgu_d = din("sgu", [2 * 128, NHEAD * 128])
    mask_d = din("mask", [2 * 128, 128])
    ident_d = din("ident", [128, 128])
    posv_d = din("posv", [128, 16])
    stcT_d = din("stcT", [D, NQ * NPRE])
    stpT_d = din("stpT", [2 * D, NQ * PPRE])
    stc_tm = din("stc_tm", [NQ, NPRE * D])
    stp_tm = din("stp_tm", [NQ, PPRE * 2 * D])

    y_d = dout("y", [NTOUT * 128, D])
    vout_d = dout("vout", [128, D])
    gluTs_d = dout("gluTs", [D, 128])
    xcTs_d = dout("xcTs", [2 * D, 128])
    gluTp_d = dout("gluTp", [D, NPRE])
    xcTp_d = dout("xcTp", [2 * D, PPRE])
    convs_old = dout("convs_old", [NQ, (NPRE - DS) * D])
    pools_old = dout("pools_old", [NQ, (PPRE - DS) * 2 * D])

    C_G0, C_G1, C_VG, C_VB, C_CB, C_CNG, C_CNB = [i * KD for i in range(7)]
    C_PS = 7 * KD
    C_CW = 8 * KD + KC
    HW = NHEAD * 128

    with ExitStack() as es:
        def sb(name, shape, dt=F32):
            return es.enter_context(nc.sbuf_tensor("sb_" + name, list(shape), dt))

        tr = Tr(nc, es)
        ps = [es.enter_context(nc.psum_tensor(f"ps{b}", [128, 512], F32)) for b in range(8)]
        ps_rr = [0]

        def psalloc(n):
            r = [(ps_rr[0] + i) % 8 for i in range(n)]
            ps_rr[0] = (ps_rr[0] + n) % 8
            return r

        RGN = max(8960, 3 * D // 2 + 5 * (NHEAD * 128) // 2)
        xres = sb("xres", [128, BLK, D])
        hT = sb("hT", [128, KD, TMAX], BF16)
        RG = sb("RG", [128, RGN])
        yT = sb("yT", [128, 8, TMAX], BF16)
        NWB = 3
        wblk = [sb(f"wb{i}", [128, 4096], BF16) for i in range(NWB)]
        cols = sb("cols", [128, NCOL])
        rs_p = sb("rs_p", [128, HW])
        sb_p = sb("sb_p", [128, HW])
        wTm_p = sb("wTm_p", [128, HW], BF16)
        wTs = sb("wTs", [128, BLK, HW], BF16)
        nm = sb("nm", [128, BLK, 128], BF16)
        ident = sb("ident", [128, 128])
        ones_f = sb("ones_f", [128, 128])
        ones_b = sb("ones_b", [128, 128], BF16)
        maskt = sb("maskt", [128, 128])
        carry_c = sb("carry_c", [128, KD, NPRE])
        carry_x = sb("carry_x", [128, KC, PPRE])
        invc = sb("invc", [128, 4, 16])
        posv = sb("posv", [128, 16])
        ug = sb("ug", [128, 512])
        sg = sb("sg", [128, 512])
        Et = sb("Et", [128, 512])
        At = sb("At", [128, 2, 128])
        small = sb("small", [128, 64])
        bnst = sb("bnst", [128, BLK, 8, 6])

        vbf = RG[:, 0:BLK * D // 2].bitcast(BF16)
        T1O = 3 * D // 2
        rs_s = RG[:, T1O:T1O + HW]
        sb_s = RG[:, T1O + HW:T1O + 2 * HW]
        wTm_s = RG[:, T1O + 2 * HW:T1O + 2 * HW + HW // 2].bitcast(BF16)
        assert T1O + 2 * HW + HW // 2 <= RGN
        off = [0]

        def rg(n):
            a = off[0]
            off[0] += n
            assert off[0] <= RGN, off[0]
            return a

        LC = NPRE + TMAX
        cbuf = [rg(TMAX) for _ in range(HT)]
        cvb = [rg(LC) for _ in range(2)]
        sqb = rg(TMAX)
        meanb, rstdb, msqb = rg(TMAX), rg(TMAX), rg(TMAX)
        sgbb = [rg(TMAX) for _ in range(2)]
        sigb = rg(TMAX)
        cnb = [rg(TMAX) for _ in range(2)]
        cvs = [rg(NQ * (NPRE + DS)) for _ in range(2)]
        off[0] = 0
        dT_off = rg(CGT * TMAX // 2)
        LX = PPRE + TMAX
        xcb = [rg(LX) for _ in range(2)]
        sAb, sBb = rg(LX), rg(LX)
        sgcb = [rg(TMAX) for _ in range(2)]
        t16 = rg(16)
        LSX = NQ * (PPRE + DS)
        xcs = [rg(LSX) for _ in range(2)]
        sAs, sBs = rg(LSX), rg(LSX)
        dT = RG[:, dT_off:dT_off + CGT * TMAX // 2].bitcast(BF16)

        def fm(a, n):
            return RG[:, a:a + n]

        P3K = [("cb", q) for q in range(HT)] + [("cn", 0), ("cn", 1), ("sgb", 0), ("sgb", 1), "sig", ("cvb", 0), ("cvb", 1), "sq",
                                               "mean", "msq", "lnrstd", ("cvsP", 0), ("cvsP", 1), ("cvsN", 0), ("cvsN", 1)]
        P5K = ["dT", ("xcb", 0), ("xcb", 1), "sA", "sB", ("sgc", 0), ("sgc", 1), "t16", ("xcsP", 0), ("xcsP", 1), ("xcsN", 0), ("xcsN", 1), "sAs", "sBs"]

        wb_i = [0]

        def wload(src_ap, nk, width):
            s = wb_i[0] % NWB
            wb_i[0] += 1
            dst = wblk[s][:, 0:nk * width].rearrange("p (k c) -> p k c", k=nk)
            src = src_ap.rearrange("(k p) c -> p k c", p=128)
            tr.dma("pool", lambda e: e.dma_start(out=dst, in_=src), f"wb{s}", writes=[("wb", s)])
            return s

        def inproj_fm(wd, col0, T):
            bk = psalloc(2)
            nkh = KD // KH
            for kh in range(nkh):
                s = wload(wd[kh * KH * 128:(kh + 1) * KH * 128, col0:col0 + 256], KH, 256)
                for fi in range(2):
                    for kc in range(KH):
                        last = (kh == nkh - 1 and kc == KH - 1)
                        lhsT = wblk[s][:, kc * 256 + fi * 128: kc * 256 + (fi + 1) * 128]
                        rhs = hT[:, kh * KH + kc, 0:T]
                        out = ps[bk[fi]][:, 0:T]
                        st = (kh == 0 and kc == 0)
                        tr.op("pe", lambda e: e.matmul(out, lhsT=lhsT, rhs=rhs, start=st, stop=last),
                              reads=[("wb", s), "hT"], writes=[("ps", bk[fi])], sig=(last or kc == KH - 1))
            return bk

        def outproj(wd, row0, nt):
            for cb in range(NCB):
                s = wload(wd[row0:row0 + 1024, cb * 512:(cb + 1) * 512], 8, 512)
                bk = psalloc(nt)
                for i in range(nt):
                    for kc in range(8):
                        lhsT = yT[:, kc, i * 128:(i + 1) * 128]
                        rhs = wblk[s][:, kc * 512:(kc + 1) * 512]
                        out = ps[bk[i]][:, 0:512]
                        tr.op("pe", lambda e: e.matmul(out, lhsT=lhsT, rhs=rhs, start=(kc == 0), stop=(kc == 7)),
                              reads=[("wb", s), "yT"], writes=[("ps", bk[i])], sig=(kc == 7))
                for i in range(nt):
                    xs = xres[:, i, cb * 512:(cb + 1) * 512]
                    pin = ps[bk[i]][:, 0:512]
                    tr.op("dve", lambda e: e.tensor_tensor(out=xs, in0=pin, in1=xs, op=ALU.add),
                          reads=[("ps", bk[i]), ("xres", i)], writes=[("xres", i)])

        def row_rstd(i, dst_col):
            for cb in range(NCB):
                tr.op("act", lambda e: e.activation(out=ug[:, :], in_=xres[:, i, cb * 512:(cb + 1) * 512],
                                                    func=AF.Square, accum_out=small[:, cb:cb + 1]),
                      reads=[("xres", i)], writes=["ug", "ssc"])
            tr.op("dve", lambda e: e.reduce_sum(out=small[:, 16:17], in_=small[:, 0:NCB], axis=AX.X),
                  reads=["ssc"], writes=["ss"])
            tr.op("dve", lambda e: e.tensor_scalar(out=small[:, 17:18], in0=small[:, 16:17], scalar1=1.0 / D,
                                                   scalar2=RMS_EPS, op0=ALU.mult, op1=ALU.add),
                  reads=["ss"], writes=["ss2"])
            tr.op("act", lambda e: e.activation(out=small[:, 19:20], in_=small[:, 17:18], func=AF.Sqrt),
                  reads=["ss2"], writes=["ss3"])
            tr.op("dve", lambda e: e.reciprocal(out=small[:, dst_col:dst_col + 1], in_=small[:, 19:20]),
                  reads=["ss3"], writes=[("rstd", dst_col)])

        def rmsnorm_to_hT(i, gc0):
            row_rstd(i, 18)
            for cb in range(NCB):
                xb, xk = (sg, "sg") if cb % 2 == 0 else (Et, "Et")
                tr.op("act", lambda e: e.activation(out=xb[:, :], in_=xres[:, i, cb * 512:(cb + 1) * 512], func=AF.Copy,
                                                    scale=small[:, 18:19]),
                      reads=[("xres", i), ("rstd", 18)], writes=[xk])
                b = psalloc(1)[0]
                for q in range(4):
                    tr.op("pe", lambda e: e.transpose(out=ps[b][:, q * 128:(q + 1) * 128], in_=xb[:, q * 128:(q + 1) * 128],
                                                      identity=ident[:, :]),
                          reads=[xk, "ident"], writes=[("ps", b)], sig=(q == 3))
                for q in range(4):
                    kc = cb * 4 + q
                    o = hT[:, kc, i * 128:(i + 1) * 128]
                    pin = ps[b][:, q * 128:(q + 1) * 128]
                    gcol = cols[:, gc0 + kc:gc0 + kc + 1]
                    if q % 2 == 0:
                        tr.op("act", lambda e: e.activation(out=o, in_=pin, func=AF.Copy, scale=gcol),
                              reads=[("ps", b), "cols"], writes=["hT"])
                    else:
                        tr.op("dve", lambda e: e.tensor_scalar(out=o, in0=pin, scalar1=gcol, scalar2=None, op0=ALU.mult),
                              reads=[("ps", b), "cols"], writes=["hT"])

        tr.dma("sp", lambda e: e.dma_start(out=cols[:, :], in_=cols_d[:, :]), "c0a", writes=["cols"])
        tr.dma("sp", lambda e: e.dma_start(out=ident[:, :], in_=ident_d[:, :]), "c0b", writes=["ident"])
        tr.dma("sp", lambda e: e.dma_start(out=posv[:, :], in_=posv_d[:, :]), "c0c", writes=["posv"])
        tr.op("dve", lambda e: e.memset(ones_f[:, :], 1.0), writes=["ones_f"])
        tr.op("dve", lambda e: e.memset(ones_b[:, :], 1.0), writes=["ones_b"])
        tr.op("dve", lambda e: e.memset(carry_c[:, :, :], 0.0), writes=["carry_c"])
        tr.op("dve", lambda e: e.memset(carry_x[:, :, :], 0.0), writes=["carry_x"])
        for g, w in enumerate(POOLW):
            tr.op("dve", lambda e: e.tensor_scalar(out=invc[:, g, :], in0=posv[:, :], scalar1=1.0, scalar2=float(w),
                                                   op0=ALU.add, op1=ALU.min),
                  reads=["posv"], writes=["invc"])
        tr.op("dve", lambda e: e.reciprocal(out=invc[:, :, :], in_=invc[:, :, :]), reads=["invc"], writes=["invc"])

        def load_type_consts(ty, wTm_, rs_, sb_, kw, kr, ks):
            tr.dma("sp", lambda e: e.dma_start(out=maskt[:, :], in_=mask_d[ty * 128:(ty + 1) * 128, :]),
                   "c1b", writes=["maskt"])
            tr.dma("sp", lambda e: e.dma_start(out=sb_, in_=sgub_d[ty:ty + 1, :].partition_broadcast(128)),
                   "c1c", writes=[ks])
            for hh in range(HW // 512):
                tr.dma("sp", lambda e: e.dma_start(out=Et[:, :], in_=sgu_d[ty * 128:(ty + 1) * 128, hh * 512:(hh + 1) * 512]),
                       "c1a", writes=["Et"])
                for h4 in range(4):
                    h = hh * 4 + h4
                    tr.op("dve", lambda e: e.tensor_tensor(out=wTm_[:, h * 128:(h + 1) * 128],
                                                           in0=Et[:, h4 * 128:(h4 + 1) * 128], in1=maskt[:, :], op=ALU.mult),
                          reads=["Et", "maskt"], writes=[kw])
            for hh in range(HW // 512):
                b = psalloc(1)[0]
                tr.op("pe", lambda e: e.matmul(ps[b][:, 0:512], lhsT=ones_b[:, :], rhs=wTm_[:, hh * 512:(hh + 1) * 512],
                                               start=True, stop=True),
                      reads=["ones_b", kw], writes=[("ps", b)])
                tr.op("act", lambda e: e.activation(out=rs_[:, hh * 512:(hh + 1) * 512], in_=ps[b][:, 0:512], func=AF.Copy),
                      reads=[("ps", b)], writes=[kr])

        load_type_consts(0, wTm_p[:, :], rs_p[:, :], sb_p[:, :], "wTm_p", "rs_p", "sb_p")

        ptiles = list(range(NPT + 1))
        blocks = [[(t, 0) for t in ptiles[0:BLK]]]
        rest = ptiles[BLK:]
        blocks += [[(t, 0) for t in rest[i:i + 3]] for i in range(0, len(rest), 3)]
        if len(blocks[-1]) <= 2:
            blocks[-1].append((NPT + 1, 1))
        else:
            blocks.append([(NPT + 1, 1)])
        for blk in blocks:
            assert not (any(t == 1 for _, t in blk) and len(blk) > 3)
        print("token blocks:", blocks)

        for bi, blk in enumerate(blocks):
            nt = len(blk)
            T = nt * 128
            has_s = blk[-1][1] == 1
            npp = nt - 1 if has_s else nt
            Tp = npp * 128
            last_p = npp > 0 and blk[npp - 1][0] == NPT
            tr.alias(["vbf", "t1c"], ["RG"] + P5K)

            for i, (tl, ty) in enumerate(blk):
                tr.dma("sp", lambda e: e.dma_start(out=xres[:, i, :], in_=xin[tl * 128:(tl + 1) * 128, :]),
                       f"xin{i}", writes=[("xres", i)])
            if has_s:
                load_type_consts(1, wTm_s, rs_s, sb_s, "t1c", "t1c", "t1c")
            for i in range(nt):
                rmsnorm_to_hT(i, C_G0)

            for cb in range(NCB):
                bk = psalloc(nt)
                nq = KD // 8
                for q in range(nq):
                    s = wload(w_in_e[q * 1024:(q + 1) * 1024, D + cb * 512: D + (cb + 1) * 512], 8, 512)
                    for i in range(nt):
                        for kc in range(8):
                            last = (q == nq - 1 and kc == 7)
                            lhsT = hT[:, q * 8 + kc, i * 128:(i + 1) * 128]
                            rhs = wblk[s][:, kc * 512:(kc + 1) * 512]
                            out = ps[bk[i]][:, 0:512]
                            tr.op("pe", lambda e: e.matmul(out, lhsT=lhsT, rhs=rhs, start=(q == 0 and kc == 0), stop=last),
                                  reads=[("wb", s), "hT"], writes=[("ps", bk[i])], sig=(kc == 7))
                for i in range(nt):
                    pin = ps[bk[i]][:, 0:512]
                    ob = vbf[:, i * D + cb * 512: i * D + (cb + 1) * 512]
                    if blk[i][1] == 1:
                        tr.op("act", lambda e: e.activation(out=Et[:, :], in_=pin, func=AF.Gelu),
                              reads=[("ps", bk[i])], writes=["Et"])
                        tr.op("dve", lambda e: e.bn_stats(out=bnst[:, i, cb, :], in_=Et[:, :]),
                              reads=["Et"], writes=["bnst"])
                        tr.op("dve", lambda e: e.tensor_copy(out=ob, in_=Et[:, :]), reads=["Et"], writes=["vbf"])
                        tr.dma("sp", lambda e: e.dma_start(out=vout_d[:, cb * 512:(cb + 1) * 512], in_=Et[:, :]),
                               "vpark", reads=["Et"], writes=["voutd"])
                    else:
                        tr.op("act", lambda e: e.activation(out=ob, in_=pin, func=AF.Gelu),
                              reads=[("ps", bk[i])], writes=["vbf"])
                        tr.op("dve", lambda e: e.bn_stats(out=bnst[:, i, cb, :], in_=ob),
                              reads=["vbf"], writes=["bnst"])
            for i in range(nt):
                mv = small[:, 20 + 4 * i: 22 + 4 * i]
                rs_ = small[:, 22 + 4 * i: 23 + 4 * i]
                ngm = small[:, 23 + 4 * i: 24 + 4 * i]
                wsrc, wk = (wTm_s, "t1c") if blk[i][1] == 1 else (wTm_p[:, :], "wTm_p")
                tr.op("dve", lambda e: e.bn_aggr(out=mv, in_=bnst[:, i, 0:NCB, :]), reads=["bnst"], writes=[("mv", i)])
                tr.op("dve", lambda e: e.tensor_scalar(out=rs_, in0=mv[:, 1:2], scalar1=LN_EPS, scalar2=None, op0=ALU.add),
                      reads=[("mv", i)], writes=[("mv", i)])
                tr.op("act", lambda e: e.activation(out=rs_, in_=rs_, func=AF.Sqrt), reads=[("mv", i)], writes=[("mv", i)])
                tr.op("dve", lambda e: e.reciprocal(out=rs_, in_=rs_), reads=[("mv", i)], writes=[("mv", i)])
                tr.op("dve", lambda e: e.tensor_scalar(out=ngm, in0=mv[:, 0:1], scalar1=-1.0, scalar2=None, op0=ALU.mult),
                      reads=[("mv", i)], writes=[("mv", i)])
                tr.op("dve", lambda e: e.tensor_scalar(out=wTs[:, i, :], in0=wsrc, scalar1=rs_, scalar2=None, op0=ALU.mult),
                      reads=[("mv", i), wk], writes=["wTs"])
                tr.op("dve", lambda e: e.tensor_scalar(out=nm[:, i, :], in0=ones_f[:, :], scalar1=ngm, scalar2=None, op0=ALU.mult),
                      reads=[("mv", i), "ones_f"], writes=["nm"])
            if has_s:
                i = nt - 1
                mv = small[:, 20 + 4 * i: 22 + 4 * i]
                rs_ = small[:, 22 + 4 * i: 23 + 4 * i]
                for cb in range(NCB):
                    c0 = cb * 512
                    tr.dma("sp", lambda e: e.dma_start(out=Et[:, :], in_=vout_d[:, c0:c0 + 512]), "vback", reads=["voutd"], writes=["Et"])
                    tr.dma("sp", lambda e: e.dma_start(out=ug[:, :], in_=vg_d[0:1, c0:c0 + 512].partition_broadcast(128)),
                           "vg0", writes=["ug"])
                    tr.dma("sp", lambda e: e.dma_start(out=sg[:, :], in_=vb_d[0:1, c0:c0 + 512].partition_broadcast(128)),
                           "vb0", writes=["sg"])
                    tr.op("dve", lambda e: e.tensor_scalar(out=Et[:, :], in0=Et[:, :], scalar1=mv[:, 0:1], scalar2=rs_,
                                                           op0=ALU.subtract, op1=ALU.mult),
                          reads=["Et", ("mv", i)], writes=["Et"])
                    tr.op("dve", lambda e: e.tensor_tensor(out=Et[:, :], in0=Et[:, :], in1=ug[:, :], op=ALU.mult),
                          reads=["Et", "ug"], writes=["Et"])
                    tr.op("dve", lambda e: e.tensor_tensor(out=Et[:, :], in0=Et[:, :], in1=sg[:, :], op=ALU.add),
                          reads=["Et", "sg"], writes=["Et"])
                    tr.dma("sp", lambda e: e.dma_start(out=vout_d[:, c0:c0 + 512], in_=Et[:, :]), "vout", reads=["Et"], writes=["voutd"])

            for g in range(KD // 8):
                for pr in range(4):
                    j0 = g * 8 + 2 * pr
                    bu = inproj_fm(w_in_e, j0 * 128, T)
                    bg = inproj_fm(w_in_e, 2 * D + j0 * 128, T)
                    for fi in range(2):
                        j = j0 + fi
                        h = j // HT
                        tr.op("act", lambda e: e.activation(out=ug[:, 0:T], in_=ps[bu[fi]][:, 0:T], func=AF.Gelu),
                              reads=[("ps", bu[fi])], writes=["ug"])
                        tr.op("act", lambda e: e.activation(out=sg[:, 0:T], in_=ps[bg[fi]][:, 0:T], func=AF.Silu),
                              reads=[("ps", bg[fi])], writes=["sg"])
                        bm = psalloc(1)[0]
                        for i in range(nt):
                            out = ps[bm][:, i * 128:(i + 1) * 128]
                            tr.op("pe", lambda e: e.matmul(out, lhsT=vbf[:, i * D + j * 128: i * D + (j + 1) * 128],
                                                           rhs=wTs[:, i, h * 128:(h + 1) * 128], start=True, stop=False),
                                  reads=["vbf", "wTs"], writes=[("ps", bm)], sig=False)
                            tr.op("pe", lambda e: e.matmul(out, lhsT=nm[:, i, :], rhs=wTs[:, i, h * 128:(h + 1) * 128],
                                                           start=False, stop=True),
                                  reads=["nm", "wTs"], writes=[("ps", bm)], sig=(i == nt - 1))
                        for ty in ([0] if npp > 0 else []) + ([1] if has_s else []):
                            r_, s_, kk = (rs_s, sb_s, "t1c") if ty == 1 else (rs_p[:, :], sb_p[:, :], "rs_p")
                            tr.op("dve", lambda e: e.scalar_tensor_tensor(out=At[:, ty, :], in0=r_[:, h * 128:(h + 1) * 128],
                                                                          scalar=cols[:, C_VB + j:C_VB + j + 1],
                                                                          in1=s_[:, h * 128:(h + 1) * 128], op0=ALU.mult, op1=ALU.add),
                                  reads=[kk, "sb_p", "cols"], writes=["At"])
                        for i in range(nt):
                            ty = blk[i][1]
                            tr.op("dve", lambda e: e.scalar_tensor_tensor(out=Et[:, i * 128:(i + 1) * 128],
                                                                          in0=ps[bm][:, i * 128:(i + 1) * 128],
                                                                          scalar=cols[:, C_VG + j:C_VG + j + 1],
                                                                          in1=At[:, ty, :], op0=ALU.mult, op1=ALU.add),
                                  reads=[("ps", bm), "At", "cols"], writes=["Et"])
                        tr.op("dve", lambda e: e.tensor_tensor(out=Et[:, 0:T], in0=Et[:, 0:T], in1=ug[:, 0:T], op=ALU.mult),
                              reads=["Et", "ug"], writes=["Et"])
                        jj = 2 * pr + fi
                        tr.op("dve", lambda e: e.tensor_tensor(out=yT[:, jj, 0:T], in0=Et[:, 0:T], in1=sg[:, 0:T], op=ALU.mult),
                              reads=["Et", "sg"], writes=["yT"])
                outproj(w_out_e, g * 1024, nt)

            for k_ in P3K:
                tr.alias([k_], ["vbf", "t1c"])
            for g in range(KD // 8):
                for hd in range(8 // HT):
                    j_h0 = g * 8 + hd * HT
                    for pr in range(HT // 2):
                        j0 = j_h0 + 2 * pr
                        ba = inproj_fm(w_in_e, 3 * D + j0 * 128, T)
                        bb = inproj_fm(w_in_e, 4 * D + j0 * 128, T)
                        for fi in range(2):
                            j = j0 + fi
                            q = 2 * pr + fi
                            sgt = fm(sigb, T)
                            tr.op("act", lambda e: e.activation(out=sgt, in_=ps[bb[fi]][:, 0:T], func=AF.Sigmoid),
                                  reads=[("ps", bb[fi])], writes=["sig"])
                            cacc = fm(cbuf[q], T)
                            wk0 = cols[:, C_CW + j * CONV_W: C_CW + j * CONV_W + 1]
                            cbias = cols[:, C_CB + j:C_CB + j + 1]
                            if npp > 0:
                                ext = fm(cvb[fi], LC)
                                tr.op("act", lambda e: e.activation(out=ext[:, 0:NPRE], in_=carry_c[:, j, :], func=AF.Copy),
                                      reads=["carry_c"], writes=[("cvb", fi)])
                                tr.op("dve", lambda e: e.tensor_tensor(out=ext[:, NPRE:NPRE + Tp], in0=ps[ba[fi]][:, 0:Tp],
                                                                       in1=sgt[:, 0:Tp], op=ALU.mult),
                                      reads=[("ps", ba[fi]), "sig"], writes=[("cvb", fi)])
                                tr.op("act", lambda e: e.activation(out=carry_c[:, j, :], in_=ext[:, Tp:Tp + NPRE], func=AF.Copy),
                                      reads=[("cvb", fi)], writes=["carry_c"])
                                tr.op("dve", lambda e: e.tensor_scalar(out=cacc[:, 0:Tp], in0=ext[:, 0:Tp], scalar1=wk0, scalar2=cbias,
                                                                       op0=ALU.mult, op1=ALU.add),
                                      reads=[("cvb", fi), "cols"], writes=[("cb", q)])
                                for k in range(1, CONV_W):
                                    wk = cols[:, C_CW + j * CONV_W + k: C_CW + j * CONV_W + k + 1]
                                    tr.op("dve", lambda e: e.scalar_tensor_tensor(out=cacc[:, 0:Tp], in0=ext[:, k:k + Tp], scalar=wk,
                                                                                  in1=cacc[:, 0:Tp], op0=ALU.mult, op1=ALU.add),
                                          reads=[("cvb", fi), "cols", ("cb", q)], writes=[("cb", q)])
                            if has_s:
                                LS = NPRE + DS
                                ext3 = RG[:, cvs[fi]:cvs[fi] + NQ * LS].rearrange("p (q l) -> p q l", q=NQ)
                                tr.dma("sp", lambda e: e.dma_start(
                                    out=ext3[:, :, 0:NPRE], in_=stcT_d[j * 128:(j + 1) * 128, :].rearrange("p (q l) -> p q l", q=NQ)),
                                    f"stc{fi}", writes=[("cvsP", fi)])
                                pa3 = ps[ba[fi]][:, Tp:T].rearrange("p (q l) -> p q l", q=NQ)
                                sg3 = sgt[:, Tp:T].rearrange("p (q l) -> p q l", q=NQ)
                                tr.op("dve", lambda e: e.tensor_tensor(out=ext3[:, :, NPRE:LS], in0=pa3, in1=sg3, op=ALU.mult),
                                      reads=[("ps", ba[fi]), "sig"], writes=[("cvsN", fi)])
                                gq = fm(cnb[fi], 128)
                                gq3 = gq.rearrange("p (q l) -> p q l", q=NQ)
                                tr.op("act", lambda e: e.activation(out=gq3, in_=ext3[:, :, NPRE:LS], func=AF.Copy),
                                      reads=[("cvsN", fi)], writes=[("cn", fi)])
                                tr.dma("sp", lambda e: e.dma_start(out=gluTs_d[j * 128:(j + 1) * 128, :], in_=gq),
                                       f"glo{fi}", reads=[("cn", fi)])
                                c3 = cacc[:, Tp:T].rearrange("p (q l) -> p q l", q=NQ)
                                tr.op("dve", lambda e: e.tensor_scalar(out=c3, in0=ext3[:, :, 0:DS], scalar1=wk0, scalar2=cbias,
                                                                       op0=ALU.mult, op1=ALU.add),
                                      reads=[("cvsP", fi), ("cvsN", fi), "cols", ("cb", q)], writes=[("cb", q)])
                                for k in range(1, CONV_W):
                                    wk = cols[:, C_CW + j * CONV_W + k: C_CW + j * CONV_W + k + 1]
                                    tr.op("dve", lambda e: e.scalar_tensor_tensor(out=c3, in0=ext3[:, :, k:k + DS], scalar=wk, in1=c3,
                                                                                  op0=ALU.mult, op1=ALU.add),
                                          reads=[("cvsP", fi), ("cvsN", fi), "cols", ("cb", q)], writes=[("cb", q)])
                    sgb_of = {}
                    bgbs = []
                    for pr in range(HT // 2):
                        bgbs.append(inproj_fm(w_in_e, 5 * D + (j_h0 + 2 * pr) * 128, T))
                    b1, b2 = psalloc(2)
                    for q in range(HT):
                        cq = fm(cbuf[q], T)
                        sq = fm(sqb, T)
                        tr.op("act", lambda e: e.activation(out=sq, in_=cq, func=AF.Square), reads=[("cb", q)], writes=["sq"])
                        tr.op("pe", lambda e: e.matmul(ps[b1][:, 0:T], lhsT=ones_f[:, :], rhs=cq, start=(q == 0), stop=(q == HT - 1)),
                              reads=[("cb", q), "ones_f"], writes=[("ps", b1)], sig=(q == HT - 1))
                        tr.op("pe", lambda e: e.matmul(ps[b2][:, 0:T], lhsT=ones_f[:, :], rhs=sq, start=(q == 0), stop=(q == HT - 1)),
                              reads=["sq", "ones_f"], writes=[("ps", b2)], sig=True)
                    mean, rstd, msq = fm(meanb, T), fm(rstdb, T), fm(msqb, T)
                    nfe = float(HT * 128)
                    tr.op("dve", lambda e: e.tensor_scalar(out=mean, in0=ps[b1][:, 0:T], scalar1=1.0 / nfe, scalar2=None, op0=ALU.mult),
                          reads=[("ps", b1)], writes=["mean"])
                    tr.op("dve", lambda e: e.tensor_tensor(out=msq, in0=mean, in1=mean, op=ALU.mult), reads=["mean"], writes=["msq"])
                    tr.op("dve", lambda e: e.scalar_tensor_tensor(out=rstd, in0=ps[b2][:, 0:T], scalar=1.0 / nfe, in1=msq,
                                                                  op0=ALU.mult, op1=ALU.subtract),
                          reads=[("ps", b2), "msq"], writes=["lnrstd"])
                    tr.op("dve", lambda e: e.tensor_scalar(out=rstd, in0=rstd, scalar1=LN_EPS, scalar2=None, op0=ALU.add),
                          reads=["lnrstd"], writes=["lnrstd"])
                    tr.op("act", lambda e: e.activation(out=rstd, in_=rstd, func=AF.Sqrt), reads=["lnrstd"], writes=["lnrstd"])
                    tr.op("dve", lambda e: e.reciprocal(out=rstd, in_=rstd), reads=["lnrstd"], writes=["lnrstd"])
                    for pr in range(HT // 2):
                        bgb = bgbs[pr]
                        for fi in range(2):
                            j = j_h0 + 2 * pr + fi
                            q = 2 * pr + fi
                            sgbt = fm(sgbb[fi], T)
                            tr.op("act", lambda e: e.activation(out=sgbt, in_=ps[bgb[fi]][:, 0:T], func=AF.Silu),
                                  reads=[("ps", bgb[fi])], writes=[("sgb", fi)])
                            cq = fm(cbuf[q], T)
                            cn = fm(cnb[fi], T)
                            tr.op("dve", lambda e: e.tensor_tensor(out=cn, in0=cq, in1=mean, op=ALU.subtract),
                                  reads=[("cb", q), "mean"], writes=[("cn", fi)])
                            tr.op("dve", lambda e: e.tensor_tensor(out=cn, in0=cn, in1=rstd, op=ALU.mult),
                                  reads=[("cn", fi), "lnrstd"], writes=[("cn", fi)])
                            tr.op("act", lambda e: e.activation(out=cn, in_=cn, func=AF.Silu,
                                                                scale=cols[:, C_CNG + j:C_CNG + j + 1],
                                                                bias=cols[:, C_CNB + j:C_CNB + j + 1]),
                                  reads=[("cn", fi), "cols"], writes=[("cn", fi)])
                            jj = hd * HT + q
                            tr.op("dve", lambda e: e.tensor_tensor(out=yT[:, jj, 0:T], in0=cn, in1=sgbt, op=ALU.mult),
                                  reads=[("cn", fi), ("sgb", fi)], writes=["yT"])
                outproj(w_out_e, (KD // 8 + g) * 1024, nt)

            if last_p:
                tr.dma("sp", lambda e: e.dma_start(out=gluTp_d.rearrange("(k p) t -> p k t", p=128), in_=carry_c[:, :, :]),
                       "glp", reads=["carry_c"])

            for i in range(nt):
                rmsnorm_to_hT(i, C_G1)

            for k_ in P5K:
                tr.alias([k_], P3K + ["vbf", "t1c"])
            for gp, w in enumerate(POOLW):
                nstep = gp + 1
                for pr in range(CGT // 2):
                    cj0 = gp * CGT + 2 * pr
                    bx = inproj_fm(w_in_o, cj0 * 128, T)
                    for fi in range(2):
                        cj = cj0 + fi
                        jl = 2 * pr + fi
                        if npp > 0:
                            ext = fm(xcb[fi], LX)
                            tr.op("act", lambda e: e.activation(out=ext[:, 0:PPRE], in_=carry_x[:, cj, :], func=AF.Copy),
                                  reads=["carry_x"], writes=[("xcb", fi)])
                            tr.op("act", lambda e: e.activation(out=ext[:, PPRE:PPRE + Tp], in_=ps[bx[fi]][:, 0:Tp], func=AF.Copy),
                                  reads=[("ps", bx[fi])], writes=[("xcb", fi)])
                            tr.op("act", lambda e: e.activation(out=carry_x[:, cj, :], in_=ext[:, Tp:Tp + PPRE], func=AF.Copy),
                                  reads=[("xcb", fi)], writes=["carry_x"])
                            L = PPRE + Tp
                            cur, curk = ext, ("xcb", fi)
                            for m in range(nstep):
                                sh = 1 << m
                                vs_ = (1 << (m + 1)) - 1
                                nxt = fm(sAb if m % 2 == 0 else sBb, LX)
                                nk = "sA" if m % 2 == 0 else "sB"
                                c_, n_ = cur, nxt
                                tr.op("dve", lambda e: e.tensor_tensor(out=n_[:, vs_:L], in0=c_[:, vs_:L], in1=c_[:, vs_ - sh:L - sh], op=ALU.add),
                                      reads=[curk], writes=[nk])
                                cur, curk = nxt, nk
                            dsl = dT[:, jl * TMAX: jl * TMAX + Tp]
                            c_ = cur
                            tr.op("dve", lambda e: e.scalar_tensor_tensor(out=dsl, in0=c_[:, PPRE:PPRE + Tp], scalar=1.0 / w,
                                                                          in1=ext[:, PPRE:PPRE + Tp], op0=ALU.mult, op1=ALU.subtract),
                                  reads=[curk, ("xcb", fi)], writes=["dT"])
                            if bi == 0:
                                o0 = 128
                                t16v = RG[:, t16:t16 + 16]
                                tr.op("dve", lambda e: e.tensor_tensor(out=t16v, in0=c_[:, PPRE + o0:PPRE + o0 + 16], in1=invc[:, gp, :], op=ALU.mult),
                                      reads=[curk, "invc"], writes=["t16"])
                                tr.op("dve", lambda e: e.tensor_tensor(out=dsl[:, o0:o0 + 16], in0=t16v, in1=ext[:, PPRE + o0:PPRE + o0 + 16],
                                                                       op=ALU.subtract),
                                      reads=["t16", ("xcb", fi)], writes=["dT"])
                        if has_s:
                            LS = PPRE + DS
                            ext3 = RG[:, xcs[fi]:xcs[fi] + NQ * LS].rearrange("p (q l) -> p q l", q=NQ)
                            tr.dma("sp", lambda e: e.dma_start(
                                out=ext3[:, :, 0:PPRE], in_=stpT_d[cj * 128:(cj + 1) * 128, :].rearrange("p (q l) -> p q l", q=NQ)),
                                f"stp{fi}", writes=[("xcsP", fi)])
                            px3 = ps[bx[fi]][:, Tp:T].rearrange("p (q l) -> p q l", q=NQ)
                            tr.op("act", lambda e: e.activation(out=ext3[:, :, PPRE:LS], in_=px3, func=AF.Copy),
                                  reads=[("ps", bx[fi])], writes=[("xcsN", fi)])
                            xo = fm(sgcb[fi], 128)
                            tr.op("act", lambda e: e.activation(out=xo, in_=ps[bx[fi]][:, Tp:T], func=AF.Copy),
                                  reads=[("ps", bx[fi])], writes=[("sgc", fi)])
                            tr.dma("sp", lambda e: e.dma_start(out=xcTs_d[cj * 128:(cj + 1) * 128, :], in_=xo),
                                   f"xco{fi}", reads=[("sgc", fi)])
                            cur, curk = ext3, [("xcsP", fi), ("xcsN", fi)]
                            for m in range(nstep):
                                sh = 1 << m
                                vs_ = (1 << (m + 1)) - 1
                                a_ = sAs if m % 2 == 0 else sBs
                                nxt = RG[:, a_:a_ + NQ * LS].rearrange("p (q l) -> p q l", q=NQ)
                                nk = "sAs" if m % 2 == 0 else "sBs"
                                c_, n_ = cur, nxt
                                tr.op("dve", lambda e: e.tensor_tensor(out=n_[:, :, vs_:LS], in0=c_[:, :, vs_:LS], in1=c_[:, :, vs_ - sh:LS - sh], op=ALU.add),
                                      reads=curk, writes=[nk])
                                cur, curk = nxt, [nk]
                            dsl3 = dT[:, jl * TMAX + Tp: jl * TMAX + T].rearrange("p (q l) -> p q l", q=NQ)
                            c_ = cur
                            tr.op("dve", lambda e: e.scalar_tensor_tensor(out=dsl3, in0=c_[:, :, PPRE:LS], scalar=1.0 / w, in1=ext3[:, :, PPRE:LS],
                                                                          op0=ALU.mult, op1=ALU.subtract),
                                  reads=curk + [("xcsP", fi), ("xcsN", fi)], writes=["dT"])
                for half in range(CGT // 8):
                    for pr in range(4):
                        dj0 = half * 8 + 2 * pr
                        bk = psalloc(2)
                        s = wload(pool_w[gp * CGT * 128:(gp + 1) * CGT * 128, dj0 * 128: dj0 * 128 + 256], CGT, 256)
                        for fi in range(2):
                            for kc in range(CGT):
                                lhsT = wblk[s][:, kc * 256 + fi * 128: kc * 256 + (fi + 1) * 128]
                                rhs = dT[:, kc * TMAX: kc * TMAX + T]
                                out = ps[bk[fi]][:, 0:T]
                                tr.op("pe", lambda e: e.matmul(out, lhsT=lhsT, rhs=rhs, start=(kc == 0), stop=(kc == CGT - 1)),
                                      reads=[("wb", s), "dT"], writes=[("ps", bk[fi])], sig=(kc == CGT - 1))
                        bgc = inproj_fm(w_in_o, 2 * D + (gp * CGT + dj0) * 128, T)
                        for fi in range(2):
                            fgl = gp * CGT + dj0 + fi
                            sgct = fm(sgcb[fi], T)
                            tr.op("act", lambda e: e.activation(out=sgct, in_=ps[bgc[fi]][:, 0:T], func=AF.Silu),
                                  reads=[("ps", bgc[fi])], writes=[("sgc", fi)])
                            jj = 2 * pr + fi
                            tr.op("dve", lambda e: e.scalar_tensor_tensor(out=yT[:, jj, 0:T], in0=ps[bk[fi]][:, 0:T],
                                                                          scalar=cols[:, C_PS + fgl:C_PS + fgl + 1],
                                                                          in1=sgct, op0=ALU.mult, op1=ALU.mult),
                                  reads=[("ps", bk[fi]), ("sgc", fi), "cols"], writes=["yT"])
                    outproj(w_out_o, ((gp * CGT) // 8 + half) * 1024, nt)
            if last_p:
                tr.dma("sp", lambda e: e.dma_start(out=xcTp_d.rearrange("(k p) t -> p k t", p=128), in_=carry_x[:, :, :]),
                       "xcp", reads=["carry_x"])

            outs = [(i, tl) for i, (tl, ty) in enumerate(blk) if tl != 0]
            for i, tl in outs:
                row_rstd(i, 40 + i)
            for cb in range(NCB):
                gb, gk = (sg, "sg") if cb % 2 == 0 else (Et, "Et")
                tr.dma("sp", lambda e: e.dma_start(out=gb[:, :], in_=nf_d[0:1, cb * 512:(cb + 1) * 512].partition_broadcast(128)),
                       "nfb0" if cb % 2 == 0 else "nfb1", writes=[gk])
                for i, tl in outs:
                    xs = xres[:, i, cb * 512:(cb + 1) * 512]
                    tr.op("dve", lambda e: e.scalar_tensor_tensor(out=xs, in0=xs, scalar=small[:, 40 + i:41 + i], in1=gb[:, :],
                                                                  op0=ALU.mult, op1=ALU.mult),
                          reads=[("xres", i), ("rstd", 40 + i), gk], writes=[("xres", i)])
            for i, tl in outs:
                ot = tl - 1
                tr.dma("sp", lambda e: e.dma_start(out=y_d[ot * 128:(ot + 1) * 128, :], in_=xres[:, i, :]),
                       f"yo{i}", reads=[("xres", i)])
            tr.alias(["RG"], P5K + P3K + ["vbf", "t1c"])

        tr.dma("sp", lambda e: e.dma_start(out=convs_old[:, :], in_=stc_tm[:, DS * D:NPRE * D]), "cso")
        tr.dma("sp", lambda e: e.dma_start(out=pools_old[:, :], in_=stp_tm[:, DS * 2 * D:PPRE * 2 * D]), "pso")
        tr.final_wait("sp")
        print("bass ops emitted:", tr.nops)
    return nc


_PROG_CACHE = {}


def _prep_inputs(inp):
    xp = np.asarray(inp["x_prompt"], np.float32)
    xs = np.asarray(inp["x_sample"], np.float32)
    NB, SEQ, D = xp.shape
    DB, DS, _ = xs.shape
    KD = D // 128
    KC = 2 * KD
    half = SEQ // 2
    NPT = half // 128
    NQ = DB // NCORES
    assert NB * 2 == NCORES and NQ * DS == 128

    def colmaj(v):
        v = np.asarray(v, np.float32).reshape(-1, 128)
        return np.ascontiguousarray(v.T)

    cw = np.asarray(inp["conv_w"], np.float32)[0]
    cw_cols = np.ascontiguousarray(cw.T.reshape(KD, 128, CONV_W).transpose(1, 0, 2).reshape(128, KD * CONV_W))
    cols = np.concatenate([
        colmaj(inp["norm_in"][0]), colmaj(inp["norm_in"][1]), colmaj(inp["v_norm_g"][0]), colmaj(inp["v_norm_b"][0]),
        colmaj(inp["conv_b"][0]), colmaj(inp["conv_norm_g"][0]), colmaj(inp["conv_norm_b"][0]),
        colmaj(inp["pool_scale"][0]), np.zeros((128, KD), np.float32), cw_cols], axis=1)
    cols = np.ascontiguousarray(cols, np.float32)

    sgw = np.asarray(inp["sgu_w"], np.float32)[0]
    sgb = np.asarray(inp["sgu_b"], np.float32)[0]
    wT_p = np.ascontiguousarray(sgw.transpose(2, 0, 1).reshape(128, NHEAD * 128))
    wT_s = np.zeros((128, NHEAD, 128), np.float32)
    for q in range(NQ):
        wT_s[q * DS:(q + 1) * DS, :, q * DS:(q + 1) * DS] = sgw[:, :DS, :DS].transpose(2, 0, 1)
    sgu = np.concatenate([wT_p, wT_s.reshape(128, NHEAD * 128)], axis=0)
    s_i = np.arange(128)[:, None]
    t_i = np.arange(128)[None, :]
    mask_p = (s_i <= t_i).astype(np.float32)
    mask_s = ((s_i // DS == t_i // DS) & (s_i <= t_i)).astype(np.float32)
    mask = np.concatenate([mask_p, mask_s], axis=0)
    sgub = np.stack([sgb.reshape(-1), np.tile(sgb[:, :DS], (1, NQ)).reshape(-1)], axis=0).astype(np.float32)
    ident = np.eye(128, dtype=np.float32)

    shared = {
        "w_in_e": np.asarray(inp["w_in_even"], np.float32)[0],
        "w_out_e": np.asarray(inp["w_out_even"], np.float32)[0],
        "w_in_o": np.asarray(inp["w_in_odd"], np.float32)[0],
        "pool_w": np.asarray(inp["pool_w"], np.float32)[0].reshape(4 * (2 * D // 4), 2 * D // 4),
        "w_out_o": np.asarray(inp["w_out_odd"], np.float32)[0],
        "cols": cols, "nf": np.asarray(inp["norm_f"], np.float32).reshape(1, D),
        "vg": np.asarray(inp["v_norm_g"], np.float32).reshape(1, D),
        "vb": np.asarray(inp["v_norm_b"], np.float32).reshape(1, D),
        "sgub": sgub, "sgu": sgu, "mask": mask, "ident": ident,
    }
    stc = np.asarray(inp["state_conv"], np.float32)[0]
    stp = np.asarray(inp["state_pool"], np.float32)[0]
    in_maps = []
    for c in range(NCORES):
        b, hf = c // 2, c % 2
        halo = xp[b, half - 128:half] if hf == 1 else np.zeros((128, D), np.float32)
        xin = np.concatenate([halo, xp[b, hf * half:(hf + 1) * half], xs[c * NQ:(c + 1) * NQ].reshape(128, D)], axis=0)
        sc = stc[c * NQ:(c + 1) * NQ]
        sp_ = stp[c * NQ:(c + 1) * NQ]
        m = dict(shared)
        m["xin"] = np.ascontiguousarray(xin)
        m["posv"] = np.ascontiguousarray(np.broadcast_to((hf * half + np.arange(16, dtype=np.float32))[None, :], (128, 16)))
        m["stcT"] = np.ascontiguousarray(sc.transpose(2, 0, 1).reshape(D, NQ * NPRE))
        m["stpT"] = np.ascontiguousarray(sp_.transpose(2, 0, 1).reshape(2 * D, NQ * PPRE))
        m["stc_tm"] = np.ascontiguousarray(sc.reshape(NQ, NPRE * D))
        m["stp_tm"] = np.ascontiguousarray(sp_.reshape(NQ, PPRE * 2 * D))
        in_maps.append(m)
    return in_maps, (NB, SEQ, D, DB, DS, NPT, NQ)


def kernel(**inputs):
    in_maps, (NB, SEQ, D, DB, DS, NPT, NQ) = _prep_inputs(inputs)
    key = (D, NPT)
    if key not in _PROG_CACHE:
        _PROG_CACHE[key] = build_program(D, NPT)
    nc = _PROG_CACHE[key]
    res = run_bass_kernel_spmd(nc, in_maps, core_ids=list(range(NCORES))).results
    half = SEQ // 2
    y_prompt = np.zeros((NB, SEQ, D), np.float32)
    y_sample = np.zeros((DB, DS, D), np.float32)
    new_v = np.zeros((1, DB, DS, D), np.float32)
    conv_p = np.zeros((1, NB, NPRE, D), np.float32)
    conv_s = np.zeros((1, DB, NPRE, D), np.float32)
    pool_p = np.zeros((1, NB, PPRE, 2 * D), np.float32)
    pool_s = np.zeros((1, DB, PPRE, 2 * D), np.float32)
    for c in range(NCORES):
        r = res[c]
        b, hf = c // 2, c % 2
        y = r["y"].reshape(NPT + 1, 128, D)
        y_prompt[b, hf * half:(hf + 1) * half] = y[:NPT].reshape(half, D)
        qs = slice(c * NQ, (c + 1) * NQ)
        y_sample[qs] = y[NPT].reshape(NQ, DS, D)
        new_v[0, qs] = r["vout"].reshape(NQ, DS, D)
        conv_s[0, qs, :NPRE - DS] = r["convs_old"].reshape(NQ, NPRE - DS, D)
        conv_s[0, qs, NPRE - DS:] = r["gluTs"].T.reshape(NQ, DS, D)
        pool_s[0, qs, :PPRE - DS] = r["pools_old"].reshape(NQ, PPRE - DS, 2 * D)
        pool_s[0, qs, PPRE - DS:] = r["xcTs"].T.reshape(NQ, DS, 2 * D)
        if hf == 1:
            conv_p[0, b] = r["gluTp"].T
            pool_p[0, b] = r["xcTp"].T
    return (y_prompt, y_sample, new_v, conv_p, conv_s, pool_p, pool_s)
```
